# Optimizing a Trainium2 kernel written in Bass

```python
import jax, jax.numpy as jnp
from jax import lax
import numpy as np

D_MODEL = 1024
BATCH = 2
SEQ = 8192
DEPTH = 4

MLA_HEADS = 8
MLA_NOPE = 64
MLA_ROPE = 32
MLA_V = 64
Q_LORA = 384
KV_LORA = 256
MLA_WIDTH = MLA_HEADS * MLA_V

DIL_PAIRS = ((128, 1), (512, 4), (2048, 16))
DIL_GROUPS = 3
DIL_HEADS = 8
DIL_HD = 64
DIL_WIDTH = DIL_HEADS * DIL_HD
ROT_DIM = DIL_HD // 4

MIX_WIDTH = MLA_WIDTH + DIL_WIDTH
ROPE_THETA = 500000.0
Q_BLOCK = 128
EPS = 1e-6

IN_SPLITS = (Q_LORA, KV_LORA, MLA_ROPE, MLA_WIDTH, 3 * DIL_GROUPS * DIL_WIDTH, DIL_WIDTH)
IN_WIDTH = Q_LORA + KV_LORA + MLA_ROPE + MLA_WIDTH + 3 * DIL_GROUPS * DIL_WIDTH + DIL_WIDTH

kernel_name = "hymba_mla_dilated_window_encoder"


def rms_norm(x, g):
    xf = x.astype(jnp.float32)
    y = xf * lax.rsqrt(jnp.mean(xf * xf, axis=-1, keepdims=True) + EPS)
    return (y * g.astype(jnp.float32)).astype(x.dtype)


def rope_tables(seq, dim):
    inv = 1.0 / (ROPE_THETA ** (jnp.arange(0, dim, 2, dtype=jnp.float32) / dim))
    ang = jnp.arange(seq, dtype=jnp.float32)[:, None] * inv[None, :]
    return jnp.cos(ang), jnp.sin(ang)


def apply_rope(x, cos, sin):
    xf = x.astype(jnp.float32)
    x1, x2 = jnp.split(xf, 2, axis=-1)
    c = cos[:, None, :]
    s = sin[:, None, :]
    return jnp.concatenate([x1 * c - x2 * s, x1 * s + x2 * c], axis=-1).astype(x.dtype)


def partial_rope(x, cos, sin):
    return jnp.concatenate([apply_rope(x[..., :ROT_DIM], cos, sin), x[..., ROT_DIM:]], axis=-1)


def mla_attention(c_q, c_kv, k_r, q_norm_g, kv_norm_g, w_uq, w_ukv, cos, sin):
    B, S, _ = c_q.shape
    q = (rms_norm(c_q, q_norm_g) @ w_uq).reshape(B, S, MLA_HEADS, MLA_NOPE + MLA_ROPE)
    q_nope = q[..., :MLA_NOPE]
    q_rope = apply_rope(q[..., MLA_NOPE:], cos, sin)
    kv = (rms_norm(c_kv, kv_norm_g) @ w_ukv).reshape(B, S, MLA_HEADS, MLA_NOPE + MLA_V)
    k_nope = kv[..., :MLA_NOPE]
    v = kv[..., MLA_NOPE:]
    k_rope = apply_rope(k_r[:, :, None, :], cos, sin)[:, :, 0, :]
    scale = (MLA_NOPE + MLA_ROPE) ** -0.5
    nb = S // Q_BLOCK
    qn_b = q_nope.reshape(B, nb, Q_BLOCK, MLA_HEADS, MLA_NOPE).swapaxes(0, 1)
    qr_b = q_rope.reshape(B, nb, Q_BLOCK, MLA_HEADS, MLA_ROPE).swapaxes(0, 1)

    def block(args):
        qn, qr = args
        s = (jnp.einsum('bqhd,bkhd->bhqk', qn, k_nope).astype(jnp.float32)
             + jnp.einsum('bqhr,bkr->bhqk', qr, k_rope).astype(jnp.float32)) * scale
        p = jax.nn.softmax(s, axis=-1)
        return jnp.einsum('bhqk,bkhd->bqhd', p.astype(v.dtype), v)

    o = lax.map(block, (qn_b, qr_b))
    return o.swapaxes(0, 1).reshape(B, S, MLA_WIDTH)


def to_strided(t, d):
    B, S = t.shape[:2]
    rest = t.shape[2:]
    return t.reshape(B, S // d, d, *rest).swapaxes(1, 2).reshape(B * d, S // d, *rest)


def from_strided(t, B, d):
    L = t.shape[1]
    rest = t.shape[2:]
    return t.reshape(B, d, L, *rest).swapaxes(1, 2).reshape(B, L * d, *rest)


def banded_attention(q, k, v, half):
    N, L, H, Dh = q.shape
    nb = -(-L // half)
    Lp = nb * half
    pad = Lp - L
    qp = jnp.pad(q, ((0, 0), (0, pad), (0, 0), (0, 0))).reshape(N, nb, half, H, Dh)

    def key_windows(t):
        tp = jnp.pad(t, ((0, 0), (half, pad + half), (0, 0), (0, 0))).reshape(N, nb + 2, half, H, Dh)
        return jnp.concatenate([tp[:, :-2], tp[:, 1:-1], tp[:, 2:]], axis=2)

    kw = key_windows(k)
    vw = key_windows(v)
    qpos = jnp.arange(Lp).reshape(nb, half)
    kpos = (jnp.arange(nb)[:, None] - 1) * half + jnp.arange(3 * half)[None, :]
    valid = ((jnp.abs(qpos[:, :, None] - kpos[:, None, :]) <= half)
             & (kpos[:, None, :] >= 0) & (kpos[:, None, :] < L))
    s = jnp.einsum('nbqhd,nbkhd->nbhqk', qp, kw).astype(jnp.float32) * (Dh ** -0.5)
    s = jnp.where(valid[None, :, None], s, -jnp.inf)
    m = jnp.max(s, axis=-1, keepdims=True)
    e = jnp.exp(s - m)
    l = jnp.sum(e, axis=-1, keepdims=True)
    o = jnp.einsum('nbhqk,nbkhd->nbqhd', (e / l).astype(v.dtype), vw)
    lse = (m + jnp.log(l))[..., 0]
    o = o.reshape(N, Lp, H, Dh)[:, :L]
    lse = lse.swapaxes(2, 3).reshape(N, Lp, H)[:, :L]
    return o, lse


def dilated_attention(dil_qkv, cos, sin):
    B, S, _ = dil_qkv.shape
    qkv = dil_qkv.reshape(B, S, DIL_GROUPS, 3, DIL_HEADS, DIL_HD)
    outs, lses = [], []
    for g, (window, dil) in enumerate(DIL_PAIRS):
        q = partial_rope(qkv[:, :, g, 0], cos, sin)
        k = partial_rope(qkv[:, :, g, 1], cos, sin)
        v = qkv[:, :, g, 2]
        o, lse = banded_attention(to_strided(q, dil), to_strided(k, dil), to_strided(v, dil),
                                  window // (2 * dil))
        outs.append(from_strided(o, B, dil))
        lses.append(from_strided(lse, B, dil))
    alpha = jax.nn.softmax(jnp.stack(lses, axis=0), axis=0)
    out = jnp.einsum('gbsh,gbshd->bshd', alpha, jnp.stack(outs, axis=0).astype(jnp.float32))
    return out.astype(dil_qkv.dtype).reshape(B, S, DIL_WIDTH)


def setup_inputs(seed: int = 0) -> dict:
    key = jax.random.key(seed)
    ks = jax.random.split(key, 10)
    f32 = jnp.float32
    x = jax.random.normal(ks[0], (BATCH, SEQ, D_MODEL), f32)
    norm_g = 1.0 + 0.02 * jax.random.normal(ks[1], (DEPTH, D_MODEL), f32)
    w_in = jax.random.normal(ks[2], (DEPTH, D_MODEL, IN_WIDTH), f32) * D_MODEL ** -0.5
    q_norm_g = 1.0 + 0.02 * jax.random.normal(ks[3], (DEPTH, Q_LORA), f32)
    kv_norm_g = 1.0 + 0.02 * jax.random.normal(ks[4], (DEPTH, KV_LORA), f32)
    w_uq = jax.random.normal(ks[5], (DEPTH, Q_LORA, MLA_HEADS * (MLA_NOPE + MLA_ROPE)), f32) * Q_LORA ** -0.5
    w_ukv = jax.random.normal(ks[6], (DEPTH, KV_LORA, MLA_HEADS * (MLA_NOPE + MLA_V)), f32) * KV_LORA ** -0.5
    w_out = jax.random.normal(ks[7], (DEPTH, MIX_WIDTH, D_MODEL), f32) * MIX_WIDTH ** -0.5
    final_g = 1.0 + 0.02 * jax.random.normal(ks[8], (D_MODEL,), f32)
    return {"x": x, "norm_g": norm_g, "w_in": w_in, "q_norm_g": q_norm_g, "kv_norm_g": kv_norm_g,
            "w_uq": w_uq, "w_ukv": w_ukv, "w_out": w_out, "final_g": final_g}


def reference(x, norm_g, w_in, q_norm_g, kv_norm_g, w_uq, w_ukv, w_out, final_g):
    S = x.shape[1]
    cos_m, sin_m = rope_tables(S, MLA_ROPE)
    cos_d, sin_d = rope_tables(S, ROT_DIM)
    split_points = [sum(IN_SPLITS[:i + 1]) for i in range(len(IN_SPLITS) - 1)]
    for layer in range(DEPTH):
        h = rms_norm(x, norm_g[layer])
        p = h @ w_in[layer]
        c_q, c_kv, k_r, gate_a, dil_qkv, gate_b = jnp.split(p, split_points, axis=-1)
        a = mla_attention(c_q, c_kv, k_r, q_norm_g[layer], kv_norm_g[layer],
                          w_uq[layer], w_ukv[layer], cos_m, sin_m) * jax.nn.silu(gate_a)
        b = dilated_attention(dil_qkv, cos_d, sin_d) * jax.nn.silu(gate_b)
        x = x + jnp.concatenate([a, b], axis=-1) @ w_out[layer]
    return rms_norm(x, final_g)
```

```python
import numpy as np
import ml_dtypes
from contextlib import ExitStack
import concourse.bass as bass
import concourse.mybir as mybir
from concourse.bass_utils import run_bass_kernel_spmd

F32 = mybir.dt.float32
BF16 = mybir.dt.bfloat16
AF = mybir.ActivationFunctionType
ALU = mybir.AluOpType
NPBF = ml_dtypes.bfloat16

D = 1024
T = 2048
S_FULL = 8192
NCORE = 8
DEPTH = 4
EPS = 1e-6
INW = 6304
DILS = (1, 4, 16)
HALO = tuple(64 * d for d in DILS)
MLA_SCALE = 96 ** -0.5
DIL_SCALE = 64 ** -0.5
MASKNEG = -30000.0


class Sched:
    EP = 2000
    ENGS = ("pe", "act", "dve", "pool", "sp")

    def __init__(self, nc, es):
        self.nc, self.es = nc, es
        self.ops = {e: [] for e in self.ENGS}
        self.nops = {e: 0 for e in self.ENGS}
        self.esems = {}
        self.lastw = {}
        self.readers = {}
        self.waited = {e: {} for e in self.ENGS}
        self.dsem = {}
        self.last_tok = {}
        self.pending_barrier = {e: None for e in self.ENGS}
        self.nsem = 0

    def _newsem(self, name):
        self.nsem += 1
        return self.es.enter_context(self.nc.semaphore(name))

    def _esem(self, e, epoch):
        k = (e, epoch)
        if k not in self.esems:
            self.esems[k] = self._newsem(f"e_{e}_{epoch}")
        return self.esems[k]

    def _waits(self, eng, reads, writes):
        toks = []
        for k in reads:
            w = self.lastw.get(k)
            if w is not None:
                toks.append(w)
        for k in writes:
            w = self.lastw.get(k)
            if w is not None:
                toks.append(w)
            toks.extend(self.readers.get(k, {}).values())
        if self.pending_barrier[eng] is not None:
            toks.extend(self.pending_barrier[eng])
            self.pending_barrier[eng] = None
        need = {}
        for (skey, sem, val, src) in toks:
            if src == eng and eng == "pe":
                continue
            if val > self.waited[eng].get(skey, 0):
                self.waited[eng][skey] = val
                need[skey] = (sem, val)
        return list(need.values())

    def _commit(self, tok, reads, writes):
        for k in writes:
            self.lastw[k] = tok
            self.readers[k] = {}
        for k in reads:
            if k in writes:
                continue
            self.readers.setdefault(k, {})[tok[0]] = tok
        self.last_tok[tok[0]] = tok

    def op(self, eng, fn, reads=(), writes=()):
        waits = self._waits(eng, reads, writes)
        idx = self.nops[eng]
        self.nops[eng] += 1
        epoch, val = idx // self.EP, idx % self.EP + 1
        sem = self._esem(eng, epoch)
        tok = ((eng, epoch), sem, val, eng)
        self.ops[eng].append((waits, fn, sem, 1))
        self._commit(tok, reads, writes)

    def dma(self, q, out, in_, reads=(), writes=(), slot=None):
        waits = self._waits(q, reads, writes)
        if slot not in self.dsem:
            self.dsem[slot] = [self._newsem(f"d_{len(self.dsem)}"), 0]
        ent = self.dsem[slot]
        ent[1] += 16
        tok = (("dma", slot), ent[0], ent[1], "dma")
        self.ops[q].append((waits, lambda e, o=out, i=in_: e.dma_start(out=o, in_=i), ent[0], 16))
        self._commit(tok, reads, writes)

    def barrier(self):
        toks = list(self.last_tok.values())
        for e in self.ENGS:
            self.pending_barrier[e] = list(toks)

    def wait_all(self, eng="sp"):
        self.barrier()
        waits = self._waits(eng, (), ())
        self.ops[eng].append((waits, None, None, 0))

    def emit(self):
        nc = self.nc
        ops = self.ops
        self.ops = {e: [] for e in self.ENGS}

        def replay(name, e):
            for waits, fn, sem, inc in ops[name]:
                for (s, v) in waits:
                    e.wait_ge(s, v)
                if fn is not None:
                    fn(e).then_inc(sem, inc)

        with nc.Block() as block:
            @block.tensor
            def _(e):
                replay("pe", e)

            @block.scalar
            def _(e):
                replay("act", e)

            @block.vector
            def _(e):
                replay("dve", e)

            @block.gpsimd
            def _(e):
                replay("pool", e)

            @block.sync
            def _(e):
                replay("sp", e)


def mm_group(lhs_rhs, out):
    def fn(pe):
        n = len(lhs_rhs)
        ins = None
        for i, (l, r) in enumerate(lhs_rhs):
            ins = pe.matmul(out, lhsT=l, rhs=r, start=(i == 0), stop=(i == n - 1))
        return ins
    return fn


def _rope_tables(pos, dim):
    inv = (1.0 / (500000.0 ** (np.arange(0, dim, 2, dtype=np.float32) / np.float32(dim)))).astype(np.float32)
    ang = pos.astype(np.float32)[:, None] * inv[None, :]
    return np.cos(ang).astype(np.float32), np.sin(ang).astype(np.float32)


def _consts_A(rank):
    pos = np.arange(rank * T, (rank + 1) * T)
    cm, sm = _rope_tables(pos, 32)
    cd, sd = _rope_tables(pos, 16)
    c96 = np.ones((96, T), np.float32)
    s96 = np.zeros((96, T), np.float32)
    c96[64:80] = cm.T
    c96[80:96] = cm.T
    s96[64:80] = sm.T
    s96[80:96] = sm.T
    c128 = np.ones((128, T), np.float32)
    s128 = np.zeros((128, T), np.float32)
    for b in (0, 64):
        c128[b:b + 8] = cd.T
        c128[b + 8:b + 16] = cd.T
        s128[b:b + 8] = sd.T
        s128[b + 8:b + 16] = sd.T
    r96 = np.zeros((96, 96), np.float32)
    for i in range(16):
        r96[80 + i, 64 + i] = -1.0
        r96[64 + i, 80 + i] = 1.0
    r128 = np.zeros((128, 128), np.float32)
    for b in (0, 64):
        for i in range(8):
            r128[b + 8 + i, b + i] = -1.0
            r128[b + i, b + 8 + i] = 1.0
    return {
        "c96": c96, "s96": s96, "c128": c128, "s128": s128,
        "r96": r96.astype(NPBF), "r128": r128.astype(NPBF),
        "ident": np.eye(128, dtype=np.float32).astype(NPBF),
    }


def _consts_B(rank):
    n = np.arange(128)[:, None]
    m = np.arange(128)[None, :]
    m0 = np.where(m <= n, 0.0, MASKNEG)
    m1 = np.where(m >= n, 0.0, MASKNEG)
    m0f = m0.copy()
    m1l = m1.copy()
    if rank == 0:
        m0f[:64, :] = MASKNEG
    if rank == 3:
        m1l[64:, :] = MASKNEG
    masks = np.stack([m0, m0f, m1, m1l]).astype(np.float32).astype(NPBF)
    e32 = np.zeros((32, 96), np.float32)
    for i in range(32):
        e32[i, 64 + i] = 1.0
    return {"masks": masks, "ident": np.eye(128, dtype=np.float32).astype(NPBF), "e32": e32.astype(NPBF)}


def build_A():
    nc = bass.Bass("TRN2", target_bir_lowering=False)
    dt_in = lambda n, s, d=F32: nc.dram_tensor(n, s, d, kind="ExternalInput").ap()
    dt_out = lambda n, s, d=BF16: nc.dram_tensor(n, s, d, kind="ExternalOutput").ap()
    x = dt_in("x", [T, D])
    w_in = dt_in("w_in", [D, INW])
    ng = dt_in("ng", [128, 8])
    w_uq = dt_in("w_uq", [384, 768])
    qg = dt_in("qg", [128, 3])
    c96d = dt_in("c96", [96, T])
    s96d = dt_in("s96", [96, T])
    c128d = dt_in("c128", [128, T])
    s128d = dt_in("s128", [128, T])
    r96d = dt_in("r96", [96, 96], BF16)
    r128d = dt_in("r128", [128, 128], BF16)
    identd = dt_in("ident", [128, 128], BF16)
    lat_o = dt_out("lat", [288, T])
    qT_o = dt_out("qT", [8, 96, T])
    gT_o = dt_out("gT", [1024, T])
    qd_o = dt_out("qd", [3, 512, T])
    kd_o = dt_out("kd", [3, 512, T])
    vd_o = dt_out("vd", [3, T, 512])
    emit_A(nc, x, w_in, ng, w_uq, qg, c96d, s96d, c128d, s128d, r96d, r128d, identd,
           lat_o, qT_o, gT_o, qd_o, kd_o, vd_o)
    return nc


def emit_A(nc, x, w_in, ng, w_uq, qg, c96d, s96d, c128d, s128d, r96d, r128d, identd,
           lat_o, qT_o, gT_o, qd_o, kd_o, vd_o):
    w_in_v = w_in.rearrange("(c p) n -> p c n", p=128)
    w_uq_v = w_uq.rearrange("(c p) n -> p c n", p=128)
    with ExitStack() as es:
        S = Sched(nc, es)
        sb = lambda n, s, d=BF16: es.enter_context(nc.sbuf_tensor(n, s, d))
        ps = lambda n, s, d=F32: es.enter_context(nc.psum_tensor(n, s, d))
        hT = sb("hT", [128, 8, T])
        c96 = sb("c96s", [96, T], F32)
        s96 = sb("s96s", [96, T], F32)
        c128 = sb("c128s", [128, T], F32)
        s128 = sb("s128s", [128, T], F32)
        r96 = sb("r96s", [96, 96])
        r128 = sb("r128s", [128, 128])
        ident = sb("idents", [128, 128])
        ones = sb("ones", [128, 128])
        ng8 = sb("ng8", [128, 8], F32)
        qg3 = sb("qg3", [128, 3], F32)
        xin = [sb(f"xin{i}", [128, D], F32) for i in range(2)]
        junk = sb("junk", [128, D], F32)
        hb = [sb(f"hb{i}", [128, D]) for i in range(2)]
        ssq = sb("ssq", [128, 16], F32)
        epst = sb("epst", [128, 1], F32)
        rstd = sb("rstd", [128, 16], F32)
        wf = [sb(f"wf{i}", [128, 8, 512], F32) for i in range(2)]
        wb = [sb(f"wb{i}", [128, 8, 512]) for i in range(2)]
        wkr = sb("wkr", [128, 8, 96])
        wuqf = sb("wuqf", [128, 3, 768], F32)
        wuqb = sb("wuqb", [128, 3, 768])
        cq_b = sb("cq_b", [128, 3, 512])
        sq_b = sb("sq_b", [128, 3, 512])
        cqn = sb("cqn", [128, 3, 512])
        rbc = sb("rbc", [128, 512], F32)
        qh = [sb(f"qh{i}", [96, 512]) for i in range(2)]
        t1 = [sb(f"t1_{i}", [128, 512], F32) for i in range(2)]
        t2 = [sb(f"t2_{i}", [128, 512], F32) for i in range(2)]
        pbf = [sb(f"pbf{i}", [128, 512]) for i in range(2)]
        ob = [sb(f"ob{i}", [128, 512]) for i in range(3)]
        pacc = [ps(f"pacc{i}", [128, 512]) for i in range(3)]
        prot = [ps(f"prot{i}", [128, 512]) for i in range(2)]
        pssq = ps("pssq", [128, 512])
        ptr = ps("ptr", [128, 1024], BF16)

        for i, (dst, src) in enumerate([(c96, c96d), (s96, s96d), (c128, c128d), (s128, s128d),
                                        (r96, r96d), (r128, r128d), (ident, identd), (ng8, ng), (qg3, qg)]):
            S.dma("sp", dst[:], src, writes=[dst.name], slot=("c", i))
        S.op("pool", lambda e: e.memset(ones[:], 1.0), writes=["ones"])
        S.op("pool", lambda e: e.memset(wkr[:], 0.0), writes=["wkr"])
        S.op("pool", lambda e: e.memset(ssq[:], 0.0), writes=["ssq"])
        S.op("pool", lambda e: e.memset(epst[:], EPS), writes=["epst"])
        S.dma("sp", wuqf[:], w_uq_v, writes=["wuqf"], slot=("wuq",))
        S.op("pool", lambda e: e.tensor_tensor(out=wuqb[:], in0=wuqf[:],
                                               in1=qg3[:, :].unsqueeze(2).broadcast_to([128, 3, 768]), op=ALU.mult),
             reads=["wuqf", qg3.name], writes=["wuqb"])

        for t in range(16):
            xs, hbs = xin[t % 2], hb[t % 2]
            S.dma("sp", xs[:], x[t * 128:(t + 1) * 128, :], writes=[xs.name], slot=("xin", t % 2))
            S.op("act", lambda e, xs=xs, t=t: e.activation(out=junk[:], in_=xs[:], func=AF.Square,
                                                           accum_out=ssq[:, t:t + 1]),
                 reads=[xs.name, "ssq"], writes=["junk", ("ssq", t)])
            S.op("act", lambda e, t=t: e.activation(out=rstd[:, t:t + 1], in_=ssq[:, t:t + 1], func=AF.Sqrt,
                                                    scale=1.0 / D, bias=epst[:, 0:1]),
                 reads=[("ssq", t), "epst"], writes=[("rstd", t)])
            S.op("dve", lambda e, t=t: e.reciprocal(out=rstd[:, t:t + 1], in_=rstd[:, t:t + 1]),
                 reads=[("rstd", t)], writes=[("rstd", t)])
            S.op("dve", lambda e, xs=xs, hbs=hbs, t=t: e.tensor_scalar(out=hbs[:], in0=xs[:], scalar1=rstd[:, t:t + 1],
                                                                       scalar2=None, op0=ALU.mult),
                 reads=[xs.name, ("rstd", t)], writes=[hbs.name])

            def tr(pe, hbs=hbs):
                ins = None
                for c in range(8):
                    ins = pe.transpose(out=ptr[:, c * 128:(c + 1) * 128], in_=hbs[:, c * 128:(c + 1) * 128],
                                       identity=ident[:])
                return ins
            S.op("pe", tr, reads=[hbs.name, ident.name], writes=["ptr"])
            S.op("act", lambda e, t=t: e.activation(out=hT[:, :, t * 128:(t + 1) * 128],
                                                    in_=ptr[:, :].rearrange("p (c n) -> p c n", c=8), func=AF.Copy),
                 reads=["ptr"], writes=[("hT", t // 4)])

        state = {"wg": 0, "pa": 0, "pr": 0, "ob": 0, "pb": 0, "qh": 0}

        def load_w(col0, ncols, kr=False):
            i = state["wg"] % 2
            state["wg"] += 1
            S.dma("sp", wf[i][:, :, 0:ncols], w_in_v[:, :, col0:col0 + ncols], writes=[wf[i].name], slot=("wf", i))
            S.op("pool", lambda e: e.tensor_tensor(out=wb[i][:, :, 0:ncols], in0=wf[i][:, :, 0:ncols],
                                                   in1=ng8[:, :].unsqueeze(2).broadcast_to([128, 8, ncols]), op=ALU.mult),
                 reads=[wf[i].name, ng8.name], writes=[wb[i].name])
            return wb[i]

        def nxt(k, n):
            v = state[k] % n
            state[k] += 1
            return v

        hTk = [("hT", i) for i in range(4)]

        def proj_fm(wt, c0, m, tb, out_ps):
            S.op("pe", mm_group([(wt[:, c, c0:c0 + m], hT[:, c, tb * 512:(tb + 1) * 512]) for c in range(8)],
                                out_ps[0:m, :]),
                 reads=[wt.name, ("hT", tb)], writes=[out_ps.name])

        def rope_store(src_bf, rows, rT, cs, sn, tb, dst_ap, p0=0):
            i = nxt("pr", 2)
            pr = prot[i]
            S.op("pe", mm_group([(rT[0:rows, 0:rows], src_bf[0:rows, :])], pr[0:rows, :]),
                 reads=[src_bf.name, rT.name], writes=[pr.name])
            a, b = t1[i], t2[i]
            tok = slice(tb * 512, (tb + 1) * 512)
            S.op("pool", lambda e: e.tensor_tensor(out=a[p0:rows, :], in0=src_bf[p0:rows, :], in1=cs[p0:rows, tok],
                                                   op=ALU.mult),
                 reads=[src_bf.name, cs.name], writes=[a.name])
            S.op("dve", lambda e: e.tensor_tensor(out=b[p0:rows, :], in0=pr[p0:rows, :], in1=sn[p0:rows, tok],
                                                  op=ALU.mult),
                 reads=[pr.name, sn.name], writes=[b.name])
            S.op("dve", lambda e: e.tensor_tensor(out=src_bf[p0:rows, :], in0=a[p0:rows, :], in1=b[p0:rows, :],
                                                  op=ALU.add),
                 reads=[a.name, b.name], writes=[src_bf.name])
            S.dma("pool", dst_ap, src_bf[dst_rows(p0, rows, dst_ap), :], reads=[src_bf.name], slot=("st", src_bf.name))

        def dst_rows(p0, rows, dst_ap):
            n = dst_ap.shape[0]
            return slice(rows - n, rows)

        def latent_norm(nchunk, width, tb):
            S.op("pe", mm_group([(ones[:, :], sq_b[:, c, :]) for c in range(nchunk)], pssq[:, :]),
                 reads=["sq_b", "ones"], writes=["pssq"])
            S.op("act", lambda e: e.activation(out=rbc[:], in_=pssq[:], func=AF.Sqrt, scale=1.0 / width,
                                               bias=epst[:, 0:1]), reads=["pssq", "epst"], writes=["rbc"])
            S.op("dve", lambda e: e.reciprocal(out=rbc[:], in_=rbc[:]), reads=["rbc"], writes=["rbc"])
            for c in range(nchunk):
                S.op("dve", lambda e, c=c: e.tensor_tensor(out=cqn[:, c, :], in0=cq_b[:, c, :], in1=rbc[:], op=ALU.mult),
                     reads=["cq_b", "rbc"], writes=["cqn"])

        def latent_chunks(wt, nchunk, tb):
            for c3 in range(nchunk):
                pa = pacc[nxt("pa", 3)]
                proj_fm(wt, c3 * 128, 128, tb, pa)
                S.op("act", lambda e, pa=pa, c3=c3: e.activation(out=cq_b[:, c3, :], in_=pa[:], func=AF.Copy),
                     reads=[pa.name], writes=["cq_b"])
                S.op("act", lambda e, pa=pa, c3=c3: e.activation(out=sq_b[:, c3, :], in_=pa[:], func=AF.Square),
                     reads=[pa.name], writes=["sq_b"])

        wt = load_w(0, 384)
        for tb in range(4):
            latent_chunks(wt, 3, tb)
            latent_norm(3, 384, tb)
            for h in range(8):
                pa = pacc[nxt("pa", 3)]
                q = qh[nxt("qh", 2)]
                S.op("pe", mm_group([(wuqb[:, c, h * 96:(h + 1) * 96], cqn[:, c, :]) for c in range(3)], pa[0:96, :]),
                     reads=["wuqb", "cqn"], writes=[pa.name])
                S.op("act", lambda e, pa=pa, q=q: e.activation(out=q[:], in_=pa[0:96, :], func=AF.Copy),
                     reads=[pa.name], writes=[q.name])
                rope_store(q, 96, r96, c96, s96, tb, qT_o[h, :, tb * 512:(tb + 1) * 512], p0=64)

        wt = load_w(384, 288)
        S.op("pool", lambda e: e.tensor_copy(out=wkr[:, :, 64:96], in_=wt[:, :, 256:288]), reads=[wt.name], writes=["wkr"])
        lat_v = lat_o[0:256, :].rearrange("(c p) t -> p c t", p=128)
        for tb in range(4):
            latent_chunks(wt, 2, tb)
            latent_norm(2, 256, tb)
            S.dma("pool", lat_v[:, :, tb * 512:(tb + 1) * 512], cqn[:, 0:2, :], reads=["cqn"], slot=("st", "cqn"))
            pa = pacc[nxt("pa", 3)]
            q = qh[nxt("qh", 2)]
            proj_fm(wkr, 0, 96, tb, pa)
            S.op("act", lambda e, pa=pa, q=q: e.activation(out=q[:], in_=pa[0:96, :], func=AF.Copy),
                 reads=[pa.name], writes=[q.name])
            rope_store(q, 96, r96, c96, s96, tb, lat_o[256:288, tb * 512:(tb + 1) * 512], p0=64)

        def gate_blocks(col0, row0):
            wt = load_w(col0, 512)
            for blk in range(4):
                for tb in range(4):
                    pa = pacc[nxt("pa", 3)]
                    o = ob[nxt("ob", 3)]
                    proj_fm(wt, blk * 128, 128, tb, pa)
                    S.op("act", lambda e, pa=pa, o=o: e.activation(out=o[:], in_=pa[:], func=AF.Silu),
                         reads=[pa.name], writes=[o.name])
                    S.dma("pool", gT_o[row0 + blk * 128: row0 + (blk + 1) * 128, tb * 512:(tb + 1) * 512], o[:],
                          reads=[o.name], slot=("st", o.name))

        def rope_blocks(col0, dst):
            wt = load_w(col0, 512)
            for blk in range(4):
                for tb in range(4):
                    pa = pacc[nxt("pa", 3)]
                    p = pbf[nxt("pb", 2)]
                    proj_fm(wt, blk * 128, 128, tb, pa)
                    S.op("act", lambda e, pa=pa, p=p: e.activation(out=p[:], in_=pa[:], func=AF.Copy),
                         reads=[pa.name], writes=[p.name])
                    rope_store(p, 128, r128, c128, s128, tb, dst[blk * 128:(blk + 1) * 128, tb * 512:(tb + 1) * 512])

        def v_blocks(col0, dst):
            wt = load_w(col0, 512)
            for t in range(16):
                pa = pacc[nxt("pa", 3)]
                o = ob[nxt("ob", 3)]
                S.op("pe", mm_group([(hT[:, c, t * 128:(t + 1) * 128], wt[:, c, 0:512]) for c in range(8)], pa[:, :]),
                     reads=[wt.name, ("hT", t // 4)], writes=[pa.name])
                S.op("act", lambda e, pa=pa, o=o: e.activation(out=o[:], in_=pa[:], func=AF.Copy),
                     reads=[pa.name], writes=[o.name])
                S.dma("pool", dst[t * 128:(t + 1) * 128, :], o[:], reads=[o.name], slot=("st", o.name))

        gate_blocks(672, 0)
        for g in range(3):
            base = 1184 + g * 1536
            rope_blocks(base, qd_o[g])
            rope_blocks(base + 512, kd_o[g])
            v_blocks(base + 1024, vd_o[g])
        gate_blocks(5792, 512)
        S.wait_all("sp")
        S.emit()


_CACHE = {}


def _get(name, builder):
    if name not in _CACHE:
        _CACHE[name] = builder()
    return _CACHE[name]


def run_A(x_shards, w_in_l, ng_l, w_uq_l, qg_l):
    nc = _get("A", build_A)
    in_maps = []
    for c in range(NCORE):
        m = {"x": x_shards[c], "w_in": w_in_l, "ng": ng_l, "w_uq": w_uq_l, "qg": qg_l}
        m.update(_get(("cA", c % 4), lambda: _consts_A(c % 4)))
        in_maps.append(m)
    res = run_bass_kernel_spmd(nc, in_maps, core_ids=list(range(NCORE)))
    return res.results


def build_B(mla_heads=tuple(range(8)), dil_hps=tuple(range(4)), debug=False, dil_groups=(0, 1, 2)):
    nc = bass.Bass("TRN2", target_bir_lowering=False)
    dt_in = lambda n, s, d=F32: nc.dram_tensor(n, s, d, kind="ExternalInput").ap()
    dt_out = lambda n, s, d=F32: nc.dram_tensor(n, s, d, kind="ExternalOutput").ap()
    a = dict(
        x=dt_in("x", [T, D]),
        latT=dt_in("latT", [288, S_FULL], BF16),
        qT=dt_in("qT", [8, 96, T], BF16),
        gT=dt_in("gT", [1024, T], BF16),
        qd=dt_in("qd", [3, 512, T], BF16),
        kd=[dt_in(f"kd{g}", [512, T + 2 * HALO[g]], BF16) for g in range(3)],
        vd=[dt_in(f"vd{g}", [T + 2 * HALO[g], 512], BF16) for g in range(3)],
        w_ukv=dt_in("w_ukv", [256, 1024]),
        kvg=dt_in("kvg", [128, 2]),
        w_out=dt_in("w_out", [D, D]),
        fg=dt_in("fg", [1, D]),
        masks=dt_in("masks", [4, 128, 128], BF16),
        ident=dt_in("ident", [128, 128], BF16),
        e32=dt_in("e32", [32, 96], BF16),
        xo=dt_out("xo", [T, D]),
        yo=dt_out("yo", [T, D]),
    )
    if debug:
        a["mixo"] = nc.dram_tensor("mixo", [1024, T], BF16, kind="ExternalOutput").ap()
    emit_B(nc, mla_heads=mla_heads, dil_hps=dil_hps, dil_groups=dil_groups, **a)
    return nc


def emit_B(nc, x, latT, qT, gT, qd, kd, vd, w_ukv, kvg, w_out, fg, masks, ident, e32, xo, yo,
           mla_heads=tuple(range(8)), dil_hps=tuple(range(4)), mixo=None, dil_groups=(0, 1, 2)):
    with ExitStack() as es:
        S = Sched(nc, es)
        sb = lambda n, s, d=BF16: es.enter_context(nc.sbuf_tensor(n, s, d))
        pb = [es.enter_context(nc.psum_tensor(f"pb{i}", [128, 512], F32)) for i in range(8)]
        pS, pAcc, pK = pb[0:3], pb[3:5], pb[5:7]
        mixedT = sb("mixedT", [128, 8, T])
        identS = sb("identS", [128, 128])
        e32S = sb("e32S", [32, 96])
        maskS = sb("maskS", [128, 4, 128])
        onesf = sb("onesf", [128, 64], F32)
        kvg2 = sb("kvg2", [128, 2], F32)
        epst = sb("epst", [128, 1], F32)
        rd = sb("rd", [128, 512], F32)
        tmp = [sb(f"tmp{i}", [128, 512], F32) for i in range(2)]
        gts = [sb(f"gts{i}", [128, 512]) for i in range(2)]
        S.dma("sp", identS[:], ident, writes=["identS"], slot=("c", 0))
        S.dma("sp", e32S[:], e32, writes=["e32S"], slot=("c", 1))
        S.dma("sp", maskS[:], masks.rearrange("k n m -> n k m"), writes=["maskS"], slot=("c", 2))
        S.dma("sp", kvg2[:], kvg, writes=["kvg2"], slot=("c", 3))
        S.op("pool", lambda e: e.memset(onesf[:], 1.0), writes=["onesf"])
        S.op("pool", lambda e: e.memset(epst[:], EPS), writes=["epst"])
        st = {"tmp": 0, "gts": 0, "acc": 0, "pk": 0}

        def nxt(k, n):
            v = st[k] % n
            st[k] += 1
            return v

        def normalize_gate(src_o, src_den, gate_ap, gate_reads, po, chunk, qb, src_reads):
            tok = slice(qb * 512, (qb + 1) * 512)
            pk = pK[nxt("pk", 2)]
            tm = tmp[nxt("tmp", 2)]
            S.op("dve", lambda e: e.reciprocal(out=rd[64:65, :], in_=src_den), reads=src_reads, writes=["rd"])
            S.op("pe", mm_group([(onesf[64:65, 0:64], rd[64:65, :])], pk[0:64, :]), reads=["rd", "onesf"],
                 writes=[pk.name])
            S.op("dve", lambda e: e.tensor_tensor(out=tm[po:po + 64, :], in0=src_o, in1=pk[0:64, :], op=ALU.mult),
                 reads=src_reads + [pk.name], writes=[tm.name])
            S.op("pool", lambda e: e.tensor_tensor(out=mixedT[po:po + 64, chunk, tok], in0=tm[po:po + 64, :],
                                                   in1=gate_ap, op=ALU.mult),
                 reads=[tm.name] + gate_reads, writes=[("mixedT", chunk)])

        with ExitStack() as es1:
            sb1 = lambda n, s, d=BF16: es1.enter_context(nc.sbuf_tensor(n, s, d))
            lat = sb1("lat", [128, 3, S_FULL])
            KT = sb1("KT", [96, S_FULL])
            V = sb1("V", [128, 64, 65])
            Qh = [sb1(f"Qh{i}", [96, T]) for i in range(2)]
            PT = [sb1(f"PT{i}", [128, 512]) for i in range(3)]
            osb = sb1("osb", [64, 512], F32)
            wukvf = sb1("wukvf", [128, 2, 1024], F32)
            wk96 = sb1("wk96", [128, 2, 8, 96])
            wv = sb1("wv", [128, 2, 8, 64])
            S.dma("sp", wukvf[:], w_ukv.rearrange("(c p) n -> p c n", p=128), writes=["wukvf"], slot=("wukv",))
            S.op("pool", lambda e: e.memset(wk96[:], 0.0), writes=["wk96"])
            S.op("pool", lambda e: e.memset(V[:], 1.0), writes=["V"])
            for c in range(2):
                wsrc = wukvf[:, c, :].rearrange("p (h k) -> p h k", h=8)
                S.op("pool", lambda e, c=c, wsrc=wsrc: e.tensor_tensor(
                    out=wk96[:, c, :, 0:64], in0=wsrc[:, :, 0:64],
                    in1=kvg2[:, c:c + 1].unsqueeze(2).broadcast_to([128, 8, 64]), op=ALU.mult),
                    reads=["wukvf", "kvg2"], writes=["wk96"])
                S.op("pool", lambda e, c=c, wsrc=wsrc: e.tensor_tensor(
                    out=wv[:, c, :, :], in0=wsrc[:, :, 64:128],
                    in1=kvg2[:, c:c + 1].unsqueeze(2).broadcast_to([128, 8, 64]), op=ALU.mult),
                    reads=["wukvf", "kvg2"], writes=["wv"])
            lat_v = latT[0:256, :].rearrange("(c p) t -> p c t", p=128)
            for i in range(4):
                sl = slice(i * 2048, (i + 1) * 2048)
                S.dma("sp", lat[:, 0:2, sl], lat_v[:, :, sl], writes=[("lat", i)], slot=("lat", i))
                S.dma("sp", lat[0:32, 2, sl], latT[256:288, sl], writes=[("latr", i)], slot=("latr", i))

            for h in mla_heads:
                po, chunk = (h % 2) * 64, h // 2
                Q = Qh[h % 2]
                S.dma("sp", Q[:], qT[h], writes=[Q.name], slot=("Q", h % 2))
                for tb in range(16):
                    pk = pK[nxt("pk", 2)]
                    sl = slice(tb * 512, (tb + 1) * 512)
                    S.op("pe", mm_group([(wk96[:, 0, h, :], lat[:, 0, sl]), (wk96[:, 1, h, :], lat[:, 1, sl]),
                                         (e32S[0:32, :], lat[0:32, 2, sl])], pk[0:96, :]),
                         reads=["wk96", "e32S", ("lat", tb // 4), ("latr", tb // 4)], writes=[pk.name])
                    eng = "dve" if tb % 2 else "act"
                    if eng == "act":
                        S.op("act", lambda e, pk=pk, sl=sl: e.activation(out=KT[:, sl], in_=pk[0:96, :], func=AF.Copy),
                             reads=[pk.name], writes=[("KT", tb)])
                    else:
                        S.op("dve", lambda e, pk=pk, sl=sl: e.tensor_copy(out=KT[:, sl], in_=pk[0:96, :]),
                             reads=[pk.name], writes=[("KT", tb)])
                for k8 in range(8):
                    pk = pK[nxt("pk", 2)]

                    def vb(pe, pk=pk, k8=k8, h=h):
                        ins = None
                        for j in range(8):
                            kt = k8 * 8 + j
                            for c in range(2):
                                ins = pe.matmul(pk[:, j * 64:(j + 1) * 64], lhsT=lat[:, c, kt * 128:(kt + 1) * 128],
                                                rhs=wv[:, c, h, :], start=(c == 0), stop=(c == 1))
                        return ins
                    S.op("pe", vb, reads=["wv", ("lat", k8 // 2)], writes=[pk.name])
                    S.op("dve", lambda e, pk=pk, k8=k8: e.tensor_copy(
                        out=V[:, k8 * 8:(k8 + 1) * 8, 0:64], in_=pk[:, :].rearrange("p (j d) -> p j d", j=8)),
                        reads=[pk.name], writes=[("V", k8)])
                for qb in range(4):
                    qsl = slice(qb * 512, (qb + 1) * 512)
                    acc = pAcc[nxt("acc", 2)]
                    g_i = nxt("gts", 2)
                    S.dma("sp", gts[g_i][po:po + 64, :], gT[h * 64:(h + 1) * 64, qsl], writes=[gts[g_i].name],
                          slot=("gts", g_i))

                    def qk(kt):
                        s_ = pS[kt % 3]
                        S.op("pe", mm_group([(KT[:, kt * 128:(kt + 1) * 128], Q[:, qsl])], s_[:, :]),
                             reads=[("KT", kt // 4), Q.name], writes=[s_.name])
                    qk(0)
                    qk(1)
                    for kt in range(64):
                        s_, p_ = pS[kt % 3], PT[kt % 3]
                        S.op("act", lambda e, s_=s_, p_=p_: e.activation(out=p_[:], in_=s_[:], func=AF.Exp,
                                                                         scale=MLA_SCALE),
                             reads=[s_.name], writes=[p_.name])
                        if kt + 2 < 64:
                            qk(kt + 2)
                        S.op("pe", lambda pe, kt=kt, p_=p_, acc=acc: pe.matmul(
                            acc[0:65, :], lhsT=V[:, kt, :], rhs=p_[:], start=(kt == 0), stop=(kt == 63)),
                            reads=[p_.name, ("V", kt // 8)], writes=[acc.name])
                    S.op("act", lambda e, acc=acc: e.activation(out=osb[:], in_=acc[0:64, :], func=AF.Copy),
                         reads=[acc.name], writes=["osb"])
                    normalize_gate(osb[0:64, :], acc[64:65, :], gts[g_i][po:po + 64, :], [gts[g_i].name], po, chunk, qb,
                                   ["osb", acc.name])
            S.barrier()
            S.emit()

        with ExitStack() as es2:
            sb2 = lambda n, s, d=BF16: es2.enter_context(nc.sbuf_tensor(n, s, d))
            Kg = [sb2(f"Kg{i}", [128, T + 2 * HALO[2]]) for i in range(2)]
            Qg = [sb2(f"Qg{i}", [128, T]) for i in range(2)]
            Vraw = [sb2(f"Vraw{i}", [128, 32, 128]) for i in range(2)]
            Vaug = [sb2(f"Vaug{i}", [128, 32, 2, 65]) for i in range(2)]
            accT = sb2("accT", [65, 2, T], F32)
            gtb = sb2("gtb", [128, T])
            P2 = [sb2(f"P2_{i}", [128, 2, 128]) for i in range(3)]
            for i in range(2):
                S.op("pool", lambda e, i=i: e.memset(Vaug[i][:], 1.0), writes=[Vaug[i].name])
            li = 0
            for hp in dil_hps:
                S.dma("sp", gtb[:], gT[512 + hp * 128: 512 + (hp + 1) * 128, :], writes=["gtb"], slot=("gtb",))
                for g in dil_groups:
                    d, halo = DILS[g], HALO[g]
                    W = T + 2 * halo
                    nbr = 16 // d + 1
                    njb = 16 // d
                    bi = li % 2
                    li += 1
                    kg, qg_, vr, va = Kg[bi], Qg[bi], Vraw[bi], Vaug[bi]
                    S.dma("sp", kg[:, 0:W], kd[g][hp * 128:(hp + 1) * 128, :], writes=[kg.name], slot=("kg", bi))
                    S.dma("sp", qg_[:], qd[g, hp * 128:(hp + 1) * 128, :], writes=[qg_.name], slot=("qg", bi))
                    vsrc = vd[g].rearrange("(kb n r) c -> n r kb c", n=128, r=d)
                    for r in range(d):
                        S.dma("sp", vr[:, r * nbr:(r + 1) * nbr, :], vsrc[:, r, :, hp * 128:(hp + 1) * 128],
                              writes=[vr.name], slot=("vr", bi, r))
                    nblk = d * nbr
                    for hl in range(2):
                        S.op("pool", lambda e, hl=hl, va=va, vr=vr, nblk=nblk: e.tensor_copy(
                            out=va[:, 0:nblk, hl, 0:64], in_=vr[:, 0:nblk, hl * 64:(hl + 1) * 64]),
                            reads=[vr.name], writes=[va.name])
                    accv = [accT[:, hl, :].rearrange("p (jb m r) -> p r jb m", r=d, m=128) for hl in range(2)]
                    tiles = []
                    for hl in range(2):
                        if d == 1:
                            quads = [[(0, 4 * q + j) for j in range(4)] for q in range(4)]
                            dsts = [accv[hl][:, 0, 4 * q:4 * q + 4, :] for q in range(4)]
                        elif d == 4:
                            quads = [[(r, j) for j in range(4)] for r in range(4)]
                            dsts = [accv[hl][:, r, 0:4, :] for r in range(4)]
                        else:
                            quads = [[(4 * q + j, 0) for j in range(4)] for q in range(4)]
                            dsts = [accv[hl][:, 4 * q:4 * q + 4, 0, :] for q in range(4)]
                        for quad, dst in zip(quads, dsts):
                            for j, (r, jb) in enumerate(quad):
                                tiles.append((hl, r, jb, j, dst))

                    def qk2(i):
                        hl, r, jb, j, dst = tiles[i]
                        po = hl * 64
                        s_ = pS[i % 3]
                        prs = []
                        for half in range(2):
                            kb = jb + half
                            k0 = r + d * kb * 128
                            q0 = r + d * jb * 128
                            mi = (1 if jb == 0 else 0) if half == 0 else (3 if jb == njb - 1 else 2)
                            prs.append((half, kg[po:po + 64, k0:k0 + d * 127 + 1:d], qg_[po:po + 64, q0:q0 + d * 127 + 1:d], mi))

                        def fn(pe):
                            ins = None
                            for half, kap, qap, mi in prs:
                                pe.matmul(s_[:, half * 128:(half + 1) * 128], lhsT=kap, rhs=qap, start=True, stop=False)
                                ins = pe.matmul(s_[:, half * 128:(half + 1) * 128], lhsT=identS[:, :],
                                                rhs=maskS[:, mi, :], start=False, stop=True)
                            return ins
                        S.op("pe", fn, reads=[kg.name, qg_.name, "identS", "maskS"], writes=[s_.name])
                    qk2(0)
                    for i, (hl, r, jb, j, dst) in enumerate(tiles):
                        s_, p_ = pS[i % 3], P2[i % 3]
                        pv = pAcc[(i // 4) % 2]
                        S.op("act", lambda e, s_=s_, p_=p_: e.activation(
                            out=p_[:, :, :], in_=s_[:, 0:256].rearrange("p (a m) -> p a m", a=2), func=AF.Exp,
                            scale=DIL_SCALE), reads=[s_.name], writes=[p_.name])
                        if i + 1 < len(tiles):
                            qk2(i + 1)

                        def pvf(pe, r=r, jb=jb, j=j, hl=hl, p_=p_, pv=pv, va=va, nbr=nbr):
                            ins = None
                            for half in range(2):
                                ins = pe.matmul(pv[0:65, j * 128:(j + 1) * 128], lhsT=va[:, r * nbr + jb + half, hl, :],
                                                rhs=p_[:, half, :], start=(half == 0), stop=(half == 1))
                            return ins
                        S.op("pe", pvf, reads=[p_.name, va.name], writes=[pv.name])
                        if j == 3:
                            src = pv[0:65, :].rearrange("p (a m) -> p a m", a=4)
                            if g == dil_groups[0]:
                                S.op("act", lambda e, dst=dst, src=src: e.activation(out=dst, in_=src, func=AF.Copy),
                                     reads=[pv.name], writes=[("accT", hl)])
                            else:
                                S.op("dve", lambda e, dst=dst, src=src: e.tensor_tensor(out=dst, in0=src, in1=dst,
                                                                                        op=ALU.add),
                                     reads=[pv.name, ("accT", hl)], writes=[("accT", hl)])
                for hl in range(2):
                    po = hl * 64
                    for qb in range(4):
                        tok = slice(qb * 512, (qb + 1) * 512)
                        normalize_gate(accT[0:64, hl, tok], accT[64:65, hl, tok], gtb[po:po + 64, tok], ["gtb"], po,
                                       4 + hp, qb, [("accT", hl)])
            S.barrier()
            S.emit()

        with ExitStack() as es3:
            sb3 = lambda n, s, d=BF16: es3.enter_context(nc.sbuf_tensor(n, s, d))
            wof = sb3("wof", [128, 8, 512], F32)
            wob = sb3("wob", [128, 8, D])
            fgb = sb3("fgb", [128, D], F32)
            xin = [sb3(f"xin{i}", [128, D], F32) for i in range(2)]
            xn = [sb3(f"xn{i}", [128, D], F32) for i in range(2)]
            yv = [sb3(f"yv{i}", [128, D], F32) for i in range(2)]
            junk = sb3("junk", [128, D], F32)
            ssq = sb3("ssq", [128, 16], F32)
            rstd = sb3("rstd", [128, 16], F32)
            S.op("pool", lambda e: e.memset(ssq[:], 0.0), writes=["ssq"])
            S.dma("sp", fgb[:], fg.partition_broadcast(128), writes=["fgb"], slot=("fgb",))
            w_out_v = w_out.rearrange("(c p) n -> p c n", p=128)
            if mixo is not None:
                S.dma("pool", mixo.rearrange("(c p) t -> p c t", p=128), mixedT[:], reads=[("mixedT", c) for c in range(8)],
                      slot=("mixo",))
            for nb in range(2):
                S.dma("sp", wof[:], w_out_v[:, :, nb * 512:(nb + 1) * 512], writes=["wof"], slot=("wof",))
                S.op("pool", lambda e, nb=nb: e.tensor_copy(out=wob[:, :, nb * 512:(nb + 1) * 512], in_=wof[:]),
                     reads=["wof"], writes=["wob"])
            for t in range(16):
                xs, xnn, yy = xin[t % 2], xn[t % 2], yv[t % 2]
                S.dma("sp", xs[:], x[t * 128:(t + 1) * 128, :], writes=[xs.name], slot=("xin", t % 2))
                for nb in range(2):
                    pa = pS[(2 * t + nb) % 3]
                    S.op("pe", mm_group([(mixedT[:, c, t * 128:(t + 1) * 128], wob[:, c, nb * 512:(nb + 1) * 512])
                                         for c in range(8)], pa[:, :]),
                         reads=["wob"] + [("mixedT", c) for c in range(8)], writes=[pa.name])
                    S.op("dve", lambda e, pa=pa, nb=nb, xs=xs, xnn=xnn: e.tensor_tensor(
                        out=xnn[:, nb * 512:(nb + 1) * 512], in0=pa[:, :], in1=xs[:, nb * 512:(nb + 1) * 512], op=ALU.add),
                        reads=[pa.name, xs.name], writes=[xnn.name])
                S.dma("pool", xo[t * 128:(t + 1) * 128, :], xnn[:], reads=[xnn.name], slot=("xo", t % 2))
                S.op("act", lambda e, xnn=xnn, t=t: e.activation(out=junk[:], in_=xnn[:], func=AF.Square,
                                                                 accum_out=ssq[:, t:t + 1]),
                     reads=[xnn.name, "ssq"], writes=["junk", ("ssq", t)])
                S.op("act", lambda e, t=t: e.activation(out=rstd[:, t:t + 1], in_=ssq[:, t:t + 1], func=AF.Sqrt,
                                                        scale=1.0 / D, bias=epst[:, 0:1]),
                     reads=[("ssq", t), "epst"], writes=[("rstd", t)])
                S.op("dve", lambda e, t=t: e.reciprocal(out=rstd[:, t:t + 1], in_=rstd[:, t:t + 1]),
                     reads=[("rstd", t)], writes=[("rstd", t)])
                S.op("dve", lambda e, xnn=xnn, yy=yy, t=t: e.scalar_tensor_tensor(
                    out=yy[:], in0=xnn[:], scalar=rstd[:, t:t + 1], in1=fgb[:], op0=ALU.mult, op1=ALU.mult),
                    reads=[xnn.name, ("rstd", t), "fgb"], writes=[yy.name])
                S.dma("pool", yo[t * 128:(t + 1) * 128, :], yy[:], reads=[yy.name], slot=("yo", t % 2))
            S.wait_all("sp")
            S.emit()


def run_B(in_maps):
    nc = _get("B", build_B)
    res = run_bass_kernel_spmd(nc, in_maps, core_ids=list(range(NCORE)))
    return res.results


def _exchange(resA, x_shards, w_ukv_l, kvg_l, w_out_l, fg):
    in_maps = []
    for b in range(2):
        cores = [4 * b + r for r in range(4)]
        lat_full = np.concatenate([np.asarray(resA[c]["lat"]) for c in cores], axis=1)
        kd_full = np.concatenate([np.asarray(resA[c]["kd"]) for c in cores], axis=2)
        vd_full = np.concatenate([np.asarray(resA[c]["vd"]) for c in cores], axis=1)
        for r in range(4):
            c = cores[r]
            m = {"x": x_shards[c], "latT": lat_full, "qT": np.asarray(resA[c]["qT"]), "gT": np.asarray(resA[c]["gT"]),
                 "qd": np.asarray(resA[c]["qd"]), "w_ukv": w_ukv_l, "kvg": kvg_l, "w_out": w_out_l, "fg": fg}
            for g in range(3):
                hl = HALO[g]
                kp = np.zeros((512, S_FULL + 2 * hl), dtype=kd_full.dtype)
                kp[:, hl:hl + S_FULL] = kd_full[g]
                vp = np.zeros((S_FULL + 2 * hl, 512), dtype=vd_full.dtype)
                vp[hl:hl + S_FULL] = vd_full[g]
                m[f"kd{g}"] = np.ascontiguousarray(kp[:, r * T: r * T + T + 2 * hl])
                m[f"vd{g}"] = np.ascontiguousarray(vp[r * T: r * T + T + 2 * hl])
            m.update(_get(("cB", r), lambda: _consts_B(r)))
            in_maps.append(m)
    return in_maps


def kernel(x, norm_g, w_in, q_norm_g, kv_norm_g, w_uq, w_ukv, w_out, final_g):
    x = np.asarray(x, dtype=np.float32)
    xs = [np.ascontiguousarray(x[c // 4, (c % 4) * T:(c % 4 + 1) * T]) for c in range(NCORE)]
    fg = np.ascontiguousarray(np.asarray(final_g, np.float32).reshape(1, D))
    ys = None
    for l in range(DEPTH):
        ng8 = np.ascontiguousarray(np.asarray(norm_g[l], np.float32).reshape(8, 128).T)
        qg3 = np.ascontiguousarray(np.asarray(q_norm_g[l], np.float32).reshape(3, 128).T)
        kvg2 = np.ascontiguousarray(np.asarray(kv_norm_g[l], np.float32).reshape(2, 128).T)
        resA = run_A(xs, np.ascontiguousarray(w_in[l], dtype=np.float32), ng8,
                     np.ascontiguousarray(w_uq[l], dtype=np.float32), qg3)
        in_maps = _exchange(resA, xs, np.ascontiguousarray(w_ukv[l], dtype=np.float32), kvg2,
                            np.ascontiguousarray(w_out[l], dtype=np.float32), fg)
        resB = run_B(in_maps)
        xs = [np.asarray(resB[c]["xo"]) for c in range(NCORE)]
        ys = [np.asarray(resB[c]["yo"]) for c in range(NCORE)]
    out = np.zeros((2, S_FULL, D), np.float32)
    for c in range(NCORE):
        out[c // 4, (c % 4) * T:(c % 4 + 1) * T] = ys[c]
    return out
```

```python
import numpy as np
import ml_dtypes
from contextlib import ExitStack
import concourse.bass as bass
import concourse.mybir as mybir
from concourse.bass_utils import run_bass_kernel_spmd

F32 = mybir.dt.float32
BF16 = mybir.dt.bfloat16
AF = mybir.ActivationFunctionType
ALU = mybir.AluOpType
NPBF = ml_dtypes.bfloat16

D = 1024
T = 2048
S_FULL = 8192
NCORE = 8
DEPTH = 4
EPS = 1e-6
INW = 6304
DILS = (1, 4, 16)
HALO = tuple(64 * d for d in DILS)
MLA_SCALE = 96 ** -0.5
DIL_SCALE = 64 ** -0.5
MASKNEG = -32768.0
OVERLAP_EXCHANGE = True


class Sched:
    EP = 30000
    ENGS = ("pe", "act", "dve", "pool", "sp")

    def __init__(self, nc, es):
        self.nc, self.es = nc, es
        self.ops = {e: [] for e in self.ENGS}
        self.nops = {e: 0 for e in self.ENGS}
        self.esems = {}
        self.lastw = {}
        self.readers = {}
        self.waited = {e: {} for e in self.ENGS}
        self.dsem = {}
        self.last_tok = {}
        self.pending_barrier = {e: None for e in self.ENGS}
        self.nsem = 0

    def _newsem(self, name):
        self.nsem += 1
        return self.es.enter_context(self.nc.semaphore(name))

    def _esem(self, e, epoch):
        k = (e, epoch)
        if k not in self.esems:
            self.esems[k] = self._newsem(f"e_{e}_{epoch}")
        return self.esems[k]

    def _waits(self, eng, reads, writes):
        toks = []
        for k in reads:
            w = self.lastw.get(k)
            if w is not None:
                toks.append(w)
        for k in writes:
            w = self.lastw.get(k)
            if w is not None:
                toks.append(w)
            toks.extend(self.readers.get(k, {}).values())
        if self.pending_barrier[eng] is not None:
            toks.extend(self.pending_barrier[eng])
            self.pending_barrier[eng] = None
        need = {}
        for (skey, sem, val, src) in toks:
            if src == eng and eng == "pe":
                continue
            if val > self.waited[eng].get(skey, 0):
                self.waited[eng][skey] = val
                need[skey] = (sem, val)
        return list(need.values())

    def _commit(self, tok, reads, writes):
        for k in writes:
            self.lastw[k] = tok
            self.readers[k] = {}
        for k in reads:
            if k in writes:
                continue
            self.readers.setdefault(k, {})[tok[0]] = tok
        self.last_tok[tok[0]] = tok

    def op(self, eng, fn, reads=(), writes=()):
        waits = self._waits(eng, reads, writes)
        idx = self.nops[eng]
        self.nops[eng] += 1
        epoch, val = idx // self.EP, idx % self.EP + 1
        sem = self._esem(eng, epoch)
        tok = ((eng, epoch), sem, val, eng)
        self.ops[eng].append((waits, fn, sem, 1))
        self._commit(tok, reads, writes)

    def dma(self, q, out, in_, reads=(), writes=(), slot=None):
        waits = self._waits(q, reads, writes)
        if slot not in self.dsem:
            self.dsem[slot] = [self._newsem(f"d_{len(self.dsem)}"), 0]
        ent = self.dsem[slot]
        ent[1] += 16
        tok = (("dma", slot), ent[0], ent[1], "dma")
        self.ops[q].append((waits, lambda e, o=out, i=in_: e.dma_start(out=o, in_=i), ent[0], 16))
        self._commit(tok, reads, writes)

    def custom(self, eng, fn, slot, reads=(), writes=()):
        waits = self._waits(eng, reads, writes)
        if slot not in self.dsem:
            self.dsem[slot] = [self._newsem(f"d_{len(self.dsem)}"), 0]
        ent = self.dsem[slot]
        ent[1] += 1
        tok = (("dma", slot), ent[0], ent[1], "dma")
        self.ops[eng].append((waits, fn, ent[0], 1))
        self._commit(tok, reads, writes)

    def barrier(self):
        toks = list(self.last_tok.values())
        for e in self.ENGS:
            self.pending_barrier[e] = list(toks)

    def wait_all(self, eng="sp"):
        self.barrier()
        waits = self._waits(eng, (), ())
        self.ops[eng].append((waits, None, None, 0))

    def emit(self):
        nc = self.nc
        ops = self.ops
        self.ops = {e: [] for e in self.ENGS}

        def replay(name, e):
            for waits, fn, sem, inc in ops[name]:
                for (s, v) in waits:
                    e.wait_ge(s, v)
                if fn is not None:
                    fn(e).then_inc(sem, inc)

        with nc.Block() as block:
            @block.tensor
            def _(e):
                replay("pe", e)

            @block.scalar
            def _(e):
                replay("act", e)

            @block.vector
            def _(e):
                replay("dve", e)

            @block.gpsimd
            def _(e):
                replay("pool", e)

            @block.sync
            def _(e):
                replay("sp", e)


def pay_tensors(nc, name, mult):
    t = {}
    t[(0,)] = nc.dram_tensor(f"{name}_g0", [mult * 4 * HALO[0], 512], BF16)
    t[(1,)] = nc.dram_tensor(f"{name}_g1", [mult * 4 * HALO[1], 512], BF16)
    for k in range(4):
        t[(2, k)] = nc.dram_tensor(f"{name}_g2_{k}", [mult * HALO[2], 512], BF16)
    return t


def pay_piece(tens, g, kind, side):
    k = kind * 2 + side
    if g == 2:
        return tens[(2, k)], 0, HALO[2]
    return tens[(g,)], k * HALO[g], 4 * HALO[g]


def pay_k(tens, j, g, side, f0, nf, t0, nt, ncand=None):
    halo = HALO[g]
    h, roff, rpr = pay_piece(tens, g, 0, side)
    base = (j * rpr + roff) * 512 + f0 * halo + t0
    if ncand is None:
        return bass.AP(h, base, [[halo, nf], [1, nt]])
    return bass.AP(h, base, [[halo, nf], [rpr * 512, ncand], [1, nt]])


def pay_v(tens, j, g, side, r0, nr):
    h, roff, rpr = pay_piece(tens, g, 1, side)
    base = (j * rpr + roff + r0) * 512
    return bass.AP(h, base, [[512, nr], [1, 512]])


def pay_v_cand(tens, j, g, side, hp, d):
    h, roff, rpr = pay_piece(tens, g, 1, side)
    base = (j * rpr + roff) * 512 + hp * 128
    return bass.AP(h, base, [[d * 512, 64], [512, d], [1, 128]])


def mm_group(lhs_rhs, out):
    def fn(pe):
        n = len(lhs_rhs)
        ins = None
        for i, (l, r) in enumerate(lhs_rhs):
            ins = pe.matmul(out, lhsT=l, rhs=r, start=(i == 0), stop=(i == n - 1))
        return ins
    return fn


def _rope_tables(pos, dim):
    inv = (1.0 / (500000.0 ** (np.arange(0, dim, 2, dtype=np.float32) / np.float32(dim)))).astype(np.float32)
    ang = pos.astype(np.float32)[:, None] * inv[None, :]
    return np.cos(ang).astype(np.float32), np.sin(ang).astype(np.float32)


def _consts_A(rank):
    pos = np.arange(rank * T, (rank + 1) * T)
    cm, sm = _rope_tables(pos, 32)
    cd, sd = _rope_tables(pos, 16)
    c96 = np.ones((96, T), np.float32)
    s96 = np.zeros((96, T), np.float32)
    c96[64:80] = cm.T
    c96[80:96] = cm.T
    s96[64:80] = sm.T
    s96[80:96] = sm.T
    c128 = np.ones((128, T), np.float32)
    s128 = np.zeros((128, T), np.float32)
    for b in (0, 64):
        c128[b:b + 8] = cd.T
        c128[b + 8:b + 16] = cd.T
        s128[b:b + 8] = sd.T
        s128[b + 8:b + 16] = sd.T
    r96 = np.zeros((96, 96), np.float32)
    for i in range(16):
        r96[80 + i, 64 + i] = -1.0
        r96[64 + i, 80 + i] = 1.0
    r128 = np.zeros((128, 128), np.float32)
    for b in (0, 64):
        for i in range(8):
            r128[b + 8 + i, b + i] = -1.0
            r128[b + i, b + 8 + i] = 1.0
    return {
        "c96": c96, "s96": s96, "c128": c128, "s128": s128,
        "r96": r96.astype(NPBF), "r128": r128.astype(NPBF),
        "ident": np.eye(128, dtype=np.float32).astype(NPBF),
    }


def _consts_B(rank):
    n = np.arange(128)[:, None]
    m = np.arange(128)[None, :]
    m0 = np.where(m <= n, 0.0, MASKNEG)
    m1 = np.where(m >= n, 0.0, MASKNEG)
    m0f = m0.copy()
    m1l = m1.copy()
    if rank == 0:
        m0f[:64, :] = MASKNEG
    if rank == 3:
        m1l[64:, :] = MASKNEG
    masks = np.stack([m0, m0f, m1, m1l]).astype(np.float32).astype(NPBF)
    e32 = np.zeros((32, 96), np.float32)
    for i in range(32):
        e32[i, 64 + i] = 1.0
    return {"masks": masks, "ident": np.eye(128, dtype=np.float32).astype(NPBF), "e32": e32.astype(NPBF)}


def build_A():
    nc = bass.Bass("TRN2", target_bir_lowering=False)
    dt_in = lambda n, s, d=F32: nc.dram_tensor(n, s, d, kind="ExternalInput").ap()
    dt_out = lambda n, s, d=BF16: nc.dram_tensor(n, s, d, kind="ExternalOutput").ap()
    x = dt_in("x", [T, D])
    w_in = dt_in("w_in", [D, INW])
    ng = dt_in("ng", [128, 8])
    w_uq = dt_in("w_uq", [384, 768])
    qg = dt_in("qg", [128, 3])
    c96d = dt_in("c96", [96, T])
    s96d = dt_in("s96", [96, T])
    c128d = dt_in("c128", [128, T])
    s128d = dt_in("s128", [128, T])
    r96d = dt_in("r96", [96, 96], BF16)
    r128d = dt_in("r128", [128, 128], BF16)
    identd = dt_in("ident", [128, 128], BF16)
    lat_o = dt_out("lat", [288, T])
    qT_o = dt_out("qT", [8, 96, T])
    gT_o = dt_out("gT", [1024, T])
    qd_o = dt_out("qd", [3, 512, T])
    kd_o = dt_out("kd", [3, 512, T])
    vd_o = dt_out("vd", [3, T, 512])
    emit_A(nc, x, w_in, ng, w_uq, qg, c96d, s96d, c128d, s128d, r96d, r128d, identd,
           lat_o, qT_o, gT_o, qd_o, kd_o, vd_o)
    return nc


def emit_A(nc, x, w_in, ng, w_uq, qg, c96d, s96d, c128d, s128d, r96d, r128d, identd,
           lat_o, qT_o, gT_o, qd_o, kd_o, vd_o, S=None, psum=None, pay=None, tag="", last=True, after_rope=None):
    w_in_v = w_in.rearrange("(c p) n -> p c n", p=128)
    w_uq_v = w_uq.rearrange("(c p) n -> p c n", p=128)
    with ExitStack() as es:
        if S is None:
            S = Sched(nc, es)
        sb = lambda n, s, d=BF16: es.enter_context(nc.sbuf_tensor(n + tag, s, d))
        if psum is None:
            ps = lambda n, s, d=F32: es.enter_context(nc.psum_tensor(n, s, d))
        else:
            ps = lambda n, s, d=F32: psum[n]
        hT = sb("hT", [128, 8, T])
        c96 = sb("c96s", [96, T], F32)
        s96 = sb("s96s", [96, T], F32)
        c128 = sb("c128s", [128, T], F32)
        s128 = sb("s128s", [128, T], F32)
        r96 = sb("r96s", [96, 96])
        r128 = sb("r128s", [128, 128])
        ident = sb("idents", [128, 128])
        ones = sb("ones", [128, 128])
        ng8 = sb("ng8", [128, 8], F32)
        qg3 = sb("qg3", [128, 3], F32)
        xin = [sb(f"xin{i}", [128, D], F32) for i in range(2)]
        junk = sb("junk", [128, D], F32)
        hb = [sb(f"hb{i}", [128, D]) for i in range(2)]
        ssq = sb("ssq", [128, 16], F32)
        epst = sb("epst", [128, 1], F32)
        rstd = sb("rstd", [128, 16], F32)
        wf = [sb(f"wf{i}", [128, 8, 512], F32) for i in range(2)]
        wb = [sb(f"wb{i}", [128, 8, 512]) for i in range(2)]
        wkr = sb("wkr", [128, 8, 96])
        wuqf = sb("wuqf", [128, 3, 768], F32)
        wuqb = sb("wuqb", [128, 3, 768])
        cq_b = sb("cq_b", [128, 3, 512])
        sq_b = sb("sq_b", [128, 3, 512])
        cqn = sb("cqn", [128, 3, 512])
        rbc = sb("rbc", [128, 512], F32)
        qh = [sb(f"qh{i}", [96, 512]) for i in range(3)]
        t1 = [sb(f"t1_{i}", [128, 512], F32) for i in range(2)]
        t2 = [sb(f"t2_{i}", [128, 512], F32) for i in range(2)]
        pbf = [sb(f"pbf{i}", [128, 512]) for i in range(3)]
        ob = [sb(f"ob{i}", [128, 512]) for i in range(3)]
        pacc = [ps(f"pacc{i}", [128, 512]) for i in range(3)]
        prot = [ps(f"prot{i}", [128, 512]) for i in range(2)]
        pssq = ps("pssq", [128, 512])
        ptr = ps("ptr", [128, 1024], BF16)

        for i, (dst, src) in enumerate([(c96, c96d), (s96, s96d), (c128, c128d), (s128, s128d),
                                        (r96, r96d), (r128, r128d), (ident, identd), (ng8, ng), (qg3, qg)]):
            S.dma("sp", dst[:], src, writes=[dst.name], slot=("c", i))
        S.op("pool", lambda e: e.memset(ones[:], 1.0), writes=["ones"])
        S.op("pool", lambda e: e.memset(wkr[:], 0.0), writes=[wkr.name])
        S.op("pool", lambda e: e.memset(ssq[:], 0.0), writes=["ssq"])
        S.op("pool", lambda e: e.memset(epst[:], EPS), writes=["epst"])
        S.dma("sp", wuqf[:], w_uq_v, writes=["wuqf"], slot=("wuq",))
        S.op("pool", lambda e: e.tensor_tensor(out=wuqb[:], in0=wuqf[:],
                                               in1=qg3[:, :].unsqueeze(2).broadcast_to([128, 3, 768]), op=ALU.mult),
             reads=["wuqf", qg3.name], writes=["wuqb"])

        for t in range(16):
            xs, hbs = xin[t % 2], hb[t % 2]
            S.dma("sp", xs[:], x[t * 128:(t + 1) * 128, :], writes=[xs.name], slot=("xin", t % 2))
            S.op("act", lambda e, xs=xs, t=t: e.activation(out=junk[:], in_=xs[:], func=AF.Square,
                                                           accum_out=ssq[:, t:t + 1]),
                 reads=[xs.name, "ssq"], writes=["junk", ("ssq", t)])
            S.op("act", lambda e, t=t: e.activation(out=rstd[:, t:t + 1], in_=ssq[:, t:t + 1], func=AF.Sqrt,
                                                    scale=1.0 / D, bias=epst[:, 0:1]),
                 reads=[("ssq", t), "epst"], writes=[("rstd", t)])
            S.op("dve", lambda e, t=t: e.reciprocal(out=rstd[:, t:t + 1], in_=rstd[:, t:t + 1]),
                 reads=[("rstd", t)], writes=[("rstd", t)])
            S.op("dve", lambda e, xs=xs, hbs=hbs, t=t: e.tensor_scalar(out=hbs[:], in0=xs[:], scalar1=rstd[:, t:t + 1],
                                                                       scalar2=None, op0=ALU.mult),
                 reads=[xs.name, ("rstd", t)], writes=[hbs.name])

            def tr(pe, hbs=hbs):
                ins = None
                for c in range(8):
                    ins = pe.transpose(out=ptr[:, c * 128:(c + 1) * 128], in_=hbs[:, c * 128:(c + 1) * 128],
                                       identity=ident[:])
                return ins
            S.op("pe", tr, reads=[hbs.name, ident.name], writes=["ptr"])
            S.op("act", lambda e, t=t: e.activation(out=hT[:, :, t * 128:(t + 1) * 128],
                                                    in_=ptr[:, :].rearrange("p (c n) -> p c n", c=8), func=AF.Copy),
                 reads=["ptr"], writes=[("hT", t // 4)])

        state = {"wg": 0, "pa": 0, "pr": 0, "ob": 0, "pb": 0, "qh": 0}

        def load_w(col0, ncols, kr=False):
            i = state["wg"] % 2
            state["wg"] += 1
            S.dma("sp", wf[i][:, :, 0:ncols], w_in_v[:, :, col0:col0 + ncols], writes=[wf[i].name], slot=("wf", i))
            def cast(e):
                ins = None
                for c in range(8):
                    ins = e.activation(out=wb[i][:, c, 0:ncols], in_=wf[i][:, c, 0:ncols], func=AF.Copy,
                                       scale=ng8[:, c:c + 1])
                return ins
            S.op("act", cast, reads=[wf[i].name, ng8.name], writes=[wb[i].name])
            return wb[i]

        def nxt(k, n):
            v = state[k] % n
            state[k] += 1
            return v

        hTk = [("hT", i) for i in range(4)]

        def proj_fm(wt, c0, m, tb, out_ps):
            S.op("pe", mm_group([(wt[:, c, c0:c0 + m], hT[:, c, tb * 512:(tb + 1) * 512]) for c in range(8)],
                                out_ps[0:m, :]),
                 reads=[wt.name, ("hT", tb)], writes=[out_ps.name])

        def rope_store(src_bf, rows, rT, cs, sn, tb, dst_ap, p0=0, wkey=None):
            i = nxt("pr", 2)
            pr = prot[i]
            S.op("pe", mm_group([(rT[0:rows, 0:rows], src_bf[0:rows, :])], pr[0:rows, :]),
                 reads=[src_bf.name, rT.name], writes=[pr.name])
            a, b = t1[i], t2[i]
            tok = slice(tb * 512, (tb + 1) * 512)
            S.op("pool", lambda e: e.tensor_tensor(out=a[p0:rows, :], in0=src_bf[p0:rows, :], in1=cs[p0:rows, tok],
                                                   op=ALU.mult),
                 reads=[src_bf.name, cs.name], writes=[a.name])
            S.op("dve", lambda e: e.tensor_tensor(out=b[p0:rows, :], in0=pr[p0:rows, :], in1=sn[p0:rows, tok],
                                                  op=ALU.mult),
                 reads=[pr.name, sn.name], writes=[b.name])
            S.op("dve", lambda e: e.tensor_tensor(out=src_bf[p0:rows, :], in0=a[p0:rows, :], in1=b[p0:rows, :],
                                                  op=ALU.add),
                 reads=[a.name, b.name], writes=[src_bf.name])
            S.dma("sp", dst_ap, src_bf[dst_rows(p0, rows, dst_ap), :], reads=[src_bf.name],
                  writes=([wkey] if wkey is not None else []), slot=("st", src_bf.name))

        def dst_rows(p0, rows, dst_ap):
            n = dst_ap.shape[0]
            return slice(rows - n, rows)

        def latent_norm(nchunk, width, tb):
            S.op("pe", mm_group([(ones[:, :], sq_b[:, c, :]) for c in range(nchunk)], pssq[:, :]),
                 reads=["sq_b", "ones"], writes=["pssq"])
            S.op("act", lambda e: e.activation(out=rbc[:], in_=pssq[:], func=AF.Sqrt, scale=1.0 / width,
                                               bias=epst[:, 0:1]), reads=["pssq", "epst"], writes=["rbc"])
            S.op("dve", lambda e: e.reciprocal(out=rbc[:], in_=rbc[:]), reads=["rbc"], writes=["rbc"])
            for c in range(nchunk):
                S.op("dve", lambda e, c=c: e.tensor_tensor(out=cqn[:, c, :], in0=cq_b[:, c, :], in1=rbc[:], op=ALU.mult),
                     reads=["cq_b", "rbc"], writes=["cqn"])

        def latent_chunks(wt, nchunk, tb):
            for c3 in range(nchunk):
                pa = pacc[nxt("pa", 3)]
                proj_fm(wt, c3 * 128, 128, tb, pa)
                S.op("act", lambda e, pa=pa, c3=c3: e.activation(out=cq_b[:, c3, :], in_=pa[:], func=AF.Copy),
                     reads=[pa.name], writes=["cq_b"])
                S.op("act", lambda e, pa=pa, c3=c3: e.activation(out=sq_b[:, c3, :], in_=pa[:], func=AF.Square),
                     reads=[pa.name], writes=["sq_b"])

        pend = {"f": None}

        def defer(fn):
            prev = pend["f"]
            pend["f"] = fn
            if prev is not None:
                prev()

        def flush():
            prev = pend["f"]
            pend["f"] = None
            if prev is not None:
                prev()

        def unit_cq(wt):
            for tb in range(4):
                latent_chunks(wt, 3, tb)
                latent_norm(3, 384, tb)
                for h in range(8):
                    pa = pacc[nxt("pa", 3)]
                    q = qh[nxt("qh", 3)]
                    S.op("pe", mm_group([(wuqb[:, c, h * 96:(h + 1) * 96], cqn[:, c, :]) for c in range(3)], pa[0:96, :]),
                         reads=["wuqb", "cqn"], writes=[pa.name])
                    S.op("act", lambda e, pa=pa, q=q: e.activation(out=q[:], in_=pa[0:96, :], func=AF.Copy),
                         reads=[pa.name], writes=[q.name])
                    defer(lambda q=q, tb=tb, h=h: rope_store(q, 96, r96, c96, s96, tb,
                                                             qT_o[h, :, tb * 512:(tb + 1) * 512], p0=64))
                flush()

        def unit_ckv(wt):
            S.op("pool", lambda e: e.tensor_copy(out=wkr[:, :, 64:96], in_=wt[:, :, 256:288]), reads=[wt.name],
                 writes=[wkr.name])
            lat_v = lat_o[0:256, :].rearrange("(c p) t -> p c t", p=128)
            for tb in range(4):
                latent_chunks(wt, 2, tb)
                latent_norm(2, 256, tb)
                S.dma("sp", lat_v[:, :, tb * 512:(tb + 1) * 512], cqn[:, 0:2, :], reads=["cqn"],
                      writes=[("lat_loc", "c", tb)], slot=("st", "cqn"))
                pa = pacc[nxt("pa", 3)]
                q = qh[nxt("qh", 3)]
                proj_fm(wkr, 0, 96, tb, pa)
                S.op("act", lambda e, pa=pa, q=q: e.activation(out=q[:], in_=pa[0:96, :], func=AF.Copy),
                     reads=[pa.name], writes=[q.name])
                defer(lambda q=q, tb=tb: rope_store(q, 96, r96, c96, s96, tb,
                                                    lat_o[256:288, tb * 512:(tb + 1) * 512], p0=64,
                                                    wkey=("lat_loc", "r", tb)))
            flush()

        def gate_blocks(wt, row0):
            for blk in range(4):
                for tb in range(4):
                    pa = pacc[nxt("pa", 3)]
                    o = ob[nxt("ob", 3)]
                    proj_fm(wt, blk * 128, 128, tb, pa)
                    S.op("act", lambda e, pa=pa, o=o: e.activation(out=o[:], in_=pa[:], func=AF.Silu),
                         reads=[pa.name], writes=[o.name])
                    S.dma("sp", gT_o[row0 + blk * 128: row0 + (blk + 1) * 128, tb * 512:(tb + 1) * 512], o[:],
                          reads=[o.name], slot=("st", o.name))

        def rope_post(p, blk, tb, dst, g):
            rope_store(p, 128, r128, c128, s128, tb, dst[blk * 128:(blk + 1) * 128, tb * 512:(tb + 1) * 512])
            if pay is not None and g is not None:
                halo = HALO[g]
                if tb * 512 < halo:
                    n = min(halo - tb * 512, 512)
                    S.dma("sp", pay_k(pay, 0, g, 0, blk * 128, 128, tb * 512, n), p[:, 0:n], reads=[p.name],
                          slot=("st", p.name))
                lo, hi = max(tb * 512, T - halo), tb * 512 + 512
                if lo < hi:
                    S.dma("sp", pay_k(pay, 0, g, 1, blk * 128, 128, lo - (T - halo), hi - lo),
                          p[:, lo - tb * 512:512], reads=[p.name], slot=("st", p.name))

        def rope_blocks(wt, dst, g=None):
            for blk in range(4):
                for tb in range(4):
                    pa = pacc[nxt("pa", 3)]
                    p = pbf[nxt("pb", 3)]
                    proj_fm(wt, blk * 128, 128, tb, pa)
                    S.op("act", lambda e, pa=pa, p=p: e.activation(out=p[:], in_=pa[:], func=AF.Copy),
                         reads=[pa.name], writes=[p.name])
                    defer(lambda p=p, blk=blk, tb=tb: rope_post(p, blk, tb, dst, g))
            flush()

        def v_blocks(wt, dst, g=None):
            for t in range(16):
                pa = pacc[nxt("pa", 3)]
                o = ob[nxt("ob", 3)]
                S.op("pe", mm_group([(hT[:, c, t * 128:(t + 1) * 128], wt[:, c, 0:512]) for c in range(8)], pa[:, :]),
                     reads=[wt.name, ("hT", t // 4)], writes=[pa.name])
                S.op("act", lambda e, pa=pa, o=o: e.activation(out=o[:], in_=pa[:], func=AF.Copy),
                     reads=[pa.name], writes=[o.name])
                S.dma("sp", dst[t * 128:(t + 1) * 128, :], o[:], reads=[o.name], slot=("st", o.name))
                if pay is not None and g is not None:
                    halo = HALO[g]
                    if t * 128 < halo:
                        n = min(halo - t * 128, 128)
                        S.dma("sp", pay_v(pay, 0, g, 0, t * 128, n), o[0:n, :], reads=[o.name], slot=("st", o.name))
                    lo, hi = max(t * 128, T - halo), t * 128 + 128
                    if lo < hi:
                        S.dma("sp", pay_v(pay, 0, g, 1, lo - (T - halo), hi - lo), o[lo - t * 128:128, :],
                              reads=[o.name], slot=("st", o.name))

        units = [(0, 384, unit_cq), (384, 288, unit_ckv)]
        for g in range(3):
            base = 1184 + g * 1536
            units.append((base, 512, lambda wt, g=g: rope_blocks(wt, qd_o[g])))
            units.append((base + 512, 512, lambda wt, g=g: rope_blocks(wt, kd_o[g], g)))
        n_rope_units = len(units)
        units.append((672, 512, lambda wt: gate_blocks(wt, 0)))
        for g in range(3):
            base = 1184 + g * 1536
            units.append((base + 1024, 512, lambda wt, g=g: v_blocks(wt, vd_o[g], g)))
        units.append((5792, 512, lambda wt: gate_blocks(wt, 512)))
        wt = load_w(units[0][0], units[0][1])
        for ui, (c0, ncols, fn) in enumerate(units):
            wnext = load_w(units[ui + 1][0], units[ui + 1][1]) if ui + 1 < len(units) else None
            fn(wt)
            wt = wnext
            if ui == n_rope_units - 1 and after_rope is not None:
                after_rope()
        if last:
            S.wait_all("sp")
        else:
            S.barrier()
        S.emit()


_CACHE = {}


def _get(name, builder):
    if name not in _CACHE:
        _CACHE[name] = builder()
    return _CACHE[name]


def run_A(x_shards, w_in_l, ng_l, w_uq_l, qg_l):
    nc = _get("A", build_A)
    in_maps = []
    for c in range(NCORE):
        m = {"x": x_shards[c], "w_in": w_in_l, "ng": ng_l, "w_uq": w_uq_l, "qg": qg_l}
        m.update(_get(("cA", c % 4), lambda: _consts_A(c % 4)))
        in_maps.append(m)
    res = run_bass_kernel_spmd(nc, in_maps, core_ids=list(range(NCORE)))
    return res.results


def build_B(mla_heads=tuple(range(8)), dil_hps=tuple(range(4)), debug=False, dil_groups=(0, 1, 2)):
    nc = bass.Bass("TRN2", target_bir_lowering=False)
    dt_in = lambda n, s, d=F32: nc.dram_tensor(n, s, d, kind="ExternalInput").ap()
    dt_out = lambda n, s, d=F32: nc.dram_tensor(n, s, d, kind="ExternalOutput").ap()
    a = dict(
        x=dt_in("x", [T, D]),
        latT=dt_in("latT", [288, S_FULL], BF16),
        qT=dt_in("qT", [8, 96, T], BF16),
        gT=dt_in("gT", [1024, T], BF16),
        qd=dt_in("qd", [3, 512, T], BF16),
        kd=[dt_in(f"kd{g}", [512, T + 2 * HALO[g]], BF16) for g in range(3)],
        vd=[dt_in(f"vd{g}", [T + 2 * HALO[g], 512], BF16) for g in range(3)],
        w_ukv=dt_in("w_ukv", [256, 1024]),
        kvg=dt_in("kvg", [128, 2]),
        w_out=dt_in("w_out", [D, D]),
        fg=dt_in("fg", [1, D]),
        masks=dt_in("masks", [4, 128, 128], BF16),
        ident=dt_in("ident", [128, 128], BF16),
        e32=dt_in("e32", [32, 96], BF16),
        xo=dt_out("xo", [T, D]),
        yo=dt_out("yo", [T, D]),
    )
    if debug:
        a["mixo"] = nc.dram_tensor("mixo", [1024, T], BF16, kind="ExternalOutput").ap()
    emit_B(nc, mla_heads=mla_heads, dil_hps=dil_hps, dil_groups=dil_groups, **a)
    return nc


def emit_B(nc, x, latT, qT, gT, qd, kd, vd, w_ukv, kvg, w_out, fg, masks, ident, e32, xo, yo,
           mla_heads=tuple(range(8)), dil_hps=tuple(range(4)), mixo=None, dil_groups=(0, 1, 2),
           S=None, psum=None, tag="", fz=None, last=True, final=True):
    with ExitStack() as es:
        if S is None:
            S = Sched(nc, es)
        sb = lambda n, s, d=BF16: es.enter_context(nc.sbuf_tensor(n + tag, s, d))
        if psum is None:
            pb = [es.enter_context(nc.psum_tensor(f"pb{i}", [128, 512], F32)) for i in range(8)]
        else:
            pb = psum
        pS, pAcc, pK = pb[0:3], pb[3:5], pb[5:7]
        if fz is not None:
            selK = sb("selK", [128, 8, 128])
            selVL = sb("selVL", [64, 4, 64])
            selVR = sb("selVR", [64, 4, 128])
            S.dma("sp", selK[:], fz["selK"], writes=["selK"], slot=("c", 4))
            S.dma("sp", selVL[:], fz["selVL"], writes=["selVL"], slot=("c", 5))
            S.dma("sp", selVR[:], fz["selVR"], writes=["selVR"], slot=("c", 6))
        mixedT = sb("mixedT", [128, 8, T])
        identS = sb("identS", [128, 128])
        e32S = sb("e32S", [32, 96])
        maskS = sb("maskS", [128, 4, 128])
        onesf = sb("onesf", [128, 64], F32)
        kvg2 = sb("kvg2", [128, 2], F32)
        epst = sb("epst", [128, 1], F32)
        rd = sb("rd", [128, 512], F32)
        tmp = [sb(f"tmp{i}", [128, 512], F32) for i in range(2)]
        gts = [sb(f"gts{i}", [128, 512]) for i in range(2)]
        S.dma("sp", identS[:], ident, writes=["identS"], slot=("c", 0))
        S.dma("sp", e32S[:], e32, writes=["e32S"], slot=("c", 1))
        S.dma("sp", maskS[:], masks.rearrange("k n m -> n k m"), writes=["maskS"], slot=("c", 2))
        S.dma("sp", kvg2[:], kvg, writes=["kvg2"], slot=("c", 3))
        S.op("pool", lambda e: e.memset(onesf[:], 1.0), writes=["onesf"])
        onesb = sb("onesb", [128, 2])
        S.op("pool", lambda e: e.memset(onesb[:], 1.0), writes=["onesb"])
        S.op("pool", lambda e: e.memset(epst[:], EPS), writes=["epst"])
        m01 = sb("m01", [128, 4, 128])
        m01c = sb("m01c", [128, 4, 2, 128])
        S.op("dve", lambda e: e.tensor_scalar(out=m01[:], in0=maskS[:], scalar1=1.0 / 32768.0, scalar2=1.0,
                                              op0=ALU.mult, op1=ALU.add), reads=["maskS"], writes=["m01"])
        for ci in range(4):
            h0 = 1 if (ci & 1) else 0
            h1 = 3 if (ci & 2) else 2
            S.op("pool", lambda e, ci=ci, h0=h0: e.tensor_copy(out=m01c[:, ci, 0, :], in_=m01[:, h0, :]),
                 reads=["m01"], writes=["m01c"])
            S.op("pool", lambda e, ci=ci, h1=h1: e.tensor_copy(out=m01c[:, ci, 1, :], in_=m01[:, h1, :]),
                 reads=["m01"], writes=["m01c"])
        st = {"tmp": 0, "gts": 0, "acc": 0, "pk": 0}

        def nxt(k, n):
            v = st[k] % n
            st[k] += 1
            return v

        def get_pk():
            return pK[0] if st.get("pk_single") else pK[nxt("pk", 2)]

        def normalize_gate(src_o, src_den, gate_ap, gate_reads, po, chunk, qb, src_reads):
            tok = slice(qb * 512, (qb + 1) * 512)
            pk = get_pk()
            tm = tmp[nxt("tmp", 2)]
            S.op("dve", lambda e: e.reciprocal(out=rd[64:65, :], in_=src_den), reads=src_reads, writes=["rd"])
            S.op("pe", mm_group([(onesf[64:65, 0:64], rd[64:65, :])], pk[0:64, :]), reads=["rd", "onesf"],
                 writes=[pk.name])
            S.op("dve", lambda e: e.tensor_tensor(out=tm[po:po + 64, :], in0=src_o, in1=pk[0:64, :], op=ALU.mult),
                 reads=src_reads + [pk.name], writes=[tm.name])
            S.op("dve", lambda e: e.tensor_tensor(out=mixedT[po:po + 64, chunk, tok], in0=tm[po:po + 64, :],
                                                  in1=gate_ap, op=ALU.mult),
                 reads=[tm.name] + gate_reads, writes=[("mixedT", chunk)])

        with ExitStack() as es1:
            sb1 = lambda n, s, d=BF16: es1.enter_context(nc.sbuf_tensor(n + tag, s, d))
            lat = sb1("lat", [128, 3, S_FULL])
            KT = sb1("KT", [96, S_FULL])
            V = sb1("V", [128, 64, 65])
            Qh = [sb1(f"Qh{i}", [96, T]) for i in range(2)]
            PT = [sb1(f"PT{i}", [128, 512]) for i in range(3)]
            osb = sb1("osb", [64, 512], F32)
            wukvf = sb1("wukvf", [128, 2, 1024], F32)
            wk96 = sb1("wk96", [128, 2, 8, 96])
            wv = sb1("wv", [128, 2, 8, 64])
            S.dma("sp", wukvf[:], w_ukv.rearrange("(c p) n -> p c n", p=128), writes=["wukvf"], slot=("wukv",))
            S.op("pool", lambda e: e.memset(wk96[:], 0.0), writes=["wk96"])
            S.op("pool", lambda e: e.memset(V[:], 1.0), writes=["V"])
            for c in range(2):
                wsrc = wukvf[:, c, :].rearrange("p (h k) -> p h k", h=8)
                S.op("pool", lambda e, c=c, wsrc=wsrc: e.tensor_tensor(
                    out=wk96[:, c, :, 0:64], in0=wsrc[:, :, 0:64],
                    in1=kvg2[:, c:c + 1].unsqueeze(2).broadcast_to([128, 8, 64]), op=ALU.mult),
                    reads=["wukvf", "kvg2"], writes=["wk96"])
                S.op("pool", lambda e, c=c, wsrc=wsrc: e.tensor_tensor(
                    out=wv[:, c, :, :], in0=wsrc[:, :, 64:128],
                    in1=kvg2[:, c:c + 1].unsqueeze(2).broadcast_to([128, 8, 64]), op=ALU.mult),
                    reads=["wukvf", "kvg2"], writes=["wv"])
            if fz is not None and fz.get("after_prologue") is not None:
                fz["after_prologue"]()
            for i in range(4):
                sl = slice(i * 2048, (i + 1) * 2048)
                if fz is None:
                    lat_v = latT[0:256, :].rearrange("(c p) t -> p c t", p=128)
                    S.dma("sp", lat[:, 0:2, sl], lat_v[:, :, sl], writes=[("lat", i)], slot=("lat", i))
                    S.dma("sp", lat[0:32, 2, sl], latT[256:288, sl], writes=[("latr", i)], slot=("latr", i))
                else:
                    S.dma("sp", lat[:, 0:2, sl],
                          fz["lat_allA"][i * 256:(i + 1) * 256, :].rearrange("(c p) t -> p c t", p=128),
                          reads=["lat_allA"], writes=[("lat", i)], slot=("lat", i))
                    S.dma("sp", lat[0:32, 2, sl], fz["lat_allB"][i * 32:(i + 1) * 32, :], reads=["lat_allB"],
                          writes=[("latr", i)], slot=("latr", i))

            for h in mla_heads:
                po, chunk = (h % 2) * 64, h // 2
                Q = Qh[h % 2]
                S.dma("sp", Q[:], qT[h], writes=[Q.name], slot=("Q", h % 2))
                for tb in range(16):
                    pk = pK[nxt("pk", 2)]
                    sl = slice(tb * 512, (tb + 1) * 512)
                    S.op("pe", mm_group([(wk96[:, 0, h, :], lat[:, 0, sl]), (wk96[:, 1, h, :], lat[:, 1, sl]),
                                         (e32S[0:32, :], lat[0:32, 2, sl])], pk[0:96, :]),
                         reads=["wk96", "e32S", ("lat", tb // 4), ("latr", tb // 4)], writes=[pk.name])
                    eng = "dve" if tb % 2 else "act"
                    if eng == "act":
                        S.op("act", lambda e, pk=pk, sl=sl: e.activation(out=KT[:, sl], in_=pk[0:96, :], func=AF.Copy),
                             reads=[pk.name], writes=[("KT", tb)])
                    else:
                        S.op("dve", lambda e, pk=pk, sl=sl: e.tensor_copy(out=KT[:, sl], in_=pk[0:96, :]),
                             reads=[pk.name], writes=[("KT", tb)])
                for k8 in range(8):
                    pk = pK[nxt("pk", 2)]

                    def vb(pe, pk=pk, k8=k8, h=h):
                        ins = None
                        for j in range(8):
                            kt = k8 * 8 + j
                            for c in range(2):
                                ins = pe.matmul(pk[:, j * 64:(j + 1) * 64], lhsT=lat[:, c, kt * 128:(kt + 1) * 128],
                                                rhs=wv[:, c, h, :], start=(c == 0), stop=(c == 1))
                        return ins
                    S.op("pe", vb, reads=["wv", ("lat", k8 // 2)], writes=[pk.name])
                    S.op("dve", lambda e, pk=pk, k8=k8: e.tensor_copy(
                        out=V[:, k8 * 8:(k8 + 1) * 8, 0:64], in_=pk[:, :].rearrange("p (j d) -> p j d", j=8)),
                        reads=[pk.name], writes=[("V", k8)])
                steps = [(qb, kt) for qb in range(4) for kt in range(64)]
                accs, gidx = {}, {}
                for qb in range(4):
                    accs[qb] = pAcc[nxt("acc", 2)]
                base = st.get("sstep", 0)

                def qk(n):
                    qb, kt = steps[n]
                    s_ = pS[(base + n) % 3]
                    S.op("pe", mm_group([(KT[:, kt * 128:(kt + 1) * 128], Q[:, qb * 512:(qb + 1) * 512])], s_[:, :]),
                         reads=[("KT", kt // 4), Q.name], writes=[s_.name])
                qk(0)
                qk(1)
                for n, (qb, kt) in enumerate(steps):
                    s_, p_ = pS[(base + n) % 3], PT[(base + n) % 3]
                    acc = accs[qb]
                    if kt == 0:
                        g_i = nxt("gts", 2)
                        gidx[qb] = g_i
                        S.dma("sp", gts[g_i][po:po + 64, :], gT[h * 64:(h + 1) * 64, qb * 512:(qb + 1) * 512],
                              writes=[gts[g_i].name], slot=("gts", g_i))
                    S.op("act", lambda e, s_=s_, p_=p_: e.activation(out=p_[:], in_=s_[:], func=AF.Exp,
                                                                     scale=MLA_SCALE),
                         reads=[s_.name], writes=[p_.name])
                    if n + 2 < len(steps):
                        qk(n + 2)
                    S.op("pe", lambda pe, kt=kt, p_=p_, acc=acc: pe.matmul(
                        acc[0:65, :], lhsT=V[:, kt, :], rhs=p_[:], start=(kt == 0), stop=(kt == 63)),
                        reads=[p_.name, ("V", kt // 8)], writes=[acc.name])
                    if kt == 63:
                        g_i = gidx[qb]
                        S.op("act", lambda e, acc=acc: e.activation(out=osb[:], in_=acc[0:64, :], func=AF.Copy),
                             reads=[acc.name], writes=["osb"])
                        normalize_gate(osb[0:64, :], acc[64:65, :], gts[g_i][po:po + 64, :], [gts[g_i].name], po, chunk,
                                       qb, ["osb", acc.name])
                st["sstep"] = base + len(steps)
            if fz is not None and fz.get("lato") is not None:
                S.dma("sp", fz["lato"], lat[:], reads=[("lat", i) for i in range(4)] + [("latr", i) for i in range(4)],
                      slot=("lato",))
            S.barrier()
            S.emit()

        with ExitStack() as es2:
            sb2 = lambda n, s, d=BF16: es2.enter_context(nc.sbuf_tensor(n + tag, s, d))
            Kg = [sb2(f"Kg{i}", [128, T + 2 * HALO[2]]) for i in range(2)]
            Qg = [sb2(f"Qg{i}", [128, T]) for i in range(2)]
            Vraw = [sb2(f"Vraw{i}", [128, 32, 128]) for i in range(2)]
            accT = sb2("accT", [65, 2, T], F32)
            gtb = sb2("gtb", [128, T])
            P2p = [sb2(f"P2p{i}", [128, 4, 128]) for i in range(4)]
            Sbanks = [pS[0], pS[1], pS[2], pK[1]]
            st["pk_single"] = True
            if fz is not None:
                candK = [sb2(f"candK{i}", [128, 4, HALO[2]]) for i in range(2)]
                candV = [sb2(f"candV{i}", [64, 4, 16, 128]) for i in range(2)]
            jobs = [(hp, g) for hp in dil_hps for g in dil_groups]

            jinfo = {}

            def stage(n):
                hp, g = jobs[n]
                d, halo = DILS[g], HALO[g]
                W = T + 2 * halo
                nbr = 16 // d + 1
                njb = 16 // d
                bi = n % 2
                kg, qg_, vr = Kg[bi], Qg[bi], Vraw[bi]
                kkeys, vkeys = [(kg.name, "own")], []
                S.dma("sp", qg_[:], qd[g, hp * 128:(hp + 1) * 128, :], writes=[qg_.name], slot=("qg", bi))
                cols = slice(hp * 128, (hp + 1) * 128)
                if fz is None:
                    S.dma("sp", kg[:, 0:W], kd[g][hp * 128:(hp + 1) * 128, :], writes=[(kg.name, "own")], slot=("kg", bi))
                    vsrc = vd[g].rearrange("(kb n r) c -> n r kb c", n=128, r=d)
                    for r in range(d):
                        vkeys.append((vr.name, "o", r, 0))
                        S.dma("sp", vr[:, r * nbr:(r + 1) * nbr, :], vsrc[:, r, :, cols],
                              writes=[vkeys[-1]], slot=("vr", bi))
                else:
                    pall = fz["pay_all"]
                    pkeys = fz["pay_keys"](g)
                    S.dma("sp", kg[:, halo:halo + T], kd[g][hp * 128:(hp + 1) * 128, :], writes=[(kg.name, "own")],
                          slot=("kg", bi))
                    vown = vd[g].rearrange("(kb two n r) c -> two n r kb c", two=2, n=64, r=d)
                    for r in range(d):
                        vkeys.append((vr.name, "o", r, 0))
                        S.dma("sp", vr[64:128, r * nbr:r * nbr + njb, :], vown[0][:, r, :, cols],
                              writes=[vkeys[-1]], slot=("vr", bi))
                        vkeys.append((vr.name, "o", r, 1))
                        S.dma("sp", vr[0:64, r * nbr + 1:r * nbr + 1 + njb, :], vown[1][:, r, :, cols],
                              writes=[vkeys[-1]], slot=("vr", bi))
                    for side in range(2):
                        piece = 1 - side
                        ck = candK[side]
                        S.dma("sp", ck[:, :, 0:halo], pay_k(pall, 0, g, piece, hp * 128, 128, 0, halo, ncand=4),
                              reads=pkeys, writes=[ck.name], slot=("candK", side))
                        cv = candV[side]
                        cvkeys = [(cv.name, j) for j in range(4)]
                        for j in range(4):
                            S.dma("sp", cv[:, j, 0:d, :], pay_v_cand(pall, j, g, piece, hp, d),
                                  reads=pkeys, writes=[cvkeys[j]], slot=("candV", side))
                        dst0 = 0 if side == 0 else halo + T
                        for c0 in range(0, halo, 512):
                            nn = min(512, halo - c0)
                            pk = get_pk()
                            S.op("pe", mm_group([(selK[:, side * 4 + j, :], ck[:, j, c0:c0 + nn]) for j in range(4)],
                                                pk[:, 0:nn]), reads=["selK", ck.name], writes=[pk.name])
                            kkeys.append((kg.name, "h", side, c0))
                            S.op("act", lambda e, pk=pk, nn=nn, c0=c0, dst0=dst0, kg=kg: e.activation(
                                out=kg[:, dst0 + c0:dst0 + c0 + nn], in_=pk[:, 0:nn], func=AF.Copy),
                                reads=[pk.name], writes=[kkeys[-1]])
                        for r0 in range(0, d, 4):
                            nr = min(4, d - r0)
                            pk = get_pk()
                            if side == 0:
                                lhs = [selVL[:, j, :] for j in range(4)]
                                prow, kbs, mrows = slice(0, 64), 0, 64
                            else:
                                lhs = [selVR[:, j, :] for j in range(4)]
                                prow, kbs, mrows = slice(64, 128), nbr - 1, 128
                            S.op("pe", mm_group([(lhs[j], cv[:, j, r0:r0 + nr, :]) for j in range(4)],
                                                pk[0:mrows, 0:nr * 128]), reads=["selVL", "selVR"] + cvkeys,
                                 writes=[pk.name])
                            vkeys.append((vr.name, "h", side, r0))
                            S.op("dve", lambda e, pk=pk, nr=nr, r0=r0, prow=prow, kbs=kbs, vr=vr, nbr=nbr: e.tensor_copy(
                                out=vr[prow, r0 * nbr + kbs:(r0 + nr - 1) * nbr + kbs + 1:nbr, :],
                                in_=pk[prow, 0:nr * 128].rearrange("p (r c) -> p r c", r=nr)),
                                reads=[pk.name], writes=[vkeys[-1]])
                jinfo[n] = (kkeys, vkeys)

            def compute(n):
                hp, g = jobs[n]
                d = DILS[g]
                nbr = 16 // d + 1
                njb = 16 // d
                bi = n % 2
                kg, qg_, vr = Kg[bi], Qg[bi], Vraw[bi]
                kkeys, vkeys = jinfo[n]
                first = (g == dil_groups[0])
                pbase = st.get("pbase", 0)
                accv = [accT[:, hl, :].rearrange("p (jb m r) -> p r jb m", r=d, m=128) for hl in range(2)]
                tiles = []
                for hl in range(2):
                    if d == 1:
                        quads = [[(0, 4 * q + j) for j in range(4)] for q in range(4)]
                        dsts = [accv[hl][:, 0, 4 * q:4 * q + 4, :] for q in range(4)]
                    elif d == 4:
                        quads = [[(r, j) for j in range(4)] for r in range(4)]
                        dsts = [accv[hl][:, r, 0:4, :] for r in range(4)]
                    else:
                        quads = [[(4 * q + j, 0) for j in range(4)] for q in range(4)]
                        dsts = [accv[hl][:, 4 * q:4 * q + 4, 0, :] for q in range(4)]
                    for quad, dst in zip(quads, dsts):
                        for j, (r, jb) in enumerate(quad):
                            tiles.append((hl, r, jb, j, dst))

                pairs = [(tiles[2 * k], tiles[2 * k + 1]) for k in range(len(tiles) // 2)]
                LA = 3

                def qkpair(p):
                    bank = Sbanks[(pbase + p) % 4]
                    mms = []
                    for t, (hl, r, jb, j, dst) in enumerate(pairs[p]):
                        po = hl * 64
                        for half in range(2):
                            kb = jb + half
                            k0 = r + d * kb * 128
                            q0 = r + d * jb * 128
                            mms.append((bank[:, (2 * t + half) * 128:(2 * t + half + 1) * 128],
                                        kg[po:po + 64, k0:k0 + d * 127 + 1:d], qg_[po:po + 64, q0:q0 + d * 127 + 1:d]))

                    def fn(pe):
                        ins = None
                        for o_, kap, qap in mms:
                            ins = pe.matmul(o_, lhsT=kap, rhs=qap, start=True, stop=True)
                        return ins
                    S.op("pe", fn, reads=kkeys + [qg_.name], writes=[bank.name])
                for p in range(min(LA, len(pairs))):
                    qkpair(p)
                for p, pr_ in enumerate(pairs):
                    bank, pp = Sbanks[(pbase + p) % 4], P2p[(pbase + p) % 4]
                    S.op("act", lambda e, bank=bank, pp=pp: e.activation(
                        out=pp[:, :, :], in_=bank[:, :].rearrange("p (a m) -> p a m", a=4), func=AF.Exp,
                        scale=DIL_SCALE), reads=[bank.name], writes=[(pp.name, 0), (pp.name, 1)])
                    for t, (hl, r, jb, j, dst) in enumerate(pr_):
                        ci = (1 if jb == 0 else 0) + (2 if jb == njb - 1 else 0)
                        S.op("dve", lambda e, pp=pp, ci=ci, t=t: e.tensor_tensor(
                            out=pp[:, 2 * t:2 * t + 2, :], in0=pp[:, 2 * t:2 * t + 2, :], in1=m01c[:, ci, :, :],
                            op=ALU.mult), reads=[(pp.name, t), "m01c"], writes=[(pp.name, t)])
                    if p + LA < len(pairs):
                        qkpair(p + LA)
                    hl0, j1 = pr_[0][0], pr_[1][3]
                    pv = pAcc[((2 * p) // 4) % 2]

                    def pvf(pe, pr_=pr_, pp=pp, pv=pv, vr=vr, nbr=nbr):
                        ins = None
                        for t, (hl, r, jb, j, dst) in enumerate(pr_):
                            for half in range(2):
                                pe.matmul(pv[0:64, j * 128:(j + 1) * 128],
                                          lhsT=vr[:, r * nbr + jb + half, hl * 64:(hl + 1) * 64],
                                          rhs=pp[:, 2 * t + half, :], start=(half == 0), stop=(half == 1))
                            for half in range(2):
                                ins = pe.matmul(pv[64:65, j * 128:(j + 1) * 128], lhsT=onesb[:, 0:1],
                                                rhs=pp[:, 2 * t + half, :], start=(half == 0), stop=(half == 1))
                        return ins
                    S.op("pe", pvf, reads=[(pp.name, 0), (pp.name, 1), "onesb"] + vkeys, writes=[pv.name])
                    if j1 == 3:
                        dst = pr_[1][4]
                        src = pv[0:65, :].rearrange("p (a m) -> p a m", a=4)
                        if first:
                            S.op("dve", lambda e, dst=dst, src=src: e.tensor_copy(out=dst, in_=src),
                                 reads=[pv.name], writes=[("accT", hl0)])
                        else:
                            S.op("dve", lambda e, dst=dst, src=src: e.tensor_tensor(out=dst, in0=src, in1=dst,
                                                                                    op=ALU.add),
                                 reads=[pv.name, ("accT", hl0)], writes=[("accT", hl0)])
                st["pbase"] = pbase + len(pairs)

            if jobs:
                stage(0)
            for n, (hp, g) in enumerate(jobs):
                if g == dil_groups[0]:
                    S.dma("sp", gtb[:], gT[512 + hp * 128: 512 + (hp + 1) * 128, :], writes=["gtb"], slot=("gtb",))
                if n + 1 < len(jobs):
                    stage(n + 1)
                compute(n)
                if g == dil_groups[-1]:
                    for hl in range(2):
                        po = hl * 64
                        for qb in range(4):
                            tok = slice(qb * 512, (qb + 1) * 512)
                            normalize_gate(accT[0:64, hl, tok], accT[64:65, hl, tok], gtb[po:po + 64, tok], ["gtb"], po,
                                           4 + hp, qb, [("accT", hl)])
            S.barrier()
            S.emit()

        with ExitStack() as es3:
            sb3 = lambda n, s, d=BF16: es3.enter_context(nc.sbuf_tensor(n + tag, s, d))
            wof = sb3("wof", [128, 8, 512], F32)
            wob = sb3("wob", [128, 8, D])
            fgb = sb3("fgb", [128, D], F32)
            xin = [sb3(f"xin{i}", [128, D], F32) for i in range(2)]
            xn = [sb3(f"xn{i}", [128, D], F32) for i in range(2)]
            yv = [sb3(f"yv{i}", [128, D], F32) for i in range(2)]
            junk = sb3("junk", [128, D], F32)
            ssq = sb3("ssq", [128, 16], F32)
            rstd = sb3("rstd", [128, 16], F32)
            S.op("pool", lambda e: e.memset(ssq[:], 0.0), writes=["ssq"])
            S.dma("sp", fgb[:], fg.partition_broadcast(128), writes=["fgb"], slot=("fgb",))
            w_out_v = w_out.rearrange("(c p) n -> p c n", p=128)
            if mixo is not None:
                S.dma("pool", mixo.rearrange("(c p) t -> p c t", p=128), mixedT[:], reads=[("mixedT", c) for c in range(8)],
                      slot=("mixo",))
            for nb in range(2):
                S.dma("sp", wof[:], w_out_v[:, :, nb * 512:(nb + 1) * 512], writes=["wof"], slot=("wof",))
                S.op("pool", lambda e, nb=nb: e.tensor_copy(out=wob[:, :, nb * 512:(nb + 1) * 512], in_=wof[:]),
                     reads=["wof"], writes=["wob"])
            for t in range(16):
                xs, xnn, yy = xin[t % 2], xn[t % 2], yv[t % 2]
                S.dma("sp", xs[:], x[t * 128:(t + 1) * 128, :], writes=[xs.name], slot=("xin", t % 2))
                for nb in range(2):
                    pa = pS[(2 * t + nb) % 3]
                    S.op("pe", mm_group([(mixedT[:, c, t * 128:(t + 1) * 128], wob[:, c, nb * 512:(nb + 1) * 512])
                                         for c in range(8)], pa[:, :]),
                         reads=["wob"] + [("mixedT", c) for c in range(8)], writes=[pa.name])
                    S.op("dve", lambda e, pa=pa, nb=nb, xs=xs, xnn=xnn: e.tensor_tensor(
                        out=xnn[:, nb * 512:(nb + 1) * 512], in0=pa[:, :], in1=xs[:, nb * 512:(nb + 1) * 512], op=ALU.add),
                        reads=[pa.name, xs.name], writes=[xnn.name])
                S.dma("pool", xo[t * 128:(t + 1) * 128, :], xnn[:], reads=[xnn.name], slot=("xo", t % 2))
                if not final:
                    continue
                S.op("act", lambda e, xnn=xnn, t=t: e.activation(out=junk[:], in_=xnn[:], func=AF.Square,
                                                                 accum_out=ssq[:, t:t + 1]),
                     reads=[xnn.name, "ssq"], writes=["junk", ("ssq", t)])
                S.op("act", lambda e, t=t: e.activation(out=rstd[:, t:t + 1], in_=ssq[:, t:t + 1], func=AF.Sqrt,
                                                        scale=1.0 / D, bias=epst[:, 0:1]),
                     reads=[("ssq", t), "epst"], writes=[("rstd", t)])
                S.op("dve", lambda e, t=t: e.reciprocal(out=rstd[:, t:t + 1], in_=rstd[:, t:t + 1]),
                     reads=[("rstd", t)], writes=[("rstd", t)])
                S.op("dve", lambda e, xnn=xnn, yy=yy, t=t: e.scalar_tensor_tensor(
                    out=yy[:], in0=xnn[:], scalar=rstd[:, t:t + 1], in1=fgb[:], op0=ALU.mult, op1=ALU.mult),
                    reads=[xnn.name, ("rstd", t), "fgb"], writes=[yy.name])
                S.dma("pool", yo[t * 128:(t + 1) * 128, :], yy[:], reads=[yy.name], slot=("yo", t % 2))
            if last:
                S.wait_all("sp")
            else:
                S.barrier()
            S.emit()


def run_B(in_maps):
    nc = _get("B", build_B)
    res = run_bass_kernel_spmd(nc, in_maps, core_ids=list(range(NCORE)))
    return res.results


def _exchange(resA, x_shards, w_ukv_l, kvg_l, w_out_l, fg):
    in_maps = []
    for b in range(2):
        cores = [4 * b + r for r in range(4)]
        lat_full = np.concatenate([np.asarray(resA[c]["lat"]) for c in cores], axis=1)
        kd_full = np.concatenate([np.asarray(resA[c]["kd"]) for c in cores], axis=2)
        vd_full = np.concatenate([np.asarray(resA[c]["vd"]) for c in cores], axis=1)
        for r in range(4):
            c = cores[r]
            m = {"x": x_shards[c], "latT": lat_full, "qT": np.asarray(resA[c]["qT"]), "gT": np.asarray(resA[c]["gT"]),
                 "qd": np.asarray(resA[c]["qd"]), "w_ukv": w_ukv_l, "kvg": kvg_l, "w_out": w_out_l, "fg": fg}
            for g in range(3):
                hl = HALO[g]
                kp = np.zeros((512, S_FULL + 2 * hl), dtype=kd_full.dtype)
                kp[:, hl:hl + S_FULL] = kd_full[g]
                vp = np.zeros((S_FULL + 2 * hl, 512), dtype=vd_full.dtype)
                vp[hl:hl + S_FULL] = vd_full[g]
                m[f"kd{g}"] = np.ascontiguousarray(kp[:, r * T: r * T + T + 2 * hl])
                m[f"vd{g}"] = np.ascontiguousarray(vp[r * T: r * T + T + 2 * hl])
            m.update(_get(("cB", r), lambda: _consts_B(r)))
            in_maps.append(m)
    return in_maps


def kernel_unfused(x, norm_g, w_in, q_norm_g, kv_norm_g, w_uq, w_ukv, w_out, final_g):
    x = np.asarray(x, dtype=np.float32)
    xs = [np.ascontiguousarray(x[c // 4, (c % 4) * T:(c % 4 + 1) * T]) for c in range(NCORE)]
    fg = np.ascontiguousarray(np.asarray(final_g, np.float32).reshape(1, D))
    ys = None
    for l in range(DEPTH):
        ng8 = np.ascontiguousarray(np.asarray(norm_g[l], np.float32).reshape(8, 128).T)
        qg3 = np.ascontiguousarray(np.asarray(q_norm_g[l], np.float32).reshape(3, 128).T)
        kvg2 = np.ascontiguousarray(np.asarray(kv_norm_g[l], np.float32).reshape(2, 128).T)
        resA = run_A(xs, np.ascontiguousarray(w_in[l], dtype=np.float32), ng8,
                     np.ascontiguousarray(w_uq[l], dtype=np.float32), qg3)
        in_maps = _exchange(resA, xs, np.ascontiguousarray(w_ukv[l], dtype=np.float32), kvg2,
                            np.ascontiguousarray(w_out[l], dtype=np.float32), fg)
        resB = run_B(in_maps)
        xs = [np.asarray(resB[c]["xo"]) for c in range(NCORE)]
        ys = [np.asarray(resB[c]["yo"]) for c in range(NCORE)]
    out = np.zeros((2, S_FULL, D), np.float32)
    for c in range(NCORE):
        out[c // 4, (c % 4) * T:(c % 4 + 1) * T] = ys[c]
    return out


def _consts_F(rank):
    c = dict(_consts_A(rank))
    c.update(_consts_B(rank))
    eye = np.eye(128, dtype=np.float32)
    selK = np.zeros((128, 8, 128), np.float32)
    selVL = np.zeros((64, 4, 64), np.float32)
    selVR = np.zeros((64, 4, 128), np.float32)
    for j in range(4):
        if j == rank - 1:
            selK[:, j, :] = eye
            selVL[:, j, :] = np.eye(64, dtype=np.float32)
        if j == rank + 1:
            selK[:, 4 + j, :] = eye
            selVR[:, j, 64:128] = np.eye(64, dtype=np.float32)
    c["selK"] = selK.astype(NPBF)
    c["selVL"] = selVL.astype(NPBF)
    c["selVR"] = selVR.astype(NPBF)
    return c


def build_fused(depth=DEPTH, debug=False):
    nc = bass.Bass("TRN2", target_bir_lowering=False)
    dt_in = lambda n, s, d=F32: nc.dram_tensor(n, s, d, kind="ExternalInput").ap()
    x = dt_in("x", [T, D])
    w_in = dt_in("w_in", [DEPTH, D, INW])
    ng = dt_in("ng", [DEPTH, 128, 8])
    w_uq = dt_in("w_uq", [DEPTH, 384, 768])
    qg = dt_in("qg", [DEPTH, 128, 3])
    w_ukv = dt_in("w_ukv", [DEPTH, 256, 1024])
    kvg = dt_in("kvg", [DEPTH, 128, 2])
    w_out = dt_in("w_out", [DEPTH, D, D])
    fg = dt_in("fg", [1, D])
    c96 = dt_in("c96", [96, T])
    s96 = dt_in("s96", [96, T])
    c128 = dt_in("c128", [128, T])
    s128 = dt_in("s128", [128, T])
    r96 = dt_in("r96", [96, 96], BF16)
    r128 = dt_in("r128", [128, 128], BF16)
    ident = dt_in("ident", [128, 128], BF16)
    masks = dt_in("masks", [4, 128, 128], BF16)
    e32 = dt_in("e32", [32, 96], BF16)
    selK = dt_in("selK", [128, 8, 128], BF16)
    selVL = dt_in("selVL", [64, 4, 64], BF16)
    selVR = dt_in("selVR", [64, 4, 128], BF16)
    yo = nc.dram_tensor("yo", [T, D], F32, kind="ExternalOutput").ap()
    mixo = nc.dram_tensor("mixo", [1024, T], BF16, kind="ExternalOutput").ap() if debug else None
    lato = nc.dram_tensor("lato", [128, 3, S_FULL], BF16, kind="ExternalOutput").ap() if debug else None
    xb = [nc.dram_tensor(f"xb{i}", [T, D], F32).ap() for i in range(2)]
    lat_loc = [nc.dram_tensor(f"lat_loc{i}", [288, T], BF16).ap() for i in range(2)]
    lat_allA = [nc.dram_tensor(f"lat_allA{i}", [4 * 256, T], BF16).ap() for i in range(2)]
    lat_allB = [nc.dram_tensor(f"lat_allB{i}", [4 * 32, T], BF16).ap() for i in range(2)]
    pay_h = [pay_tensors(nc, f"pay{i}", 1) for i in range(2)]
    pay_all_h = [pay_tensors(nc, f"pay_all{i}", 4) for i in range(2)]
    qT_s = nc.dram_tensor("qT_s", [8, 96, T], BF16).ap()
    gT_s = nc.dram_tensor("gT_s", [1024, T], BF16).ap()
    qd_s = nc.dram_tensor("qd_s", [3, 512, T], BF16).ap()
    kd_s = nc.dram_tensor("kd_s", [3, 512, T], BF16).ap()
    vd_s = nc.dram_tensor("vd_s", [3, T, 512], BF16).ap()
    groups = [[0, 1, 2, 3], [4, 5, 6, 7]]
    with ExitStack() as es:
        S = Sched(nc, es)
        pb = [es.enter_context(nc.psum_tensor(f"pb{i}", [128, 512], F32)) for i in range(7)]
        ptr = es.enter_context(nc.psum_tensor("ptr", [128, 1024], BF16))
        psA = {"pacc0": pb[0], "pacc1": pb[1], "pacc2": pb[2], "prot0": pb[3], "prot1": pb[4], "pssq": pb[5],
               "ptr": ptr}
        for l in range(depth):
            xin = x if l == 0 else xb[(l - 1) % 2]
            b = l % 2
            def ag(src, dst, key, ci, reads=()):
                S.custom("pool", lambda e, src=src, dst=dst: e.collective_compute(
                    "AllGather", ALU.bypass, replica_groups=groups, ins=[src], outs=[dst]), slot=("cc", ci),
                    reads=list(reads), writes=[key])

            def lat_ags(b=b):
                ag(lat_loc[b][0:256, :], lat_allA[b][:, :], "lat_allA", 0, [("lat_loc", "c", tb) for tb in range(4)])
                ag(lat_loc[b][256:288, :], lat_allB[b][:, :], "lat_allB", 1, [("lat_loc", "r", tb) for tb in range(4)])
            emit_A(nc, xin, w_in[l], ng[l], w_uq[l], qg[l], c96, s96, c128, s128, r96, r128, ident,
                   lat_loc[b], qT_s, gT_s, qd_s, [kd_s[g] for g in range(3)], [vd_s[g] for g in range(3)],
                   S=S, psum=psA, pay=pay_h[b], tag=f"_A{l}", last=False, after_rope=lat_ags)

            def halo_ags(b=b):
                for ci, key in enumerate(pay_h[b]):
                    ag(pay_h[b][key].ap()[:, :], pay_all_h[b][key].ap()[:, :], "pay_all_" + "_".join(map(str, key)), 2 + ci)
            pay_keys = lambda g: (["pay_all_2_%d" % k for k in range(4)] if g == 2 else ["pay_all_%d" % g])
            lastl = (l == depth - 1)
            hook = halo_ags
            if not OVERLAP_EXCHANGE:
                halo_ags()
                S.barrier()
                hook = None
            emit_B(nc, xin, None, qT_s, gT_s, qd_s, kd_s, vd_s, w_ukv[l], kvg[l], w_out[l], fg, masks, ident, e32,
                   xb[l % 2], yo, S=S, psum=pb, tag=f"_B{l}", mixo=(mixo if l == depth - 1 else None),
                   fz={"lat_allA": lat_allA[b], "lat_allB": lat_allB[b], "pay_all": pay_all_h[b], "after_prologue": hook, "lato": lato,
                       "pay_keys": pay_keys, "selK": selK, "selVL": selVL, "selVR": selVR},
                   last=lastl, final=lastl)
    return nc


def kernel(x, norm_g, w_in, q_norm_g, kv_norm_g, w_uq, w_ukv, w_out, final_g):
    f32c = lambda a: np.ascontiguousarray(np.asarray(a, dtype=np.float32))
    x = f32c(x)
    shared = {
        "w_in": f32c(w_in), "w_uq": f32c(w_uq), "w_ukv": f32c(w_ukv), "w_out": f32c(w_out),
        "ng": f32c(np.asarray(norm_g, np.float32).reshape(DEPTH, 8, 128).transpose(0, 2, 1)),
        "qg": f32c(np.asarray(q_norm_g, np.float32).reshape(DEPTH, 3, 128).transpose(0, 2, 1)),
        "kvg": f32c(np.asarray(kv_norm_g, np.float32).reshape(DEPTH, 2, 128).transpose(0, 2, 1)),
        "fg": f32c(np.asarray(final_g, np.float32).reshape(1, D)),
    }
    nc = _get("F", build_fused)
    in_maps = []
    for c in range(NCORE):
        m = dict(shared)
        m["x"] = np.ascontiguousarray(x[c // 4, (c % 4) * T:(c % 4 + 1) * T])
        m.update(_get(("cF", c % 4), lambda: _consts_F(c % 4)))
        in_maps.append(m)
    res = run_bass_kernel_spmd(nc, in_maps, core_ids=list(range(NCORE))).results
    out = np.zeros((2, S_FULL, D), np.float32)
    for c in range(NCORE):
        out[c // 4, (c % 4) * T:(c % 4 + 1) * T] = np.asarray(res[c]["yo"])
    return out
```

```python
import numpy as np
import ml_dtypes
from contextlib import ExitStack
import concourse.bass as bass
import concourse.mybir as mybir
from concourse.bass_utils import run_bass_kernel_spmd

F32 = mybir.dt.float32
BF16 = mybir.dt.bfloat16
AF = mybir.ActivationFunctionType
ALU = mybir.AluOpType
NPBF = ml_dtypes.bfloat16

D = 1024
T = 2048
S_FULL = 8192
NCORE = 8
DEPTH = 4
EPS = 1e-6
INW = 6304
DILS = (1, 4, 16)
HALO = tuple(64 * d for d in DILS)
MLA_SCALE = 96 ** -0.5
DIL_SCALE = 64 ** -0.5
MASKNEG = -32768.0
OVERLAP_EXCHANGE = True


class Sched:
    EP = 30000
    ENGS = ("pe", "act", "dve", "pool", "sp")

    def __init__(self, nc, es):
        self.nc, self.es = nc, es
        self.ops = {e: [] for e in self.ENGS}
        self.nops = {e: 0 for e in self.ENGS}
        self.esems = {}
        self.lastw = {}
        self.readers = {}
        self.waited = {e: {} for e in self.ENGS}
        self.dsem = {}
        self.last_tok = {}
        self.pending_barrier = {e: None for e in self.ENGS}
        self.nsem = 0

    def _newsem(self, name):
        self.nsem += 1
        return self.es.enter_context(self.nc.semaphore(name))

    def _esem(self, e, epoch):
        k = (e, epoch)
        if k not in self.esems:
            self.esems[k] = self._newsem(f"e_{e}_{epoch}")
        return self.esems[k]

    def _waits(self, eng, reads, writes):
        toks = []
        for k in reads:
            w = self.lastw.get(k)
            if w is not None:
                toks.append(w)
        for k in writes:
            w = self.lastw.get(k)
            if w is not None:
                toks.append(w)
            toks.extend(self.readers.get(k, {}).values())
        if self.pending_barrier[eng] is not None:
            toks.extend(self.pending_barrier[eng])
            self.pending_barrier[eng] = None
        need = {}
        for (skey, sem, val, src) in toks:
            if src == eng and eng == "pe":
                continue
            if val > self.waited[eng].get(skey, 0):
                self.waited[eng][skey] = val
                need[skey] = (sem, val)
        return list(need.values())

    def _commit(self, tok, reads, writes):
        for k in writes:
            self.lastw[k] = tok
            self.readers[k] = {}
        for k in reads:
            if k in writes:
                continue
            self.readers.setdefault(k, {})[tok[0]] = tok
        self.last_tok[tok[0]] = tok

    def op(self, eng, fn, reads=(), writes=()):
        waits = self._waits(eng, reads, writes)
        idx = self.nops[eng]
        self.nops[eng] += 1
        epoch, val = idx // self.EP, idx % self.EP + 1
        sem = self._esem(eng, epoch)
        tok = ((eng, epoch), sem, val, eng)
        self.ops[eng].append((waits, fn, sem, 1))
        self._commit(tok, reads, writes)

    def dma(self, q, out, in_, reads=(), writes=(), slot=None):
        waits = self._waits(q, reads, writes)
        if slot not in self.dsem:
            self.dsem[slot] = [self._newsem(f"d_{len(self.dsem)}"), 0]
        ent = self.dsem[slot]
        ent[1] += 16
        tok = (("dma", slot), ent[0], ent[1], "dma")
        self.ops[q].append((waits, lambda e, o=out, i=in_: e.dma_start(out=o, in_=i), ent[0], 16))
        self._commit(tok, reads, writes)

    def custom(self, eng, fn, slot, reads=(), writes=()):
        waits = self._waits(eng, reads, writes)
        if slot not in self.dsem:
            self.dsem[slot] = [self._newsem(f"d_{len(self.dsem)}"), 0]
        ent = self.dsem[slot]
        ent[1] += 1
        tok = (("dma", slot), ent[0], ent[1], "dma")
        self.ops[eng].append((waits, fn, ent[0], 1))
        self._commit(tok, reads, writes)

    def barrier(self):
        toks = list(self.last_tok.values())
        for e in self.ENGS:
            self.pending_barrier[e] = list(toks)

    def wait_all(self, eng="sp"):
        self.barrier()
        waits = self._waits(eng, (), ())
        self.ops[eng].append((waits, None, None, 0))

    def emit(self):
        nc = self.nc
        ops = self.ops
        self.ops = {e: [] for e in self.ENGS}

        def replay(name, e):
            for waits, fn, sem, inc in ops[name]:
                for (s, v) in waits:
                    e.wait_ge(s, v)
                if fn is not None:
                    fn(e).then_inc(sem, inc)

        with nc.Block() as block:
            @block.tensor
            def _(e):
                replay("pe", e)

            @block.scalar
            def _(e):
                replay("act", e)

            @block.vector
            def _(e):
                replay("dve", e)

            @block.gpsimd
            def _(e):
                replay("pool", e)

            @block.sync
            def _(e):
                replay("sp", e)


def pay_tensors(nc, name, mult):
    t = {}
    t[(0,)] = nc.dram_tensor(f"{name}_g0", [mult * 4 * HALO[0], 512], BF16)
    t[(1,)] = nc.dram_tensor(f"{name}_g1", [mult * 4 * HALO[1], 512], BF16)
    for k in range(4):
        t[(2, k)] = nc.dram_tensor(f"{name}_g2_{k}", [mult * HALO[2], 512], BF16)
    return t


def pay_piece(tens, g, kind, side):
    k = kind * 2 + side
    if g == 2:
        return tens[(2, k)], 0, HALO[2]
    return tens[(g,)], k * HALO[g], 4 * HALO[g]


def pay_k(tens, j, g, side, f0, nf, t0, nt, ncand=None):
    halo = HALO[g]
    h, roff, rpr = pay_piece(tens, g, 0, side)
    base = (j * rpr + roff) * 512 + f0 * halo + t0
    if ncand is None:
        return bass.AP(h, base, [[halo, nf], [1, nt]])
    return bass.AP(h, base, [[halo, nf], [rpr * 512, ncand], [1, nt]])


def pay_v(tens, j, g, side, r0, nr):
    h, roff, rpr = pay_piece(tens, g, 1, side)
    base = (j * rpr + roff + r0) * 512
    return bass.AP(h, base, [[512, nr], [1, 512]])


def pay_v_cand(tens, j, g, side, hp, d):
    h, roff, rpr = pay_piece(tens, g, 1, side)
    base = (j * rpr + roff) * 512 + hp * 128
    return bass.AP(h, base, [[d * 512, 64], [512, d], [1, 128]])


def mm_group(lhs_rhs, out):
    def fn(pe):
        n = len(lhs_rhs)
        ins = None
        for i, (l, r) in enumerate(lhs_rhs):
            ins = pe.matmul(out, lhsT=l, rhs=r, start=(i == 0), stop=(i == n - 1))
        return ins
    return fn


def _rope_tables(pos, dim):
    inv = (1.0 / (500000.0 ** (np.arange(0, dim, 2, dtype=np.float32) / np.float32(dim)))).astype(np.float32)
    ang = pos.astype(np.float32)[:, None] * inv[None, :]
    return np.cos(ang).astype(np.float32), np.sin(ang).astype(np.float32)


def _consts_A(rank):
    pos = np.arange(rank * T, (rank + 1) * T)
    cm, sm = _rope_tables(pos, 32)
    cd, sd = _rope_tables(pos, 16)
    c96 = np.ones((96, T), np.float32)
    s96 = np.zeros((96, T), np.float32)
    c96[64:80] = cm.T
    c96[80:96] = cm.T
    s96[64:80] = sm.T
    s96[80:96] = sm.T
    c128 = np.ones((128, T), np.float32)
    s128 = np.zeros((128, T), np.float32)
    for b in (0, 64):
        c128[b:b + 8] = cd.T
        c128[b + 8:b + 16] = cd.T
        s128[b:b + 8] = sd.T
        s128[b + 8:b + 16] = sd.T
    r96 = np.zeros((96, 96), np.float32)
    for i in range(16):
        r96[80 + i, 64 + i] = -1.0
        r96[64 + i, 80 + i] = 1.0
    r128 = np.zeros((128, 128), np.float32)
    for b in (0, 64):
        for i in range(8):
            r128[b + 8 + i, b + i] = -1.0
            r128[b + i, b + 8 + i] = 1.0
    return {
        "c96": c96, "s96": s96, "c128": c128, "s128": s128,
        "r96": r96.astype(NPBF), "r128": r128.astype(NPBF),
        "ident": np.eye(128, dtype=np.float32).astype(NPBF),
    }


def _consts_B(rank):
    n = np.arange(128)[:, None]
    m = np.arange(128)[None, :]
    m0 = np.where(m <= n, 0.0, MASKNEG)
    m1 = np.where(m >= n, 0.0, MASKNEG)
    m0f = m0.copy()
    m1l = m1.copy()
    if rank == 0:
        m0f[:64, :] = MASKNEG
    if rank == 3:
        m1l[64:, :] = MASKNEG
    masks = np.stack([m0, m0f, m1, m1l]).astype(np.float32).astype(NPBF)
    e32 = np.zeros((32, 96), np.float32)
    for i in range(32):
        e32[i, 64 + i] = 1.0
    return {"masks": masks, "ident": np.eye(128, dtype=np.float32).astype(NPBF), "e32": e32.astype(NPBF)}


def build_A():
    nc = bass.Bass("TRN2", target_bir_lowering=False)
    dt_in = lambda n, s, d=F32: nc.dram_tensor(n, s, d, kind="ExternalInput").ap()
    dt_out = lambda n, s, d=BF16: nc.dram_tensor(n, s, d, kind="ExternalOutput").ap()
    x = dt_in("x", [T, D])
    w_in = dt_in("w_in", [D, INW])
    ng = dt_in("ng", [128, 8])
    w_uq = dt_in("w_uq", [384, 768])
    qg = dt_in("qg", [128, 3])
    c96d = dt_in("c96", [96, T])
    s96d = dt_in("s96", [96, T])
    c128d = dt_in("c128", [128, T])
    s128d = dt_in("s128", [128, T])
    r96d = dt_in("r96", [96, 96], BF16)
    r128d = dt_in("r128", [128, 128], BF16)
    identd = dt_in("ident", [128, 128], BF16)
    lat_o = dt_out("lat", [288, T])
    qT_o = dt_out("qT", [8, 96, T])
    gT_o = dt_out("gT", [1024, T])
    qd_o = dt_out("qd", [3, 512, T])
    kd_o = dt_out("kd", [3, 512, T])
    vd_o = dt_out("vd", [3, T, 512])
    emit_A(nc, x, w_in, ng, w_uq, qg, c96d, s96d, c128d, s128d, r96d, r128d, identd,
           lat_o, qT_o, gT_o, qd_o, kd_o, vd_o)
    return nc


def emit_A(nc, x, w_in, ng, w_uq, qg, c96d, s96d, c128d, s128d, r96d, r128d, identd,
           lat_o, qT_o, gT_o, qd_o, kd_o, vd_o, S=None, psum=None, pay=None, tag="", last=True, after_rope=None):
    w_in_v = w_in.rearrange("(c p) n -> p c n", p=128)
    w_uq_v = w_uq.rearrange("(c p) n -> p c n", p=128)
    with ExitStack() as es:
        if S is None:
            S = Sched(nc, es)
        sb = lambda n, s, d=BF16: es.enter_context(nc.sbuf_tensor(n + tag, s, d))
        if psum is None:
            ps = lambda n, s, d=F32: es.enter_context(nc.psum_tensor(n, s, d))
        else:
            ps = lambda n, s, d=F32: psum[n]
        hT = sb("hT", [128, 8, T])
        c96 = sb("c96s", [96, T], F32)
        s96 = sb("s96s", [96, T], F32)
        c128 = sb("c128s", [128, T], F32)
        s128 = sb("s128s", [128, T], F32)
        r96 = sb("r96s", [96, 96])
        r128 = sb("r128s", [128, 128])
        ident = sb("idents", [128, 128])
        ones = sb("ones", [128, 128])
        ng8 = sb("ng8", [128, 8], F32)
        qg3 = sb("qg3", [128, 3], F32)
        xin = [sb(f"xin{i}", [128, D], F32) for i in range(2)]
        junk = sb("junk", [128, D], F32)
        hb = [sb(f"hb{i}", [128, D]) for i in range(2)]
        ssq = sb("ssq", [128, 16], F32)
        epst = sb("epst", [128, 1], F32)
        rstd = sb("rstd", [128, 16], F32)
        wf = [sb(f"wf{i}", [128, 8, 512], F32) for i in range(2)]
        wb = [sb(f"wb{i}", [128, 8, 512]) for i in range(2)]
        wkr = sb("wkr", [128, 8, 96])
        wuqf = sb("wuqf", [128, 3, 768], F32)
        wuqb = sb("wuqb", [128, 3, 768])
        cq_b = sb("cq_b", [128, 3, 512])
        sq_b = sb("sq_b", [128, 3, 512])
        cqn = sb("cqn", [128, 3, 512])
        rbc = sb("rbc", [128, 512], F32)
        qh = [sb(f"qh{i}", [96, 512]) for i in range(3)]
        t1 = [sb(f"t1_{i}", [128, 512], F32) for i in range(2)]
        t2 = [sb(f"t2_{i}", [128, 512], F32) for i in range(2)]
        pbf = [sb(f"pbf{i}", [128, 512]) for i in range(3)]
        ob = [sb(f"ob{i}", [128, 512]) for i in range(3)]
        pacc = [ps(f"pacc{i}", [128, 512]) for i in range(3)]
        prot = [ps(f"prot{i}", [128, 512]) for i in range(2)]
        pssq = ps("pssq", [128, 512])
        ptr = ps("ptr", [128, 1024], BF16)

        for i, (dst, src) in enumerate([(c96, c96d), (s96, s96d), (c128, c128d), (s128, s128d),
                                        (r96, r96d), (r128, r128d), (ident, identd), (ng8, ng), (qg3, qg)]):
            S.dma("sp", dst[:], src, writes=[dst.name], slot=("c", i))
        S.op("pool", lambda e: e.memset(ones[:], 1.0), writes=["ones"])
        S.op("pool", lambda e: e.memset(wkr[:], 0.0), writes=[wkr.name])
        S.op("pool", lambda e: e.memset(ssq[:], 0.0), writes=["ssq"])
        S.op("pool", lambda e: e.memset(epst[:], EPS), writes=["epst"])
        S.dma("sp", wuqf[:], w_uq_v, writes=["wuqf"], slot=("wuq",))
        S.op("pool", lambda e: e.tensor_tensor(out=wuqb[:], in0=wuqf[:],
                                               in1=qg3[:, :].unsqueeze(2).broadcast_to([128, 3, 768]), op=ALU.mult),
             reads=["wuqf", qg3.name], writes=["wuqb"])

        for t in range(16):
            xs, hbs = xin[t % 2], hb[t % 2]
            S.dma("sp", xs[:], x[t * 128:(t + 1) * 128, :], writes=[xs.name], slot=("xin", t % 2))
            S.op("act", lambda e, xs=xs, t=t: e.activation(out=junk[:], in_=xs[:], func=AF.Square,
                                                           accum_out=ssq[:, t:t + 1]),
                 reads=[xs.name, "ssq"], writes=["junk", ("ssq", t)])
            S.op("act", lambda e, t=t: e.activation(out=rstd[:, t:t + 1], in_=ssq[:, t:t + 1], func=AF.Sqrt,
                                                    scale=1.0 / D, bias=epst[:, 0:1]),
                 reads=[("ssq", t), "epst"], writes=[("rstd", t)])
            S.op("dve", lambda e, t=t: e.reciprocal(out=rstd[:, t:t + 1], in_=rstd[:, t:t + 1]),
                 reads=[("rstd", t)], writes=[("rstd", t)])
            S.op("dve", lambda e, xs=xs, hbs=hbs, t=t: e.tensor_scalar(out=hbs[:], in0=xs[:], scalar1=rstd[:, t:t + 1],
                                                                       scalar2=None, op0=ALU.mult),
                 reads=[xs.name, ("rstd", t)], writes=[hbs.name])

            def tr(pe, hbs=hbs):
                ins = None
                for c in range(8):
                    ins = pe.transpose(out=ptr[:, c * 128:(c + 1) * 128], in_=hbs[:, c * 128:(c + 1) * 128],
                                       identity=ident[:])
                return ins
            S.op("pe", tr, reads=[hbs.name, ident.name], writes=["ptr"])
            S.op("act", lambda e, t=t: e.activation(out=hT[:, :, t * 128:(t + 1) * 128],
                                                    in_=ptr[:, :].rearrange("p (c n) -> p c n", c=8), func=AF.Copy),
                 reads=["ptr"], writes=[("hT", t // 4)])

        state = {"wg": 0, "pa": 0, "pr": 0, "ob": 0, "pb": 0, "qh": 0}

        def load_w(col0, ncols, kr=False):
            i = state["wg"] % 2
            state["wg"] += 1
            S.dma("sp", wf[i][:, :, 0:ncols], w_in_v[:, :, col0:col0 + ncols], writes=[wf[i].name], slot=("wf", i))
            def cast(e):
                ins = None
                for c in range(8):
                    ins = e.activation(out=wb[i][:, c, 0:ncols], in_=wf[i][:, c, 0:ncols], func=AF.Copy,
                                       scale=ng8[:, c:c + 1])
                return ins
            S.op("act", cast, reads=[wf[i].name, ng8.name], writes=[wb[i].name])
            return wb[i]

        def nxt(k, n):
            v = state[k] % n
            state[k] += 1
            return v

        hTk = [("hT", i) for i in range(4)]

        def proj_fm(wt, c0, m, tb, out_ps):
            S.op("pe", mm_group([(wt[:, c, c0:c0 + m], hT[:, c, tb * 512:(tb + 1) * 512]) for c in range(8)],
                                out_ps[0:m, :]),
                 reads=[wt.name, ("hT", tb)], writes=[out_ps.name])

        def rope_store(src_bf, rows, rT, cs, sn, tb, dst_ap, p0=0, wkey=None):
            i = nxt("pr", 2)
            pr = prot[i]
            S.op("pe", mm_group([(rT[0:rows, 0:rows], src_bf[0:rows, :])], pr[0:rows, :]),
                 reads=[src_bf.name, rT.name], writes=[pr.name])
            a, b = t1[i], t2[i]
            tok = slice(tb * 512, (tb + 1) * 512)
            S.op("pool", lambda e: e.tensor_tensor(out=a[p0:rows, :], in0=src_bf[p0:rows, :], in1=cs[p0:rows, tok],
                                                   op=ALU.mult),
                 reads=[src_bf.name, cs.name], writes=[a.name])
            S.op("dve", lambda e: e.tensor_tensor(out=b[p0:rows, :], in0=pr[p0:rows, :], in1=sn[p0:rows, tok],
                                                  op=ALU.mult),
                 reads=[pr.name, sn.name], writes=[b.name])
            S.op("dve", lambda e: e.tensor_tensor(out=src_bf[p0:rows, :], in0=a[p0:rows, :], in1=b[p0:rows, :],
                                                  op=ALU.add),
                 reads=[a.name, b.name], writes=[src_bf.name])
            S.dma("sp", dst_ap, src_bf[dst_rows(p0, rows, dst_ap), :], reads=[src_bf.name],
                  writes=([wkey] if wkey is not None else []), slot=("st", src_bf.name))

        def dst_rows(p0, rows, dst_ap):
            n = dst_ap.shape[0]
            return slice(rows - n, rows)

        def latent_norm(nchunk, width, tb):
            S.op("pe", mm_group([(ones[:, :], sq_b[:, c, :]) for c in range(nchunk)], pssq[:, :]),
                 reads=["sq_b", "ones"], writes=["pssq"])
            S.op("act", lambda e: e.activation(out=rbc[:], in_=pssq[:], func=AF.Sqrt, scale=1.0 / width,
                                               bias=epst[:, 0:1]), reads=["pssq", "epst"], writes=["rbc"])
            S.op("dve", lambda e: e.reciprocal(out=rbc[:], in_=rbc[:]), reads=["rbc"], writes=["rbc"])
            for c in range(nchunk):
                S.op("dve", lambda e, c=c: e.tensor_tensor(out=cqn[:, c, :], in0=cq_b[:, c, :], in1=rbc[:], op=ALU.mult),
                     reads=["cq_b", "rbc"], writes=["cqn"])

        def latent_chunks(wt, nchunk, tb):
            for c3 in range(nchunk):
                pa = pacc[nxt("pa", 3)]
                proj_fm(wt, c3 * 128, 128, tb, pa)
                S.op("act", lambda e, pa=pa, c3=c3: e.activation(out=cq_b[:, c3, :], in_=pa[:], func=AF.Copy),
                     reads=[pa.name], writes=["cq_b"])
                S.op("act", lambda e, pa=pa, c3=c3: e.activation(out=sq_b[:, c3, :], in_=pa[:], func=AF.Square),
                     reads=[pa.name], writes=["sq_b"])

        pend = {"f": None}

        def defer(fn):
            prev = pend["f"]
            pend["f"] = fn
            if prev is not None:
                prev()

        def flush():
            prev = pend["f"]
            pend["f"] = None
            if prev is not None:
                prev()

        def unit_cq(wt):
            for tb in range(4):
                latent_chunks(wt, 3, tb)
                latent_norm(3, 384, tb)
                for h in range(8):
                    pa = pacc[nxt("pa", 3)]
                    q = qh[nxt("qh", 3)]
                    S.op("pe", mm_group([(wuqb[:, c, h * 96:(h + 1) * 96], cqn[:, c, :]) for c in range(3)], pa[0:96, :]),
                         reads=["wuqb", "cqn"], writes=[pa.name])
                    S.op("act", lambda e, pa=pa, q=q: e.activation(out=q[:], in_=pa[0:96, :], func=AF.Copy),
                         reads=[pa.name], writes=[q.name])
                    defer(lambda q=q, tb=tb, h=h: rope_store(q, 96, r96, c96, s96, tb,
                                                             qT_o[h, :, tb * 512:(tb + 1) * 512], p0=64))
                flush()

        def unit_ckv(wt):
            S.op("pool", lambda e: e.tensor_copy(out=wkr[:, :, 64:96], in_=wt[:, :, 256:288]), reads=[wt.name],
                 writes=[wkr.name])
            lat_v = lat_o[0:256, :].rearrange("(c p) t -> p c t", p=128)
            for tb in range(4):
                latent_chunks(wt, 2, tb)
                latent_norm(2, 256, tb)
                S.dma("sp", lat_v[:, :, tb * 512:(tb + 1) * 512], cqn[:, 0:2, :], reads=["cqn"],
                      writes=[("lat_loc", "c", tb)], slot=("st", "cqn"))
                pa = pacc[nxt("pa", 3)]
                q = qh[nxt("qh", 3)]
                proj_fm(wkr, 0, 96, tb, pa)
                S.op("act", lambda e, pa=pa, q=q: e.activation(out=q[:], in_=pa[0:96, :], func=AF.Copy),
                     reads=[pa.name], writes=[q.name])
                defer(lambda q=q, tb=tb: rope_store(q, 96, r96, c96, s96, tb,
                                                    lat_o[256:288, tb * 512:(tb + 1) * 512], p0=64,
                                                    wkey=("lat_loc", "r", tb)))
            flush()

        def gate_blocks(wt, row0):
            for blk in range(4):
                for tb in range(4):
                    pa = pacc[nxt("pa", 3)]
                    o = ob[nxt("ob", 3)]
                    proj_fm(wt, blk * 128, 128, tb, pa)
                    S.op("act", lambda e, pa=pa, o=o: e.activation(out=o[:], in_=pa[:], func=AF.Silu),
                         reads=[pa.name], writes=[o.name])
                    S.dma("sp", gT_o[row0 + blk * 128: row0 + (blk + 1) * 128, tb * 512:(tb + 1) * 512], o[:],
                          reads=[o.name], slot=("st", o.name))

        def rope_post(p, blk, tb, dst, g):
            rope_store(p, 128, r128, c128, s128, tb, dst[blk * 128:(blk + 1) * 128, tb * 512:(tb + 1) * 512])
            if pay is not None and g is not None:
                halo = HALO[g]
                if tb * 512 < halo:
                    n = min(halo - tb * 512, 512)
                    S.dma("sp", pay_k(pay, 0, g, 0, blk * 128, 128, tb * 512, n), p[:, 0:n], reads=[p.name],
                          slot=("st", p.name))
                lo, hi = max(tb * 512, T - halo), tb * 512 + 512
                if lo < hi:
                    S.dma("sp", pay_k(pay, 0, g, 1, blk * 128, 128, lo - (T - halo), hi - lo),
                          p[:, lo - tb * 512:512], reads=[p.name], slot=("st", p.name))

        def rope_blocks(wt, dst, g=None):
            for blk in range(4):
                for tb in range(4):
                    pa = pacc[nxt("pa", 3)]
                    p = pbf[nxt("pb", 3)]
                    proj_fm(wt, blk * 128, 128, tb, pa)
                    S.op("act", lambda e, pa=pa, p=p: e.activation(out=p[:], in_=pa[:], func=AF.Copy),
                         reads=[pa.name], writes=[p.name])
                    defer(lambda p=p, blk=blk, tb=tb: rope_post(p, blk, tb, dst, g))
            flush()

        def v_blocks(wt, dst, g=None):
            for t in range(16):
                pa = pacc[nxt("pa", 3)]
                o = ob[nxt("ob", 3)]
                S.op("pe", mm_group([(hT[:, c, t * 128:(t + 1) * 128], wt[:, c, 0:512]) for c in range(8)], pa[:, :]),
                     reads=[wt.name, ("hT", t // 4)], writes=[pa.name])
                S.op("act", lambda e, pa=pa, o=o: e.activation(out=o[:], in_=pa[:], func=AF.Copy),
                     reads=[pa.name], writes=[o.name])
                S.dma("sp", dst[t * 128:(t + 1) * 128, :], o[:], reads=[o.name], slot=("st", o.name))
                if pay is not None and g is not None:
                    halo = HALO[g]
                    if t * 128 < halo:
                        n = min(halo - t * 128, 128)
                        S.dma("sp", pay_v(pay, 0, g, 0, t * 128, n), o[0:n, :], reads=[o.name], slot=("st", o.name))
                    lo, hi = max(t * 128, T - halo), t * 128 + 128
                    if lo < hi:
                        S.dma("sp", pay_v(pay, 0, g, 1, lo - (T - halo), hi - lo), o[lo - t * 128:128, :],
                              reads=[o.name], slot=("st", o.name))

        units = [(0, 384, unit_cq), (384, 288, unit_ckv)]
        for g in range(3):
            base = 1184 + g * 1536
            units.append((base, 512, lambda wt, g=g: rope_blocks(wt, qd_o[g])))
            units.append((base + 512, 512, lambda wt, g=g: rope_blocks(wt, kd_o[g], g)))
        n_rope_units = len(units)
        units.append((672, 512, lambda wt: gate_blocks(wt, 0)))
        for g in range(3):
            base = 1184 + g * 1536
            units.append((base + 1024, 512, lambda wt, g=g: v_blocks(wt, vd_o[g], g)))
        units.append((5792, 512, lambda wt: gate_blocks(wt, 512)))
        wt = load_w(units[0][0], units[0][1])
        for ui, (c0, ncols, fn) in enumerate(units):
            wnext = load_w(units[ui + 1][0], units[ui + 1][1]) if ui + 1 < len(units) else None
            fn(wt)
            wt = wnext
            if ui == n_rope_units - 1 and after_rope is not None:
                after_rope()
        if last:
            S.wait_all("sp")
        else:
            S.barrier()
        S.emit()


_CACHE = {}


def _get(name, builder):
    if name not in _CACHE:
        _CACHE[name] = builder()
    return _CACHE[name]


def run_A(x_shards, w_in_l, ng_l, w_uq_l, qg_l):
    nc = _get("A", build_A)
    in_maps = []
    for c in range(NCORE):
        m = {"x": x_shards[c], "w_in": w_in_l, "ng": ng_l, "w_uq": w_uq_l, "qg": qg_l}
        m.update(_get(("cA", c % 4), lambda: _consts_A(c % 4)))
        in_maps.append(m)
    res = run_bass_kernel_spmd(nc, in_maps, core_ids=list(range(NCORE)))
    return res.results


def build_B(mla_heads=tuple(range(8)), dil_hps=tuple(range(4)), debug=False, dil_groups=(0, 1, 2)):
    nc = bass.Bass("TRN2", target_bir_lowering=False)
    dt_in = lambda n, s, d=F32: nc.dram_tensor(n, s, d, kind="ExternalInput").ap()
    dt_out = lambda n, s, d=F32: nc.dram_tensor(n, s, d, kind="ExternalOutput").ap()
    a = dict(
        x=dt_in("x", [T, D]),
        latT=dt_in("latT", [288, S_FULL], BF16),
        qT=dt_in("qT", [8, 96, T], BF16),
        gT=dt_in("gT", [1024, T], BF16),
        qd=dt_in("qd", [3, 512, T], BF16),
        kd=[dt_in(f"kd{g}", [512, T + 2 * HALO[g]], BF16) for g in range(3)],
        vd=[dt_in(f"vd{g}", [T + 2 * HALO[g], 512], BF16) for g in range(3)],
        w_ukv=dt_in("w_ukv", [256, 1024]),
        kvg=dt_in("kvg", [128, 2]),
        w_out=dt_in("w_out", [D, D]),
        fg=dt_in("fg", [1, D]),
        masks=dt_in("masks", [4, 128, 128], BF16),
        ident=dt_in("ident", [128, 128], BF16),
        e32=dt_in("e32", [32, 96], BF16),
        xo=dt_out("xo", [T, D]),
        yo=dt_out("yo", [T, D]),
    )
    if debug:
        a["mixo"] = nc.dram_tensor("mixo", [1024, T], BF16, kind="ExternalOutput").ap()
    emit_B(nc, mla_heads=mla_heads, dil_hps=dil_hps, dil_groups=dil_groups, **a)
    return nc


def emit_B(nc, x, latT, qT, gT, qd, kd, vd, w_ukv, kvg, w_out, fg, masks, ident, e32, xo, yo,
           mla_heads=tuple(range(8)), dil_hps=tuple(range(4)), mixo=None, dil_groups=(0, 1, 2),
           S=None, psum=None, tag="", fz=None, last=True, final=True):
    with ExitStack() as es:
        if S is None:
            S = Sched(nc, es)
        sb = lambda n, s, d=BF16: es.enter_context(nc.sbuf_tensor(n + tag, s, d))
        if psum is None:
            pb = [es.enter_context(nc.psum_tensor(f"pb{i}", [128, 512], F32)) for i in range(8)]
        else:
            pb = psum
        pS, pAcc, pK = pb[0:3], pb[3:5], pb[5:7]
        if fz is not None:
            selK = sb("selK", [128, 8, 128])
            selVL = sb("selVL", [64, 4, 64])
            selVR = sb("selVR", [64, 4, 128])
            S.dma("sp", selK[:], fz["selK"], writes=["selK"], slot=("c", 4))
            S.dma("sp", selVL[:], fz["selVL"], writes=["selVL"], slot=("c", 5))
            S.dma("sp", selVR[:], fz["selVR"], writes=["selVR"], slot=("c", 6))
        mixedT = sb("mixedT", [128, 8, T])
        identS = sb("identS", [128, 128])
        e32S = sb("e32S", [32, 96])
        maskS = sb("maskS", [128, 4, 128])
        onesf = sb("onesf", [128, 64], F32)
        kvg2 = sb("kvg2", [128, 2], F32)
        epst = sb("epst", [128, 1], F32)
        rd = sb("rd", [128, 512], F32)
        tmp = [sb(f"tmp{i}", [128, 512], F32) for i in range(2)]
        gts = [sb(f"gts{i}", [128, 512]) for i in range(2)]
        S.dma("sp", identS[:], ident, writes=["identS"], slot=("c", 0))
        S.dma("sp", e32S[:], e32, writes=["e32S"], slot=("c", 1))
        S.dma("sp", maskS[:], masks.rearrange("k n m -> n k m"), writes=["maskS"], slot=("c", 2))
        S.dma("sp", kvg2[:], kvg, writes=["kvg2"], slot=("c", 3))
        S.op("pool", lambda e: e.memset(onesf[:], 1.0), writes=["onesf"])
        onesb = sb("onesb", [128, 2])
        S.op("pool", lambda e: e.memset(onesb[:], 1.0), writes=["onesb"])
        S.op("pool", lambda e: e.memset(epst[:], EPS), writes=["epst"])
        m01 = sb("m01", [128, 4, 128])
        m01c = sb("m01c", [128, 4, 2, 128])
        S.op("dve", lambda e: e.tensor_scalar(out=m01[:], in0=maskS[:], scalar1=1.0 / 32768.0, scalar2=1.0,
                                              op0=ALU.mult, op1=ALU.add), reads=["maskS"], writes=["m01"])
        for ci in range(4):
            h0 = 1 if (ci & 1) else 0
            h1 = 3 if (ci & 2) else 2
            S.op("pool", lambda e, ci=ci, h0=h0: e.tensor_copy(out=m01c[:, ci, 0, :], in_=m01[:, h0, :]),
                 reads=["m01"], writes=["m01c"])
            S.op("pool", lambda e, ci=ci, h1=h1: e.tensor_copy(out=m01c[:, ci, 1, :], in_=m01[:, h1, :]),
                 reads=["m01"], writes=["m01c"])
        st = {"tmp": 0, "gts": 0, "acc": 0, "pk": 0}

        def nxt(k, n):
            v = st[k] % n
            st[k] += 1
            return v

        def get_pk():
            return pK[0] if st.get("pk_single") else pK[nxt("pk", 2)]

        def normalize_gate(src_o, src_den, gate_ap, gate_reads, po, chunk, qb, src_reads):
            tok = slice(qb * 512, (qb + 1) * 512)
            pk = get_pk()
            tm = tmp[nxt("tmp", 2)]
            S.op("act", lambda e: e.activation(out=rd[64:65, :], in_=src_den, func=AF.Ln), reads=src_reads, writes=["rd"])
            S.op("act", lambda e: e.activation(out=rd[64:65, :], in_=rd[64:65, :], func=AF.Exp, scale=-1.0),
                 reads=["rd"], writes=["rd"])
            S.op("pe", mm_group([(onesf[64:65, 0:64], rd[64:65, :])], pk[0:64, :]), reads=["rd", "onesf"],
                 writes=[pk.name])
            S.op("dve", lambda e: e.tensor_tensor(out=tm[po:po + 64, :], in0=src_o, in1=pk[0:64, :], op=ALU.mult),
                 reads=src_reads + [pk.name], writes=[tm.name])
            S.op("dve", lambda e: e.tensor_tensor(out=mixedT[po:po + 64, chunk, tok], in0=tm[po:po + 64, :],
                                                  in1=gate_ap, op=ALU.mult),
                 reads=[tm.name] + gate_reads, writes=[("mixedT", chunk)])

        with ExitStack() as es1:
            sb1 = lambda n, s, d=BF16: es1.enter_context(nc.sbuf_tensor(n + tag, s, d))
            lat = sb1("lat", [128, 3, S_FULL])
            KT = sb1("KT", [96, S_FULL])
            V = sb1("V", [128, 64, 65])
            Qh = [sb1(f"Qh{i}", [96, T]) for i in range(2)]
            PT = [sb1(f"PT{i}", [128, 512]) for i in range(3)]
            osb = sb1("osb", [64, 512], F32)
            wukvf = sb1("wukvf", [128, 2, 1024], F32)
            wk96 = sb1("wk96", [128, 2, 8, 96])
            wv = sb1("wv", [128, 2, 8, 64])
            S.dma("sp", wukvf[:], w_ukv.rearrange("(c p) n -> p c n", p=128), writes=["wukvf"], slot=("wukv",))
            S.op("pool", lambda e: e.memset(wk96[:], 0.0), writes=["wk96"])
            S.op("pool", lambda e: e.memset(V[:], 1.0), writes=["V"])
            for c in range(2):
                wsrc = wukvf[:, c, :].rearrange("p (h k) -> p h k", h=8)
                S.op("pool", lambda e, c=c, wsrc=wsrc: e.tensor_tensor(
                    out=wk96[:, c, :, 0:64], in0=wsrc[:, :, 0:64],
                    in1=kvg2[:, c:c + 1].unsqueeze(2).broadcast_to([128, 8, 64]), op=ALU.mult),
                    reads=["wukvf", "kvg2"], writes=["wk96"])
                S.op("pool", lambda e, c=c, wsrc=wsrc: e.tensor_tensor(
                    out=wv[:, c, :, :], in0=wsrc[:, :, 64:128],
                    in1=kvg2[:, c:c + 1].unsqueeze(2).broadcast_to([128, 8, 64]), op=ALU.mult),
                    reads=["wukvf", "kvg2"], writes=["wv"])
            if fz is not None and fz.get("after_prologue") is not None:
                fz["after_prologue"]()
            for i in range(4):
                sl = slice(i * 2048, (i + 1) * 2048)
                if fz is None:
                    lat_v = latT[0:256, :].rearrange("(c p) t -> p c t", p=128)
                    S.dma("sp", lat[:, 0:2, sl], lat_v[:, :, sl], writes=[("lat", i)], slot=("lat", i))
                    S.dma("sp", lat[0:32, 2, sl], latT[256:288, sl], writes=[("latr", i)], slot=("latr", i))
                else:
                    S.dma("sp", lat[:, 0:2, sl],
                          fz["lat_allA"][i * 256:(i + 1) * 256, :].rearrange("(c p) t -> p c t", p=128),
                          reads=["lat_allA"], writes=[("lat", i)], slot=("lat", i))
                    S.dma("sp", lat[0:32, 2, sl], fz["lat_allB"][i * 32:(i + 1) * 32, :], reads=["lat_allB"],
                          writes=[("latr", i)], slot=("latr", i))

            for h in mla_heads:
                po, chunk = (h % 2) * 64, h // 2
                Q = Qh[h % 2]
                S.dma("sp", Q[:], qT[h], writes=[Q.name], slot=("Q", h % 2))
                for tb in range(16):
                    pk = pK[nxt("pk", 2)]
                    sl = slice(tb * 512, (tb + 1) * 512)
                    S.op("pe", mm_group([(wk96[:, 0, h, :], lat[:, 0, sl]), (wk96[:, 1, h, :], lat[:, 1, sl]),
                                         (e32S[0:32, :], lat[0:32, 2, sl])], pk[0:96, :]),
                         reads=["wk96", "e32S", ("lat", tb // 4), ("latr", tb // 4)], writes=[pk.name])
                    eng = "dve" if tb % 2 else "act"
                    if eng == "act":
                        S.op("act", lambda e, pk=pk, sl=sl: e.activation(out=KT[:, sl], in_=pk[0:96, :], func=AF.Copy),
                             reads=[pk.name], writes=[("KT", tb)])
                    else:
                        S.op("dve", lambda e, pk=pk, sl=sl: e.tensor_copy(out=KT[:, sl], in_=pk[0:96, :]),
                             reads=[pk.name], writes=[("KT", tb)])
                for k8 in range(8):
                    pk = pK[nxt("pk", 2)]

                    def vb(pe, pk=pk, k8=k8, h=h):
                        ins = None
                        for j in range(8):
                            kt = k8 * 8 + j
                            for c in range(2):
                                ins = pe.matmul(pk[:, j * 64:(j + 1) * 64], lhsT=lat[:, c, kt * 128:(kt + 1) * 128],
                                                rhs=wv[:, c, h, :], start=(c == 0), stop=(c == 1))
                        return ins
                    S.op("pe", vb, reads=["wv", ("lat", k8 // 2)], writes=[pk.name])
                    S.op("dve", lambda e, pk=pk, k8=k8: e.tensor_copy(
                        out=V[:, k8 * 8:(k8 + 1) * 8, 0:64], in_=pk[:, :].rearrange("p (j d) -> p j d", j=8)),
                        reads=[pk.name], writes=[("V", k8)])
                steps = [(qb, kt) for qb in range(4) for kt in range(64)]
                accs, gidx = {}, {}
                for qb in range(4):
                    accs[qb] = pAcc[nxt("acc", 2)]
                base = st.get("sstep", 0)

                def qk(n):
                    qb, kt = steps[n]
                    s_ = pS[(base + n) % 3]
                    S.op("pe", mm_group([(KT[:, kt * 128:(kt + 1) * 128], Q[:, qb * 512:(qb + 1) * 512])], s_[:, :]),
                         reads=[("KT", kt // 4), Q.name], writes=[s_.name])
                qk(0)
                qk(1)
                for n, (qb, kt) in enumerate(steps):
                    s_, p_ = pS[(base + n) % 3], PT[(base + n) % 3]
                    acc = accs[qb]
                    if kt == 0:
                        g_i = nxt("gts", 2)
                        gidx[qb] = g_i
                        S.dma("sp", gts[g_i][po:po + 64, :], gT[h * 64:(h + 1) * 64, qb * 512:(qb + 1) * 512],
                              writes=[gts[g_i].name], slot=("gts", g_i))
                    S.op("act", lambda e, s_=s_, p_=p_: e.activation(out=p_[:], in_=s_[:], func=AF.Exp,
                                                                     scale=MLA_SCALE),
                         reads=[s_.name], writes=[p_.name])
                    if n + 2 < len(steps):
                        qk(n + 2)
                    S.op("pe", lambda pe, kt=kt, p_=p_, acc=acc: pe.matmul(
                        acc[0:65, :], lhsT=V[:, kt, :], rhs=p_[:], start=(kt == 0), stop=(kt == 63)),
                        reads=[p_.name, ("V", kt // 8)], writes=[acc.name])
                    if kt == 63:
                        g_i = gidx[qb]
                        S.op("act", lambda e, acc=acc: e.activation(out=osb[:], in_=acc[0:64, :], func=AF.Copy),
                             reads=[acc.name], writes=["osb"])
                        normalize_gate(osb[0:64, :], acc[64:65, :], gts[g_i][po:po + 64, :], [gts[g_i].name], po, chunk,
                                       qb, ["osb", acc.name])
                st["sstep"] = base + len(steps)
            if fz is not None and fz.get("lato") is not None:
                S.dma("sp", fz["lato"], lat[:], reads=[("lat", i) for i in range(4)] + [("latr", i) for i in range(4)],
                      slot=("lato",))
            S.barrier()
            S.emit()

        with ExitStack() as es2:
            sb2 = lambda n, s, d=BF16: es2.enter_context(nc.sbuf_tensor(n + tag, s, d))
            Kg = [sb2(f"Kg{i}", [128, T + 2 * HALO[2]]) for i in range(2)]
            Qg = [sb2(f"Qg{i}", [128, T]) for i in range(2)]
            Vraw = [sb2(f"Vraw{i}", [128, 32, 128]) for i in range(2)]
            accT = sb2("accT", [65, 2, T], F32)
            gtb = sb2("gtb", [128, T])
            P2p = [sb2(f"P2p{i}", [128, 4, 128]) for i in range(4)]
            Sbanks = [pS[0], pS[1], pS[2], pK[1]]
            st["pk_single"] = True
            if fz is not None:
                candK = [sb2(f"candK{i}", [128, 4, HALO[2]]) for i in range(2)]
                candV = [sb2(f"candV{i}", [64, 4, 16, 128]) for i in range(2)]
            jobs = [(hp, g) for hp in dil_hps for g in dil_groups]

            jinfo = {}

            def stage(n):
                hp, g = jobs[n]
                d, halo = DILS[g], HALO[g]
                W = T + 2 * halo
                nbr = 16 // d + 1
                njb = 16 // d
                bi = n % 2
                kg, qg_, vr = Kg[bi], Qg[bi], Vraw[bi]
                kkeys, vkeys = [(kg.name, "own")], []
                S.dma("sp", qg_[:], qd[g, hp * 128:(hp + 1) * 128, :], writes=[qg_.name], slot=("qg", bi))
                cols = slice(hp * 128, (hp + 1) * 128)
                if fz is None:
                    S.dma("sp", kg[:, 0:W], kd[g][hp * 128:(hp + 1) * 128, :], writes=[(kg.name, "own")], slot=("kg", bi))
                    vsrc = vd[g].rearrange("(kb n r) c -> n r kb c", n=128, r=d)
                    for r in range(d):
                        vkeys.append((vr.name, "o", r, 0))
                        S.dma("sp", vr[:, r * nbr:(r + 1) * nbr, :], vsrc[:, r, :, cols],
                              writes=[vkeys[-1]], slot=("vr", bi))
                else:
                    pall = fz["pay_all"]
                    pkeys = fz["pay_keys"](g)
                    S.dma("sp", kg[:, halo:halo + T], kd[g][hp * 128:(hp + 1) * 128, :], writes=[(kg.name, "own")],
                          slot=("kg", bi))
                    vown = vd[g].rearrange("(kb two n r) c -> two n r kb c", two=2, n=64, r=d)
                    for r in range(d):
                        vkeys.append((vr.name, "o", r, 0))
                        S.dma("sp", vr[64:128, r * nbr:r * nbr + njb, :], vown[0][:, r, :, cols],
                              writes=[vkeys[-1]], slot=("vr", bi))
                        vkeys.append((vr.name, "o", r, 1))
                        S.dma("sp", vr[0:64, r * nbr + 1:r * nbr + 1 + njb, :], vown[1][:, r, :, cols],
                              writes=[vkeys[-1]], slot=("vr", bi))
                    for side in range(2):
                        piece = 1 - side
                        ck = candK[side]
                        S.dma("sp", ck[:, :, 0:halo], pay_k(pall, 0, g, piece, hp * 128, 128, 0, halo, ncand=4),
                              reads=pkeys, writes=[ck.name], slot=("candK", side))
                        cv = candV[side]
                        cvkeys = [(cv.name, j) for j in range(4)]
                        for j in range(4):
                            S.dma("sp", cv[:, j, 0:d, :], pay_v_cand(pall, j, g, piece, hp, d),
                                  reads=pkeys, writes=[cvkeys[j]], slot=("candV", side))
                        dst0 = 0 if side == 0 else halo + T
                        for c0 in range(0, halo, 512):
                            nn = min(512, halo - c0)
                            pk = get_pk()
                            S.op("pe", mm_group([(selK[:, side * 4 + j, :], ck[:, j, c0:c0 + nn]) for j in range(4)],
                                                pk[:, 0:nn]), reads=["selK", ck.name], writes=[pk.name])
                            kkeys.append((kg.name, "h", side, c0))
                            S.op("act", lambda e, pk=pk, nn=nn, c0=c0, dst0=dst0, kg=kg: e.activation(
                                out=kg[:, dst0 + c0:dst0 + c0 + nn], in_=pk[:, 0:nn], func=AF.Copy),
                                reads=[pk.name], writes=[kkeys[-1]])
                        for r0 in range(0, d, 4):
                            nr = min(4, d - r0)
                            pk = get_pk()
                            if side == 0:
                                lhs = [selVL[:, j, :] for j in range(4)]
                                prow, kbs, mrows = slice(0, 64), 0, 64
                            else:
                                lhs = [selVR[:, j, :] for j in range(4)]
                                prow, kbs, mrows = slice(64, 128), nbr - 1, 128
                            S.op("pe", mm_group([(lhs[j], cv[:, j, r0:r0 + nr, :]) for j in range(4)],
                                                pk[0:mrows, 0:nr * 128]), reads=["selVL", "selVR"] + cvkeys,
                                 writes=[pk.name])
                            vkeys.append((vr.name, "h", side, r0))
                            S.op("dve", lambda e, pk=pk, nr=nr, r0=r0, prow=prow, kbs=kbs, vr=vr, nbr=nbr: e.tensor_copy(
                                out=vr[prow, r0 * nbr + kbs:(r0 + nr - 1) * nbr + kbs + 1:nbr, :],
                                in_=pk[prow, 0:nr * 128].rearrange("p (r c) -> p r c", r=nr)),
                                reads=[pk.name], writes=[vkeys[-1]])
                jinfo[n] = (kkeys, vkeys)

            def compute(n):
                hp, g = jobs[n]
                d = DILS[g]
                nbr = 16 // d + 1
                njb = 16 // d
                bi = n % 2
                kg, qg_, vr = Kg[bi], Qg[bi], Vraw[bi]
                kkeys, vkeys = jinfo[n]
                first = (g == dil_groups[0])
                pbase = st.get("pbase", 0)
                accv = [accT[:, hl, :].rearrange("p (jb m r) -> p r jb m", r=d, m=128) for hl in range(2)]
                tiles = []
                for hl in range(2):
                    if d == 1:
                        quads = [[(0, 4 * q + j) for j in range(4)] for q in range(4)]
                        dsts = [accv[hl][:, 0, 4 * q:4 * q + 4, :] for q in range(4)]
                    elif d == 4:
                        quads = [[(r, j) for j in range(4)] for r in range(4)]
                        dsts = [accv[hl][:, r, 0:4, :] for r in range(4)]
                    else:
                        quads = [[(4 * q + j, 0) for j in range(4)] for q in range(4)]
                        dsts = [accv[hl][:, 4 * q:4 * q + 4, 0, :] for q in range(4)]
                    for quad, dst in zip(quads, dsts):
                        for j, (r, jb) in enumerate(quad):
                            tiles.append((hl, r, jb, j, dst))

                pairs = [(tiles[2 * k], tiles[2 * k + 1]) for k in range(len(tiles) // 2)]
                LA = 3

                def qkpair(p):
                    bank = Sbanks[(pbase + p) % 4]
                    mms = []
                    for t, (hl, r, jb, j, dst) in enumerate(pairs[p]):
                        po = hl * 64
                        for half in range(2):
                            kb = jb + half
                            k0 = r + d * kb * 128
                            q0 = r + d * jb * 128
                            mms.append((bank[:, (2 * t + half) * 128:(2 * t + half + 1) * 128],
                                        kg[po:po + 64, k0:k0 + d * 127 + 1:d], qg_[po:po + 64, q0:q0 + d * 127 + 1:d]))

                    def fn(pe):
                        ins = None
                        for o_, kap, qap in mms:
                            ins = pe.matmul(o_, lhsT=kap, rhs=qap, start=True, stop=True)
                        return ins
                    S.op("pe", fn, reads=kkeys + [qg_.name], writes=[bank.name])
                for p in range(min(LA, len(pairs))):
                    qkpair(p)
                for p, pr_ in enumerate(pairs):
                    bank, pp = Sbanks[(pbase + p) % 4], P2p[(pbase + p) % 4]
                    S.op("act", lambda e, bank=bank, pp=pp: e.activation(
                        out=pp[:, :, :], in_=bank[:, :].rearrange("p (a m) -> p a m", a=4), func=AF.Exp,
                        scale=DIL_SCALE), reads=[bank.name], writes=[(pp.name, 0), (pp.name, 1)])
                    for t, (hl, r, jb, j, dst) in enumerate(pr_):
                        ci = (1 if jb == 0 else 0) + (2 if jb == njb - 1 else 0)
                        S.op("dve", lambda e, pp=pp, ci=ci, t=t: e.tensor_tensor(
                            out=pp[:, 2 * t:2 * t + 2, :], in0=pp[:, 2 * t:2 * t + 2, :], in1=m01c[:, ci, :, :],
                            op=ALU.mult), reads=[(pp.name, t), "m01c"], writes=[(pp.name, t)])
                    if p + LA < len(pairs):
                        qkpair(p + LA)
                    hl0, j1 = pr_[0][0], pr_[1][3]
                    pv = pAcc[((2 * p) // 4) % 2]

                    def pvf(pe, pr_=pr_, pp=pp, pv=pv, vr=vr, nbr=nbr):
                        ins = None
                        for t, (hl, r, jb, j, dst) in enumerate(pr_):
                            for half in range(2):
                                pe.matmul(pv[0:64, j * 128:(j + 1) * 128],
                                          lhsT=vr[:, r * nbr + jb + half, hl * 64:(hl + 1) * 64],
                                          rhs=pp[:, 2 * t + half, :], start=(half == 0), stop=(half == 1))
                            for half in range(2):
                                ins = pe.matmul(pv[64:65, j * 128:(j + 1) * 128], lhsT=onesb[:, 0:1],
                                                rhs=pp[:, 2 * t + half, :], start=(half == 0), stop=(half == 1))
                        return ins
                    S.op("pe", pvf, reads=[(pp.name, 0), (pp.name, 1), "onesb"] + vkeys, writes=[pv.name])
                    if j1 == 3:
                        dst = pr_[1][4]
                        src = pv[0:65, :].rearrange("p (a m) -> p a m", a=4)
                        if first:
                            S.op("dve", lambda e, dst=dst, src=src: e.tensor_copy(out=dst, in_=src),
                                 reads=[pv.name], writes=[("accT", hl0)])
                        else:
                            S.op("dve", lambda e, dst=dst, src=src: e.tensor_tensor(out=dst, in0=src, in1=dst,
                                                                                    op=ALU.add),
                                 reads=[pv.name, ("accT", hl0)], writes=[("accT", hl0)])
                st["pbase"] = pbase + len(pairs)

            if jobs:
                stage(0)
            for n, (hp, g) in enumerate(jobs):
                if g == dil_groups[0]:
                    S.dma("sp", gtb[:], gT[512 + hp * 128: 512 + (hp + 1) * 128, :], writes=["gtb"], slot=("gtb",))
                if n + 1 < len(jobs):
                    stage(n + 1)
                compute(n)
                if g == dil_groups[-1]:
                    for hl in range(2):
                        po = hl * 64
                        for qb in range(4):
                            tok = slice(qb * 512, (qb + 1) * 512)
                            normalize_gate(accT[0:64, hl, tok], accT[64:65, hl, tok], gtb[po:po + 64, tok], ["gtb"], po,
                                           4 + hp, qb, [("accT", hl)])
            S.barrier()
            S.emit()

        with ExitStack() as es3:
            sb3 = lambda n, s, d=BF16: es3.enter_context(nc.sbuf_tensor(n + tag, s, d))
            wof = sb3("wof", [128, 8, 512], F32)
            wob = sb3("wob", [128, 8, D])
            fgb = sb3("fgb", [128, D], F32)
            xin = [sb3(f"xin{i}", [128, D], F32) for i in range(2)]
            xn = [sb3(f"xn{i}", [128, D], F32) for i in range(2)]
            yv = [sb3(f"yv{i}", [128, D], F32) for i in range(2)]
            junk = sb3("junk", [128, D], F32)
            ssq = sb3("ssq", [128, 16], F32)
            rstd = sb3("rstd", [128, 16], F32)
            S.op("pool", lambda e: e.memset(ssq[:], 0.0), writes=["ssq"])
            S.dma("sp", fgb[:], fg.partition_broadcast(128), writes=["fgb"], slot=("fgb",))
            w_out_v = w_out.rearrange("(c p) n -> p c n", p=128)
            if mixo is not None:
                S.dma("pool", mixo.rearrange("(c p) t -> p c t", p=128), mixedT[:], reads=[("mixedT", c) for c in range(8)],
                      slot=("mixo",))
            for nb in range(2):
                S.dma("sp", wof[:], w_out_v[:, :, nb * 512:(nb + 1) * 512], writes=["wof"], slot=("wof",))
                S.op("pool", lambda e, nb=nb: e.tensor_copy(out=wob[:, :, nb * 512:(nb + 1) * 512], in_=wof[:]),
                     reads=["wof"], writes=["wob"])
            for t in range(16):
                xs, xnn, yy = xin[t % 2], xn[t % 2], yv[t % 2]
                S.dma("sp", xs[:], x[t * 128:(t + 1) * 128, :], writes=[xs.name], slot=("xin", t % 2))
                for nb in range(2):
                    pa = pS[(2 * t + nb) % 3]
                    S.op("pe", mm_group([(mixedT[:, c, t * 128:(t + 1) * 128], wob[:, c, nb * 512:(nb + 1) * 512])
                                         for c in range(8)], pa[:, :]),
                         reads=["wob"] + [("mixedT", c) for c in range(8)], writes=[pa.name])
                    S.op("dve", lambda e, pa=pa, nb=nb, xs=xs, xnn=xnn: e.tensor_tensor(
                        out=xnn[:, nb * 512:(nb + 1) * 512], in0=pa[:, :], in1=xs[:, nb * 512:(nb + 1) * 512], op=ALU.add),
                        reads=[pa.name, xs.name], writes=[xnn.name])
                S.dma("pool", xo[t * 128:(t + 1) * 128, :], xnn[:], reads=[xnn.name], slot=("xo", t % 2))
                if not final:
                    continue
                S.op("act", lambda e, xnn=xnn, t=t: e.activation(out=junk[:], in_=xnn[:], func=AF.Square,
                                                                 accum_out=ssq[:, t:t + 1]),
                     reads=[xnn.name, "ssq"], writes=["junk", ("ssq", t)])
                S.op("act", lambda e, t=t: e.activation(out=rstd[:, t:t + 1], in_=ssq[:, t:t + 1], func=AF.Sqrt,
                                                        scale=1.0 / D, bias=epst[:, 0:1]),
                     reads=[("ssq", t), "epst"], writes=[("rstd", t)])
                S.op("dve", lambda e, t=t: e.reciprocal(out=rstd[:, t:t + 1], in_=rstd[:, t:t + 1]),
                     reads=[("rstd", t)], writes=[("rstd", t)])
                S.op("dve", lambda e, xnn=xnn, yy=yy, t=t: e.scalar_tensor_tensor(
                    out=yy[:], in0=xnn[:], scalar=rstd[:, t:t + 1], in1=fgb[:], op0=ALU.mult, op1=ALU.mult),
                    reads=[xnn.name, ("rstd", t), "fgb"], writes=[yy.name])
                S.dma("pool", yo[t * 128:(t + 1) * 128, :], yy[:], reads=[yy.name], slot=("yo", t % 2))
            if last:
                S.wait_all("sp")
            else:
                S.barrier()
            S.emit()


def run_B(in_maps):
    nc = _get("B", build_B)
    res = run_bass_kernel_spmd(nc, in_maps, core_ids=list(range(NCORE)))
    return res.results


def _exchange(resA, x_shards, w_ukv_l, kvg_l, w_out_l, fg):
    in_maps = []
    for b in range(2):
        cores = [4 * b + r for r in range(4)]
        lat_full = np.concatenate([np.asarray(resA[c]["lat"]) for c in cores], axis=1)
        kd_full = np.concatenate([np.asarray(resA[c]["kd"]) for c in cores], axis=2)
        vd_full = np.concatenate([np.asarray(resA[c]["vd"]) for c in cores], axis=1)
        for r in range(4):
            c = cores[r]
            m = {"x": x_shards[c], "latT": lat_full, "qT": np.asarray(resA[c]["qT"]), "gT": np.asarray(resA[c]["gT"]),
                 "qd": np.asarray(resA[c]["qd"]), "w_ukv": w_ukv_l, "kvg": kvg_l, "w_out": w_out_l, "fg": fg}
            for g in range(3):
                hl = HALO[g]
                kp = np.zeros((512, S_FULL + 2 * hl), dtype=kd_full.dtype)
                kp[:, hl:hl + S_FULL] = kd_full[g]
                vp = np.zeros((S_FULL + 2 * hl, 512), dtype=vd_full.dtype)
                vp[hl:hl + S_FULL] = vd_full[g]
                m[f"kd{g}"] = np.ascontiguousarray(kp[:, r * T: r * T + T + 2 * hl])
                m[f"vd{g}"] = np.ascontiguousarray(vp[r * T: r * T + T + 2 * hl])
            m.update(_get(("cB", r), lambda: _consts_B(r)))
            in_maps.append(m)
    return in_maps


def kernel_unfused(x, norm_g, w_in, q_norm_g, kv_norm_g, w_uq, w_ukv, w_out, final_g):
    x = np.asarray(x, dtype=np.float32)
    xs = [np.ascontiguousarray(x[c // 4, (c % 4) * T:(c % 4 + 1) * T]) for c in range(NCORE)]
    fg = np.ascontiguousarray(np.asarray(final_g, np.float32).reshape(1, D))
    ys = None
    for l in range(DEPTH):
        ng8 = np.ascontiguousarray(np.asarray(norm_g[l], np.float32).reshape(8, 128).T)
        qg3 = np.ascontiguousarray(np.asarray(q_norm_g[l], np.float32).reshape(3, 128).T)
        kvg2 = np.ascontiguousarray(np.asarray(kv_norm_g[l], np.float32).reshape(2, 128).T)
        resA = run_A(xs, np.ascontiguousarray(w_in[l], dtype=np.float32), ng8,
                     np.ascontiguousarray(w_uq[l], dtype=np.float32), qg3)
        in_maps = _exchange(resA, xs, np.ascontiguousarray(w_ukv[l], dtype=np.float32), kvg2,
                            np.ascontiguousarray(w_out[l], dtype=np.float32), fg)
        resB = run_B(in_maps)
        xs = [np.asarray(resB[c]["xo"]) for c in range(NCORE)]
        ys = [np.asarray(resB[c]["yo"]) for c in range(NCORE)]
    out = np.zeros((2, S_FULL, D), np.float32)
    for c in range(NCORE):
        out[c // 4, (c % 4) * T:(c % 4 + 1) * T] = ys[c]
    return out


def _consts_F(rank):
    c = dict(_consts_A(rank))
    c.update(_consts_B(rank))
    eye = np.eye(128, dtype=np.float32)
    selK = np.zeros((128, 8, 128), np.float32)
    selVL = np.zeros((64, 4, 64), np.float32)
    selVR = np.zeros((64, 4, 128), np.float32)
    for j in range(4):
        if j == rank - 1:
            selK[:, j, :] = eye
            selVL[:, j, :] = np.eye(64, dtype=np.float32)
        if j == rank + 1:
            selK[:, 4 + j, :] = eye
            selVR[:, j, 64:128] = np.eye(64, dtype=np.float32)
    c["selK"] = selK.astype(NPBF)
    c["selVL"] = selVL.astype(NPBF)
    c["selVR"] = selVR.astype(NPBF)
    return c


def build_fused(depth=DEPTH, debug=False):
    nc = bass.Bass("TRN2", target_bir_lowering=False)
    dt_in = lambda n, s, d=F32: nc.dram_tensor(n, s, d, kind="ExternalInput").ap()
    x = dt_in("x", [T, D])
    w_in = dt_in("w_in", [DEPTH, D, INW])
    ng = dt_in("ng", [DEPTH, 128, 8])
    w_uq = dt_in("w_uq", [DEPTH, 384, 768])
    qg = dt_in("qg", [DEPTH, 128, 3])
    w_ukv = dt_in("w_ukv", [DEPTH, 256, 1024])
    kvg = dt_in("kvg", [DEPTH, 128, 2])
    w_out = dt_in("w_out", [DEPTH, D, D])
    fg = dt_in("fg", [1, D])
    c96 = dt_in("c96", [96, T])
    s96 = dt_in("s96", [96, T])
    c128 = dt_in("c128", [128, T])
    s128 = dt_in("s128", [128, T])
    r96 = dt_in("r96", [96, 96], BF16)
    r128 = dt_in("r128", [128, 128], BF16)
    ident = dt_in("ident", [128, 128], BF16)
    masks = dt_in("masks", [4, 128, 128], BF16)
    e32 = dt_in("e32", [32, 96], BF16)
    selK = dt_in("selK", [128, 8, 128], BF16)
    selVL = dt_in("selVL", [64, 4, 64], BF16)
    selVR = dt_in("selVR", [64, 4, 128], BF16)
    yo = nc.dram_tensor("yo", [T, D], F32, kind="ExternalOutput").ap()
    mixo = nc.dram_tensor("mixo", [1024, T], BF16, kind="ExternalOutput").ap() if debug else None
    lato = nc.dram_tensor("lato", [128, 3, S_FULL], BF16, kind="ExternalOutput").ap() if debug else None
    xb = [nc.dram_tensor(f"xb{i}", [T, D], F32).ap() for i in range(2)]
    lat_loc = [nc.dram_tensor(f"lat_loc{i}", [288, T], BF16).ap() for i in range(2)]
    lat_allA = [nc.dram_tensor(f"lat_allA{i}", [4 * 256, T], BF16).ap() for i in range(2)]
    lat_allB = [nc.dram_tensor(f"lat_allB{i}", [4 * 32, T], BF16).ap() for i in range(2)]
    pay_h = [pay_tensors(nc, f"pay{i}", 1) for i in range(2)]
    pay_all_h = [pay_tensors(nc, f"pay_all{i}", 4) for i in range(2)]
    qT_s = nc.dram_tensor("qT_s", [8, 96, T], BF16).ap()
    gT_s = nc.dram_tensor("gT_s", [1024, T], BF16).ap()
    qd_s = nc.dram_tensor("qd_s", [3, 512, T], BF16).ap()
    kd_s = nc.dram_tensor("kd_s", [3, 512, T], BF16).ap()
    vd_s = nc.dram_tensor("vd_s", [3, T, 512], BF16).ap()
    groups = [[0, 1, 2, 3], [4, 5, 6, 7]]
    with ExitStack() as es:
        S = Sched(nc, es)
        pb = [es.enter_context(nc.psum_tensor(f"pb{i}", [128, 512], F32)) for i in range(7)]
        ptr = es.enter_context(nc.psum_tensor("ptr", [128, 1024], BF16))
        psA = {"pacc0": pb[0], "pacc1": pb[1], "pacc2": pb[2], "prot0": pb[3], "prot1": pb[4], "pssq": pb[5],
               "ptr": ptr}
        for l in range(depth):
            xin = x if l == 0 else xb[(l - 1) % 2]
            b = l % 2
            def ag(src, dst, key, ci, reads=()):
                S.custom("pool", lambda e, src=src, dst=dst: e.collective_compute(
                    "AllGather", ALU.bypass, replica_groups=groups, ins=[src], outs=[dst]), slot=("cc", ci),
                    reads=list(reads), writes=[key])

            def lat_ags(b=b):
                ag(lat_loc[b][0:256, :], lat_allA[b][:, :], "lat_allA", 0, [("lat_loc", "c", tb) for tb in range(4)])
                ag(lat_loc[b][256:288, :], lat_allB[b][:, :], "lat_allB", 1, [("lat_loc", "r", tb) for tb in range(4)])
            emit_A(nc, xin, w_in[l], ng[l], w_uq[l], qg[l], c96, s96, c128, s128, r96, r128, ident,
                   lat_loc[b], qT_s, gT_s, qd_s, [kd_s[g] for g in range(3)], [vd_s[g] for g in range(3)],
                   S=S, psum=psA, pay=pay_h[b], tag=f"_A{l}", last=False, after_rope=lat_ags)

            def halo_ags(b=b):
                for ci, key in enumerate(pay_h[b]):
                    ag(pay_h[b][key].ap()[:, :], pay_all_h[b][key].ap()[:, :], "pay_all_" + "_".join(map(str, key)), 2 + ci)
            pay_keys = lambda g: (["pay_all_2_%d" % k for k in range(4)] if g == 2 else ["pay_all_%d" % g])
            lastl = (l == depth - 1)
            hook = halo_ags
            if not OVERLAP_EXCHANGE:
                halo_ags()
                S.barrier()
                hook = None
            emit_B(nc, xin, None, qT_s, gT_s, qd_s, kd_s, vd_s, w_ukv[l], kvg[l], w_out[l], fg, masks, ident, e32,
                   xb[l % 2], yo, S=S, psum=pb, tag=f"_B{l}", mixo=(mixo if l == depth - 1 else None),
                   fz={"lat_allA": lat_allA[b], "lat_allB": lat_allB[b], "pay_all": pay_all_h[b], "after_prologue": hook, "lato": lato,
                       "pay_keys": pay_keys, "selK": selK, "selVL": selVL, "selVR": selVR},
                   last=lastl, final=lastl)
    return nc


def kernel(x, norm_g, w_in, q_norm_g, kv_norm_g, w_uq, w_ukv, w_out, final_g):
    f32c = lambda a: np.ascontiguousarray(np.asarray(a, dtype=np.float32))
    x = f32c(x)
    shared = {
        "w_in": f32c(w_in), "w_uq": f32c(w_uq), "w_ukv": f32c(w_ukv), "w_out": f32c(w_out),
        "ng": f32c(np.asarray(norm_g, np.float32).reshape(DEPTH, 8, 128).transpose(0, 2, 1)),
        "qg": f32c(np.asarray(q_norm_g, np.float32).reshape(DEPTH, 3, 128).transpose(0, 2, 1)),
        "kvg": f32c(np.asarray(kv_norm_g, np.float32).reshape(DEPTH, 2, 128).transpose(0, 2, 1)),
        "fg": f32c(np.asarray(final_g, np.float32).reshape(1, D)),
    }
    nc = _get("F", build_fused)
    in_maps = []
    for c in range(NCORE):
        m = dict(shared)
        m["x"] = np.ascontiguousarray(x[c // 4, (c % 4) * T:(c % 4 + 1) * T])
        m.update(_get(("cF", c % 4), lambda: _consts_F(c % 4)))
        in_maps.append(m)
    res = run_bass_kernel_spmd(nc, in_maps, core_ids=list(range(NCORE))).results
    out = np.zeros((2, S_FULL, D), np.float32)
    for c in range(NCORE):
        out[c // 4, (c % 4) * T:(c % 4 + 1) * T] = np.asarray(res[c]["yo"])
    return out
```

```python
import numpy as np
import ml_dtypes
from contextlib import ExitStack
import concourse.bass as bass
import concourse.mybir as mybir
from concourse.bass_utils import run_bass_kernel_spmd

F32 = mybir.dt.float32
BF16 = mybir.dt.bfloat16
AF = mybir.ActivationFunctionType
ALU = mybir.AluOpType
NPBF = ml_dtypes.bfloat16

D = 1024
T = 2048
S_FULL = 8192
NCORE = 8
DEPTH = 4
EPS = 1e-6
INW = 6304
DILS = (1, 4, 16)
HALO = tuple(64 * d for d in DILS)
MLA_SCALE = 96 ** -0.5
DIL_SCALE = 64 ** -0.5
MASKNEG = -32768.0
OVERLAP_EXCHANGE = True


class Sched:
    EP = 30000
    ENGS = ("pe", "act", "dve", "pool", "sp")

    def __init__(self, nc, es):
        self.nc, self.es = nc, es
        self.ops = {e: [] for e in self.ENGS}
        self.nops = {e: 0 for e in self.ENGS}
        self.esems = {}
        self.lastw = {}
        self.readers = {}
        self.waited = {e: {} for e in self.ENGS}
        self.dsem = {}
        self.last_tok = {}
        self.pending_barrier = {e: None for e in self.ENGS}
        self.nsem = 0

    def _newsem(self, name):
        self.nsem += 1
        return self.es.enter_context(self.nc.semaphore(name))

    def _esem(self, e, epoch):
        k = (e, epoch)
        if k not in self.esems:
            self.esems[k] = self._newsem(f"e_{e}_{epoch}")
        return self.esems[k]

    def _waits(self, eng, reads, writes):
        toks = []
        for k in reads:
            w = self.lastw.get(k)
            if w is not None:
                toks.append(w)
        for k in writes:
            w = self.lastw.get(k)
            if w is not None:
                toks.append(w)
            toks.extend(self.readers.get(k, {}).values())
        if self.pending_barrier[eng] is not None:
            toks.extend(self.pending_barrier[eng])
            self.pending_barrier[eng] = None
        need = {}
        for (skey, sem, val, src) in toks:
            if src == eng and eng == "pe":
                continue
            if val > self.waited[eng].get(skey, 0):
                self.waited[eng][skey] = val
                need[skey] = (sem, val)
        return list(need.values())

    def _commit(self, tok, reads, writes):
        for k in writes:
            self.lastw[k] = tok
            self.readers[k] = {}
        for k in reads:
            if k in writes:
                continue
            self.readers.setdefault(k, {})[tok[0]] = tok
        self.last_tok[tok[0]] = tok

    def op(self, eng, fn, reads=(), writes=()):
        waits = self._waits(eng, reads, writes)
        idx = self.nops[eng]
        self.nops[eng] += 1
        epoch, val = idx // self.EP, idx % self.EP + 1
        sem = self._esem(eng, epoch)
        tok = ((eng, epoch), sem, val, eng)
        self.ops[eng].append((waits, fn, sem, 1))
        self._commit(tok, reads, writes)

    def dma(self, q, out, in_, reads=(), writes=(), slot=None):
        waits = self._waits(q, reads, writes)
        if slot not in self.dsem:
            self.dsem[slot] = [self._newsem(f"d_{len(self.dsem)}"), 0]
        ent = self.dsem[slot]
        ent[1] += 16
        tok = (("dma", slot), ent[0], ent[1], "dma")
        self.ops[q].append((waits, lambda e, o=out, i=in_: e.dma_start(out=o, in_=i), ent[0], 16))
        self._commit(tok, reads, writes)

    def custom(self, eng, fn, slot, reads=(), writes=()):
        waits = self._waits(eng, reads, writes)
        if slot not in self.dsem:
            self.dsem[slot] = [self._newsem(f"d_{len(self.dsem)}"), 0]
        ent = self.dsem[slot]
        ent[1] += 1
        tok = (("dma", slot), ent[0], ent[1], "dma")
        self.ops[eng].append((waits, fn, ent[0], 1))
        self._commit(tok, reads, writes)

    def barrier(self):
        toks = list(self.last_tok.values())
        for e in self.ENGS:
            self.pending_barrier[e] = list(toks)

    def wait_all(self, eng="sp"):
        self.barrier()
        waits = self._waits(eng, (), ())
        self.ops[eng].append((waits, None, None, 0))

    def emit(self):
        nc = self.nc
        ops = self.ops
        self.ops = {e: [] for e in self.ENGS}

        def replay(name, e):
            for waits, fn, sem, inc in ops[name]:
                for (s, v) in waits:
                    e.wait_ge(s, v)
                if fn is not None:
                    fn(e).then_inc(sem, inc)

        with nc.Block() as block:
            @block.tensor
            def _(e):
                replay("pe", e)

            @block.scalar
            def _(e):
                replay("act", e)

            @block.vector
            def _(e):
                replay("dve", e)

            @block.gpsimd
            def _(e):
                replay("pool", e)

            @block.sync
            def _(e):
                replay("sp", e)


def pay_tensors(nc, name, mult):
    t = {}
    t[(0,)] = nc.dram_tensor(f"{name}_g0", [mult * 4 * HALO[0], 512], BF16)
    t[(1,)] = nc.dram_tensor(f"{name}_g1", [mult * 4 * HALO[1], 512], BF16)
    for k in range(4):
        t[(2, k)] = nc.dram_tensor(f"{name}_g2_{k}", [mult * HALO[2], 512], BF16)
    return t


def pay_piece(tens, g, kind, side):
    k = kind * 2 + side
    if g == 2:
        return tens[(2, k)], 0, HALO[2]
    return tens[(g,)], k * HALO[g], 4 * HALO[g]


def pay_k(tens, j, g, side, f0, nf, t0, nt, ncand=None):
    halo = HALO[g]
    h, roff, rpr = pay_piece(tens, g, 0, side)
    base = (j * rpr + roff) * 512 + f0 * halo + t0
    if ncand is None:
        return bass.AP(h, base, [[halo, nf], [1, nt]])
    return bass.AP(h, base, [[halo, nf], [rpr * 512, ncand], [1, nt]])


def pay_v(tens, j, g, side, r0, nr):
    h, roff, rpr = pay_piece(tens, g, 1, side)
    base = (j * rpr + roff + r0) * 512
    return bass.AP(h, base, [[512, nr], [1, 512]])


def pay_v_cand(tens, j, g, side, hp, d):
    h, roff, rpr = pay_piece(tens, g, 1, side)
    base = (j * rpr + roff) * 512 + hp * 128
    return bass.AP(h, base, [[d * 512, 64], [512, d], [1, 128]])


def mm_group(lhs_rhs, out):
    def fn(pe):
        n = len(lhs_rhs)
        ins = None
        for i, (l, r) in enumerate(lhs_rhs):
            ins = pe.matmul(out, lhsT=l, rhs=r, start=(i == 0), stop=(i == n - 1))
        return ins
    return fn


def _rope_tables(pos, dim):
    inv = (1.0 / (500000.0 ** (np.arange(0, dim, 2, dtype=np.float32) / np.float32(dim)))).astype(np.float32)
    ang = pos.astype(np.float32)[:, None] * inv[None, :]
    return np.cos(ang).astype(np.float32), np.sin(ang).astype(np.float32)


def _consts_A(rank):
    pos = np.arange(rank * T, (rank + 1) * T)
    cm, sm = _rope_tables(pos, 32)
    cd, sd = _rope_tables(pos, 16)
    c96 = np.ones((96, T), np.float32)
    s96 = np.zeros((96, T), np.float32)
    c96[64:80] = cm.T
    c96[80:96] = cm.T
    s96[64:80] = sm.T
    s96[80:96] = sm.T
    c128 = np.ones((128, T), np.float32)
    s128 = np.zeros((128, T), np.float32)
    for b in (0, 64):
        c128[b:b + 8] = cd.T
        c128[b + 8:b + 16] = cd.T
        s128[b:b + 8] = sd.T
        s128[b + 8:b + 16] = sd.T
    r96 = np.zeros((96, 96), np.float32)
    for i in range(16):
        r96[80 + i, 64 + i] = -1.0
        r96[64 + i, 80 + i] = 1.0
    r128 = np.zeros((128, 128), np.float32)
    for b in (0, 64):
        for i in range(8):
            r128[b + 8 + i, b + i] = -1.0
            r128[b + i, b + 8 + i] = 1.0
    return {
        "c96": c96, "s96": s96, "c128": c128, "s128": s128,
        "r96": r96.astype(NPBF), "r128": r128.astype(NPBF),
        "ident": np.eye(128, dtype=np.float32).astype(NPBF),
    }


def _consts_B(rank):
    n = np.arange(128)[:, None]
    m = np.arange(128)[None, :]
    m0 = np.where(m <= n, 0.0, MASKNEG)
    m1 = np.where(m >= n, 0.0, MASKNEG)
    m0f = m0.copy()
    m1l = m1.copy()
    if rank == 0:
        m0f[:64, :] = MASKNEG
    if rank == 3:
        m1l[64:, :] = MASKNEG
    masks = np.stack([m0, m0f, m1, m1l]).astype(np.float32).astype(NPBF)
    e32 = np.zeros((32, 96), np.float32)
    for i in range(32):
        e32[i, 64 + i] = 1.0
    return {"masks": masks, "ident": np.eye(128, dtype=np.float32).astype(NPBF), "e32": e32.astype(NPBF)}


def build_A():
    nc = bass.Bass("TRN2", target_bir_lowering=False)
    dt_in = lambda n, s, d=F32: nc.dram_tensor(n, s, d, kind="ExternalInput").ap()
    dt_out = lambda n, s, d=BF16: nc.dram_tensor(n, s, d, kind="ExternalOutput").ap()
    x = dt_in("x", [T, D])
    w_in = dt_in("w_in", [D, INW])
    ng = dt_in("ng", [128, 8])
    w_uq = dt_in("w_uq", [384, 768])
    qg = dt_in("qg", [128, 3])
    c96d = dt_in("c96", [96, T])
    s96d = dt_in("s96", [96, T])
    c128d = dt_in("c128", [128, T])
    s128d = dt_in("s128", [128, T])
    r96d = dt_in("r96", [96, 96], BF16)
    r128d = dt_in("r128", [128, 128], BF16)
    identd = dt_in("ident", [128, 128], BF16)
    lat_o = dt_out("lat", [288, T])
    qT_o = dt_out("qT", [8, 96, T])
    gT_o = dt_out("gT", [1024, T])
    qd_o = dt_out("qd", [3, 512, T])
    kd_o = dt_out("kd", [3, 512, T])
    vd_o = dt_out("vd", [3, T, 512])
    emit_A(nc, x, w_in, ng, w_uq, qg, c96d, s96d, c128d, s128d, r96d, r128d, identd,
           lat_o, qT_o, gT_o, qd_o, kd_o, vd_o)
    return nc


def emit_A(nc, x, w_in, ng, w_uq, qg, c96d, s96d, c128d, s128d, r96d, r128d, identd,
           lat_o, qT_o, gT_o, qd_o, kd_o, vd_o, S=None, psum=None, pay=None, tag="", last=True, after_rope=None):
    w_in_v = w_in.rearrange("(c p) n -> p c n", p=128)
    w_uq_v = w_uq.rearrange("(c p) n -> p c n", p=128)
    with ExitStack() as es:
        if S is None:
            S = Sched(nc, es)
        sb = lambda n, s, d=BF16: es.enter_context(nc.sbuf_tensor(n + tag, s, d))
        if psum is None:
            ps = lambda n, s, d=F32: es.enter_context(nc.psum_tensor(n, s, d))
        else:
            ps = lambda n, s, d=F32: psum[n]
        hT = sb("hT", [128, 8, T])
        c96 = sb("c96s", [96, T], F32)
        s96 = sb("s96s", [96, T], F32)
        c128 = sb("c128s", [128, T], F32)
        s128 = sb("s128s", [128, T], F32)
        r96 = sb("r96s", [96, 96])
        r128 = sb("r128s", [128, 128])
        ident = sb("idents", [128, 128])
        ones = sb("ones", [128, 128])
        ng8 = sb("ng8", [128, 8], F32)
        qg3 = sb("qg3", [128, 3], F32)
        xin = [sb(f"xin{i}", [128, D], F32) for i in range(2)]
        junk = sb("junk", [128, D], F32)
        hb = [sb(f"hb{i}", [128, D]) for i in range(2)]
        ssq = sb("ssq", [128, 16], F32)
        epst = sb("epst", [128, 1], F32)
        rstd = sb("rstd", [128, 16], F32)
        wf = [sb(f"wf{i}", [128, 8, 512], F32) for i in range(2)]
        wb = [sb(f"wb{i}", [128, 8, 512]) for i in range(2)]
        wkr = sb("wkr", [128, 8, 96])
        wuqf = sb("wuqf", [128, 3, 768], F32)
        wuqb = sb("wuqb", [128, 3, 768])
        cq_b = sb("cq_b", [128, 3, 512])
        sq_b = sb("sq_b", [128, 3, 512])
        cqn = sb("cqn", [128, 3, 512])
        rbc = sb("rbc", [128, 512], F32)
        qh = [sb(f"qh{i}", [96, 512]) for i in range(3)]
        t1 = [sb(f"t1_{i}", [128, 512], F32) for i in range(2)]
        t2 = [sb(f"t2_{i}", [128, 512], F32) for i in range(2)]
        pbf = [sb(f"pbf{i}", [128, 512]) for i in range(3)]
        ob = [sb(f"ob{i}", [128, 512]) for i in range(3)]
        pacc = [ps(f"pacc{i}", [128, 512]) for i in range(3)]
        prot = [ps(f"prot{i}", [128, 512]) for i in range(2)]
        pssq = ps("pssq", [128, 512])
        ptr = ps("ptr", [128, 1024], BF16)

        for i, (dst, src) in enumerate([(c96, c96d), (s96, s96d), (c128, c128d), (s128, s128d),
                                        (r96, r96d), (r128, r128d), (ident, identd), (ng8, ng), (qg3, qg)]):
            S.dma("sp", dst[:], src, writes=[dst.name], slot=("c", i))
        S.op("pool", lambda e: e.memset(ones[:], 1.0), writes=["ones"])
        S.op("pool", lambda e: e.memset(wkr[:], 0.0), writes=[wkr.name])
        S.op("pool", lambda e: e.memset(ssq[:], 0.0), writes=["ssq"])
        S.op("pool", lambda e: e.memset(epst[:], EPS), writes=["epst"])
        S.dma("sp", wuqf[:], w_uq_v, writes=["wuqf"], slot=("wuq",))
        S.op("pool", lambda e: e.tensor_tensor(out=wuqb[:], in0=wuqf[:],
                                               in1=qg3[:, :].unsqueeze(2).broadcast_to([128, 3, 768]), op=ALU.mult),
             reads=["wuqf", qg3.name], writes=["wuqb"])

        for t in range(16):
            xs, hbs = xin[t % 2], hb[t % 2]
            S.dma("sp", xs[:], x[t * 128:(t + 1) * 128, :], writes=[xs.name], slot=("xin", t % 2))
            S.op("act", lambda e, xs=xs, t=t: e.activation(out=junk[:], in_=xs[:], func=AF.Square,
                                                           accum_out=ssq[:, t:t + 1]),
                 reads=[xs.name, "ssq"], writes=["junk", ("ssq", t)])
            S.op("act", lambda e, t=t: e.activation(out=rstd[:, t:t + 1], in_=ssq[:, t:t + 1], func=AF.Sqrt,
                                                    scale=1.0 / D, bias=epst[:, 0:1]),
                 reads=[("ssq", t), "epst"], writes=[("rstd", t)])
            S.op("dve", lambda e, t=t: e.reciprocal(out=rstd[:, t:t + 1], in_=rstd[:, t:t + 1]),
                 reads=[("rstd", t)], writes=[("rstd", t)])
            S.op("dve", lambda e, xs=xs, hbs=hbs, t=t: e.tensor_scalar(out=hbs[:], in0=xs[:], scalar1=rstd[:, t:t + 1],
                                                                       scalar2=None, op0=ALU.mult),
                 reads=[xs.name, ("rstd", t)], writes=[hbs.name])

            def tr(pe, hbs=hbs):
                ins = None
                for c in range(8):
                    ins = pe.transpose(out=ptr[:, c * 128:(c + 1) * 128], in_=hbs[:, c * 128:(c + 1) * 128],
                                       identity=ident[:])
                return ins
            S.op("pe", tr, reads=[hbs.name, ident.name], writes=["ptr"])
            S.op("act", lambda e, t=t: e.activation(out=hT[:, :, t * 128:(t + 1) * 128],
                                                    in_=ptr[:, :].rearrange("p (c n) -> p c n", c=8), func=AF.Copy),
                 reads=["ptr"], writes=[("hT", t // 4)])

        state = {"wg": 0, "pa": 0, "pr": 0, "ob": 0, "pb": 0, "qh": 0}

        def load_w(col0, ncols, kr=False):
            i = state["wg"] % 2
            state["wg"] += 1
            S.dma("sp", wf[i][:, :, 0:ncols], w_in_v[:, :, col0:col0 + ncols], writes=[wf[i].name], slot=("wf", i))
            def cast(e):
                ins = None
                for c in range(8):
                    ins = e.activation(out=wb[i][:, c, 0:ncols], in_=wf[i][:, c, 0:ncols], func=AF.Copy,
                                       scale=ng8[:, c:c + 1])
                return ins
            S.op("act", cast, reads=[wf[i].name, ng8.name], writes=[wb[i].name])
            return wb[i]

        def nxt(k, n):
            v = state[k] % n
            state[k] += 1
            return v

        hTk = [("hT", i) for i in range(4)]

        def proj_fm(wt, c0, m, tb, out_ps):
            S.op("pe", mm_group([(wt[:, c, c0:c0 + m], hT[:, c, tb * 512:(tb + 1) * 512]) for c in range(8)],
                                out_ps[0:m, :]),
                 reads=[wt.name, ("hT", tb)], writes=[out_ps.name])

        def rope_store(src_bf, rows, rT, cs, sn, tb, dst_ap, p0=0, wkey=None):
            i = nxt("pr", 2)
            pr = prot[i]
            S.op("pe", mm_group([(rT[0:rows, 0:rows], src_bf[0:rows, :])], pr[0:rows, :]),
                 reads=[src_bf.name, rT.name], writes=[pr.name])
            a, b = t1[i], t2[i]
            tok = slice(tb * 512, (tb + 1) * 512)
            S.op("pool", lambda e: e.tensor_tensor(out=a[p0:rows, :], in0=src_bf[p0:rows, :], in1=cs[p0:rows, tok],
                                                   op=ALU.mult),
                 reads=[src_bf.name, cs.name], writes=[a.name])
            S.op("dve", lambda e: e.tensor_tensor(out=b[p0:rows, :], in0=pr[p0:rows, :], in1=sn[p0:rows, tok],
                                                  op=ALU.mult),
                 reads=[pr.name, sn.name], writes=[b.name])
            S.op("dve", lambda e: e.tensor_tensor(out=src_bf[p0:rows, :], in0=a[p0:rows, :], in1=b[p0:rows, :],
                                                  op=ALU.add),
                 reads=[a.name, b.name], writes=[src_bf.name])
            S.dma("sp", dst_ap, src_bf[dst_rows(p0, rows, dst_ap), :], reads=[src_bf.name],
                  writes=([wkey] if wkey is not None else []), slot=("st", src_bf.name))

        def dst_rows(p0, rows, dst_ap):
            n = dst_ap.shape[0]
            return slice(rows - n, rows)

        def latent_norm(nchunk, width, tb):
            S.op("pe", mm_group([(ones[:, :], sq_b[:, c, :]) for c in range(nchunk)], pssq[:, :]),
                 reads=["sq_b", "ones"], writes=["pssq"])
            S.op("act", lambda e: e.activation(out=rbc[:], in_=pssq[:], func=AF.Sqrt, scale=1.0 / width,
                                               bias=epst[:, 0:1]), reads=["pssq", "epst"], writes=["rbc"])
            S.op("dve", lambda e: e.reciprocal(out=rbc[:], in_=rbc[:]), reads=["rbc"], writes=["rbc"])
            for c in range(nchunk):
                S.op("dve", lambda e, c=c: e.tensor_tensor(out=cqn[:, c, :], in0=cq_b[:, c, :], in1=rbc[:], op=ALU.mult),
                     reads=["cq_b", "rbc"], writes=["cqn"])

        def latent_chunks(wt, nchunk, tb):
            for c3 in range(nchunk):
                pa = pacc[nxt("pa", 3)]
                proj_fm(wt, c3 * 128, 128, tb, pa)
                S.op("act", lambda e, pa=pa, c3=c3: e.activation(out=cq_b[:, c3, :], in_=pa[:], func=AF.Copy),
                     reads=[pa.name], writes=["cq_b"])
                S.op("act", lambda e, pa=pa, c3=c3: e.activation(out=sq_b[:, c3, :], in_=pa[:], func=AF.Square),
                     reads=[pa.name], writes=["sq_b"])

        pend = {"f": None}

        def defer(fn):
            prev = pend["f"]
            pend["f"] = fn
            if prev is not None:
                prev()

        def flush():
            prev = pend["f"]
            pend["f"] = None
            if prev is not None:
                prev()

        def unit_cq(wt):
            for tb in range(4):
                latent_chunks(wt, 3, tb)
                latent_norm(3, 384, tb)
                for h in range(8):
                    pa = pacc[nxt("pa", 3)]
                    q = qh[nxt("qh", 3)]
                    S.op("pe", mm_group([(wuqb[:, c, h * 96:(h + 1) * 96], cqn[:, c, :]) for c in range(3)], pa[0:96, :]),
                         reads=["wuqb", "cqn"], writes=[pa.name])
                    S.op("act", lambda e, pa=pa, q=q: e.activation(out=q[:], in_=pa[0:96, :], func=AF.Copy),
                         reads=[pa.name], writes=[q.name])
                    defer(lambda q=q, tb=tb, h=h: rope_store(q, 96, r96, c96, s96, tb,
                                                             qT_o[h, :, tb * 512:(tb + 1) * 512], p0=64))
                flush()

        def unit_ckv(wt):
            S.op("pool", lambda e: e.tensor_copy(out=wkr[:, :, 64:96], in_=wt[:, :, 256:288]), reads=[wt.name],
                 writes=[wkr.name])
            lat_v = lat_o[0:256, :].rearrange("(c p) t -> p c t", p=128)
            for tb in range(4):
                latent_chunks(wt, 2, tb)
                latent_norm(2, 256, tb)
                S.dma("sp", lat_v[:, :, tb * 512:(tb + 1) * 512], cqn[:, 0:2, :], reads=["cqn"],
                      writes=[("lat_loc", "c", tb)], slot=("st", "cqn"))
                pa = pacc[nxt("pa", 3)]
                q = qh[nxt("qh", 3)]
                proj_fm(wkr, 0, 96, tb, pa)
                S.op("act", lambda e, pa=pa, q=q: e.activation(out=q[:], in_=pa[0:96, :], func=AF.Copy),
                     reads=[pa.name], writes=[q.name])
                defer(lambda q=q, tb=tb: rope_store(q, 96, r96, c96, s96, tb,
                                                    lat_o[256:288, tb * 512:(tb + 1) * 512], p0=64,
                                                    wkey=("lat_loc", "r", tb)))
            flush()

        def gate_blocks(wt, row0):
            for blk in range(4):
                for tb in range(4):
                    pa = pacc[nxt("pa", 3)]
                    o = ob[nxt("ob", 3)]
                    proj_fm(wt, blk * 128, 128, tb, pa)
                    S.op("act", lambda e, pa=pa, o=o: e.activation(out=o[:], in_=pa[:], func=AF.Silu),
                         reads=[pa.name], writes=[o.name])
                    S.dma("sp", gT_o[row0 + blk * 128: row0 + (blk + 1) * 128, tb * 512:(tb + 1) * 512], o[:],
                          reads=[o.name], slot=("st", o.name))

        def rope_post(p, blk, tb, dst, g):
            rope_store(p, 128, r128, c128, s128, tb, dst[blk * 128:(blk + 1) * 128, tb * 512:(tb + 1) * 512])
            if pay is not None and g is not None:
                halo = HALO[g]
                if tb * 512 < halo:
                    n = min(halo - tb * 512, 512)
                    S.dma("sp", pay_k(pay, 0, g, 0, blk * 128, 128, tb * 512, n), p[:, 0:n], reads=[p.name],
                          slot=("st", p.name))
                lo, hi = max(tb * 512, T - halo), tb * 512 + 512
                if lo < hi:
                    S.dma("sp", pay_k(pay, 0, g, 1, blk * 128, 128, lo - (T - halo), hi - lo),
                          p[:, lo - tb * 512:512], reads=[p.name], slot=("st", p.name))

        def rope_blocks(wt, dst, g=None):
            for blk in range(4):
                for tb in range(4):
                    pa = pacc[nxt("pa", 3)]
                    p = pbf[nxt("pb", 3)]
                    proj_fm(wt, blk * 128, 128, tb, pa)
                    S.op("act", lambda e, pa=pa, p=p: e.activation(out=p[:], in_=pa[:], func=AF.Copy),
                         reads=[pa.name], writes=[p.name])
                    defer(lambda p=p, blk=blk, tb=tb: rope_post(p, blk, tb, dst, g))
            flush()

        def v_blocks(wt, dst, g=None):
            for t in range(16):
                pa = pacc[nxt("pa", 3)]
                o = ob[nxt("ob", 3)]
                S.op("pe", mm_group([(hT[:, c, t * 128:(t + 1) * 128], wt[:, c, 0:512]) for c in range(8)], pa[:, :]),
                     reads=[wt.name, ("hT", t // 4)], writes=[pa.name])
                S.op("act", lambda e, pa=pa, o=o: e.activation(out=o[:], in_=pa[:], func=AF.Copy),
                     reads=[pa.name], writes=[o.name])
                S.dma("sp", dst[t * 128:(t + 1) * 128, :], o[:], reads=[o.name], slot=("st", o.name))
                if pay is not None and g is not None:
                    halo = HALO[g]
                    if t * 128 < halo:
                        n = min(halo - t * 128, 128)
                        S.dma("sp", pay_v(pay, 0, g, 0, t * 128, n), o[0:n, :], reads=[o.name], slot=("st", o.name))
                    lo, hi = max(t * 128, T - halo), t * 128 + 128
                    if lo < hi:
                        S.dma("sp", pay_v(pay, 0, g, 1, lo - (T - halo), hi - lo), o[lo - t * 128:128, :],
                              reads=[o.name], slot=("st", o.name))

        units = [(0, 384, unit_cq), (384, 288, unit_ckv)]
        for g in range(3):
            base = 1184 + g * 1536
            units.append((base, 512, lambda wt, g=g: rope_blocks(wt, qd_o[g])))
            units.append((base + 512, 512, lambda wt, g=g: rope_blocks(wt, kd_o[g], g)))
        n_rope_units = len(units)
        units.append((672, 512, lambda wt: gate_blocks(wt, 0)))
        for g in range(3):
            base = 1184 + g * 1536
            units.append((base + 1024, 512, lambda wt, g=g: v_blocks(wt, vd_o[g], g)))
        units.append((5792, 512, lambda wt: gate_blocks(wt, 512)))
        wt = load_w(units[0][0], units[0][1])
        for ui, (c0, ncols, fn) in enumerate(units):
            wnext = load_w(units[ui + 1][0], units[ui + 1][1]) if ui + 1 < len(units) else None
            fn(wt)
            wt = wnext
            if ui == n_rope_units - 1 and after_rope is not None:
                after_rope()
        if last:
            S.wait_all("sp")
        else:
            S.barrier()
        S.emit()


_CACHE = {}


def _get(name, builder):
    if name not in _CACHE:
        _CACHE[name] = builder()
    return _CACHE[name]


def run_A(x_shards, w_in_l, ng_l, w_uq_l, qg_l):
    nc = _get("A", build_A)
    in_maps = []
    for c in range(NCORE):
        m = {"x": x_shards[c], "w_in": w_in_l, "ng": ng_l, "w_uq": w_uq_l, "qg": qg_l}
        m.update(_get(("cA", c % 4), lambda: _consts_A(c % 4)))
        in_maps.append(m)
    res = run_bass_kernel_spmd(nc, in_maps, core_ids=list(range(NCORE)))
    return res.results


def build_B(mla_heads=tuple(range(8)), dil_hps=tuple(range(4)), debug=False, dil_groups=(0, 1, 2)):
    nc = bass.Bass("TRN2", target_bir_lowering=False)
    dt_in = lambda n, s, d=F32: nc.dram_tensor(n, s, d, kind="ExternalInput").ap()
    dt_out = lambda n, s, d=F32: nc.dram_tensor(n, s, d, kind="ExternalOutput").ap()
    a = dict(
        x=dt_in("x", [T, D]),
        latT=dt_in("latT", [288, S_FULL], BF16),
        qT=dt_in("qT", [8, 96, T], BF16),
        gT=dt_in("gT", [1024, T], BF16),
        qd=dt_in("qd", [3, 512, T], BF16),
        kd=[dt_in(f"kd{g}", [512, T + 2 * HALO[g]], BF16) for g in range(3)],
        vd=[dt_in(f"vd{g}", [T + 2 * HALO[g], 512], BF16) for g in range(3)],
        w_ukv=dt_in("w_ukv", [256, 1024]),
        kvg=dt_in("kvg", [128, 2]),
        w_out=dt_in("w_out", [D, D]),
        fg=dt_in("fg", [1, D]),
        masks=dt_in("masks", [4, 128, 128], BF16),
        ident=dt_in("ident", [128, 128], BF16),
        e32=dt_in("e32", [32, 96], BF16),
        xo=dt_out("xo", [T, D]),
        yo=dt_out("yo", [T, D]),
    )
    if debug:
        a["mixo"] = nc.dram_tensor("mixo", [1024, T], BF16, kind="ExternalOutput").ap()
    emit_B(nc, mla_heads=mla_heads, dil_hps=dil_hps, dil_groups=dil_groups, **a)
    return nc


def emit_B(nc, x, latT, qT, gT, qd, kd, vd, w_ukv, kvg, w_out, fg, masks, ident, e32, xo, yo,
           mla_heads=tuple(range(8)), dil_hps=tuple(range(4)), mixo=None, dil_groups=(0, 1, 2),
           S=None, psum=None, tag="", fz=None, last=True, final=True):
    with ExitStack() as es:
        if S is None:
            S = Sched(nc, es)
        sb = lambda n, s, d=BF16: es.enter_context(nc.sbuf_tensor(n + tag, s, d))
        if psum is None:
            pb = [es.enter_context(nc.psum_tensor(f"pb{i}", [128, 512], F32)) for i in range(8)]
        else:
            pb = psum
        pS, pAcc, pK = pb[0:3], pb[3:5], pb[5:7]
        if fz is not None:
            selK = sb("selK", [128, 8, 128])
            selVL = sb("selVL", [64, 4, 64])
            selVR = sb("selVR", [64, 4, 128])
            S.dma("sp", selK[:], fz["selK"], writes=["selK"], slot=("c", 4))
            S.dma("sp", selVL[:], fz["selVL"], writes=["selVL"], slot=("c", 5))
            S.dma("sp", selVR[:], fz["selVR"], writes=["selVR"], slot=("c", 6))
        mixedT = sb("mixedT", [128, 8, T])
        identS = sb("identS", [128, 128])
        e32S = sb("e32S", [32, 96])
        maskS = sb("maskS", [128, 4, 128])
        onesf = sb("onesf", [128, 64], F32)
        kvg2 = sb("kvg2", [128, 2], F32)
        epst = sb("epst", [128, 1], F32)
        rd = sb("rd", [128, 512], F32)
        tmp = [sb(f"tmp{i}", [128, 512], F32) for i in range(2)]
        gts = [sb(f"gts{i}", [128, 512]) for i in range(2)]
        S.dma("sp", identS[:], ident, writes=["identS"], slot=("c", 0))
        S.dma("sp", e32S[:], e32, writes=["e32S"], slot=("c", 1))
        S.dma("sp", maskS[:], masks.rearrange("k n m -> n k m"), writes=["maskS"], slot=("c", 2))
        S.dma("sp", kvg2[:], kvg, writes=["kvg2"], slot=("c", 3))
        S.op("pool", lambda e: e.memset(onesf[:], 1.0), writes=["onesf"])
        onesb = sb("onesb", [128, 2])
        S.op("pool", lambda e: e.memset(onesb[:], 1.0), writes=["onesb"])
        S.op("pool", lambda e: e.memset(epst[:], EPS), writes=["epst"])
        m01 = sb("m01", [128, 4, 128])
        m01c = sb("m01c", [128, 4, 2, 128])
        S.op("dve", lambda e: e.tensor_scalar(out=m01[:], in0=maskS[:], scalar1=1.0 / 32768.0, scalar2=1.0,
                                              op0=ALU.mult, op1=ALU.add), reads=["maskS"], writes=["m01"])
        for ci in range(4):
            h0 = 1 if (ci & 1) else 0
            h1 = 3 if (ci & 2) else 2
            S.op("pool", lambda e, ci=ci, h0=h0: e.tensor_copy(out=m01c[:, ci, 0, :], in_=m01[:, h0, :]),
                 reads=["m01"], writes=["m01c"])
            S.op("pool", lambda e, ci=ci, h1=h1: e.tensor_copy(out=m01c[:, ci, 1, :], in_=m01[:, h1, :]),
                 reads=["m01"], writes=["m01c"])
        st = {"tmp": 0, "gts": 0, "acc": 0, "pk": 0}

        def nxt(k, n):
            v = st[k] % n
            st[k] += 1
            return v

        def get_pk():
            return pK[0] if st.get("pk_single") else pK[nxt("pk", 2)]

        def normalize_gate(src_o, src_den, gate_ap, gate_reads, po, chunk, qb, src_reads):
            tok = slice(qb * 512, (qb + 1) * 512)
            pk = get_pk()
            tm = tmp[nxt("tmp", 2)]
            S.op("act", lambda e: e.activation(out=rd[64:65, :], in_=src_den, func=AF.Ln), reads=src_reads, writes=["rd"])
            S.op("act", lambda e: e.activation(out=rd[64:65, :], in_=rd[64:65, :], func=AF.Exp, scale=-1.0),
                 reads=["rd"], writes=["rd"])
            S.op("pe", mm_group([(onesf[64:65, 0:64], rd[64:65, :])], pk[0:64, :]), reads=["rd", "onesf"],
                 writes=[pk.name])
            S.op("dve", lambda e: e.tensor_tensor(out=tm[po:po + 64, :], in0=src_o, in1=pk[0:64, :], op=ALU.mult),
                 reads=src_reads + [pk.name], writes=[tm.name])
            S.op("dve", lambda e: e.tensor_tensor(out=mixedT[po:po + 64, chunk, tok], in0=tm[po:po + 64, :],
                                                  in1=gate_ap, op=ALU.mult),
                 reads=[tm.name] + gate_reads, writes=[("mixedT", chunk)])

        with ExitStack() as es1:
            sb1 = lambda n, s, d=BF16: es1.enter_context(nc.sbuf_tensor(n + tag, s, d))
            lat = sb1("lat", [128, 3, S_FULL])
            KT = sb1("KT", [96, S_FULL])
            V = sb1("V", [128, 64, 65])
            Qh = [sb1(f"Qh{i}", [96, T]) for i in range(2)]
            PT = [sb1(f"PT{i}", [128, 512]) for i in range(3)]
            osb = sb1("osb", [64, 512], F32)
            wukvf = sb1("wukvf", [128, 2, 1024], F32)
            wk96 = sb1("wk96", [128, 2, 8, 96])
            wv = sb1("wv", [128, 2, 8, 64])
            S.dma("sp", wukvf[:], w_ukv.rearrange("(c p) n -> p c n", p=128), writes=["wukvf"], slot=("wukv",))
            S.op("pool", lambda e: e.memset(wk96[:], 0.0), writes=["wk96"])
            S.op("pool", lambda e: e.memset(V[:], 1.0), writes=["V"])
            for c in range(2):
                wsrc = wukvf[:, c, :].rearrange("p (h k) -> p h k", h=8)
                S.op("pool", lambda e, c=c, wsrc=wsrc: e.tensor_tensor(
                    out=wk96[:, c, :, 0:64], in0=wsrc[:, :, 0:64],
                    in1=kvg2[:, c:c + 1].unsqueeze(2).broadcast_to([128, 8, 64]), op=ALU.mult),
                    reads=["wukvf", "kvg2"], writes=["wk96"])
                S.op("pool", lambda e, c=c, wsrc=wsrc: e.tensor_tensor(
                    out=wv[:, c, :, :], in0=wsrc[:, :, 64:128],
                    in1=kvg2[:, c:c + 1].unsqueeze(2).broadcast_to([128, 8, 64]), op=ALU.mult),
                    reads=["wukvf", "kvg2"], writes=["wv"])
            if fz is not None and fz.get("after_prologue") is not None:
                fz["after_prologue"]()
            for i in range(4):
                sl = slice(i * 2048, (i + 1) * 2048)
                if fz is None:
                    lat_v = latT[0:256, :].rearrange("(c p) t -> p c t", p=128)
                    S.dma("sp", lat[:, 0:2, sl], lat_v[:, :, sl], writes=[("lat", i)], slot=("lat", i))
                    S.dma("sp", lat[0:32, 2, sl], latT[256:288, sl], writes=[("latr", i)], slot=("latr", i))
                else:
                    S.dma("sp", lat[:, 0:2, sl],
                          fz["lat_allA"][i * 256:(i + 1) * 256, :].rearrange("(c p) t -> p c t", p=128),
                          reads=["lat_allA"], writes=[("lat", i)], slot=("lat", i))
                    S.dma("sp", lat[0:32, 2, sl], fz["lat_allB"][i * 32:(i + 1) * 32, :], reads=["lat_allB"],
                          writes=[("latr", i)], slot=("latr", i))

            for h in mla_heads:
                po, chunk = (h % 2) * 64, h // 2
                Q = Qh[h % 2]
                S.dma("sp", Q[:], qT[h], writes=[Q.name], slot=("Q", h % 2))
                for tb in range(16):
                    pk = pK[nxt("pk", 2)]
                    sl = slice(tb * 512, (tb + 1) * 512)
                    S.op("pe", mm_group([(wk96[:, 0, h, :], lat[:, 0, sl]), (wk96[:, 1, h, :], lat[:, 1, sl]),
                                         (e32S[0:32, :], lat[0:32, 2, sl])], pk[0:96, :]),
                         reads=["wk96", "e32S", ("lat", tb // 4), ("latr", tb // 4)], writes=[pk.name])
                    eng = "dve" if tb % 2 else "act"
                    if eng == "act":
                        S.op("act", lambda e, pk=pk, sl=sl: e.activation(out=KT[:, sl], in_=pk[0:96, :], func=AF.Copy),
                             reads=[pk.name], writes=[("KT", tb)])
                    else:
                        S.op("dve", lambda e, pk=pk, sl=sl: e.tensor_copy(out=KT[:, sl], in_=pk[0:96, :]),
                             reads=[pk.name], writes=[("KT", tb)])
                for k8 in range(8):
                    pk = pK[nxt("pk", 2)]

                    def vb(pe, pk=pk, k8=k8, h=h):
                        ins = None
                        for j in range(8):
                            kt = k8 * 8 + j
                            for c in range(2):
                                ins = pe.matmul(pk[:, j * 64:(j + 1) * 64], lhsT=lat[:, c, kt * 128:(kt + 1) * 128],
                                                rhs=wv[:, c, h, :], start=(c == 0), stop=(c == 1))
                        return ins
                    S.op("pe", vb, reads=["wv", ("lat", k8 // 2)], writes=[pk.name])
                    S.op("dve", lambda e, pk=pk, k8=k8: e.tensor_copy(
                        out=V[:, k8 * 8:(k8 + 1) * 8, 0:64], in_=pk[:, :].rearrange("p (j d) -> p j d", j=8)),
                        reads=[pk.name], writes=[("V", k8)])
                steps = [(qb, kt) for qb in range(4) for kt in range(64)]
                accs, gidx = {}, {}
                for qb in range(4):
                    accs[qb] = pAcc[nxt("acc", 2)]
                base = st.get("sstep", 0)

                def qk(n):
                    qb, kt = steps[n]
                    s_ = pS[(base + n) % 3]
                    S.op("pe", mm_group([(KT[:, kt * 128:(kt + 1) * 128], Q[:, qb * 512:(qb + 1) * 512])], s_[:, :]),
                         reads=[("KT", kt // 4), Q.name], writes=[s_.name])
                qk(0)
                qk(1)
                for n, (qb, kt) in enumerate(steps):
                    s_, p_ = pS[(base + n) % 3], PT[(base + n) % 3]
                    acc = accs[qb]
                    if kt == 0:
                        g_i = nxt("gts", 2)
                        gidx[qb] = g_i
                        S.dma("sp", gts[g_i][po:po + 64, :], gT[h * 64:(h + 1) * 64, qb * 512:(qb + 1) * 512],
                              writes=[gts[g_i].name], slot=("gts", g_i))
                    S.op("act", lambda e, s_=s_, p_=p_: e.activation(out=p_[:], in_=s_[:], func=AF.Exp,
                                                                     scale=MLA_SCALE),
                         reads=[s_.name], writes=[p_.name])
                    if n + 2 < len(steps):
                        qk(n + 2)
                    S.op("pe", lambda pe, kt=kt, p_=p_, acc=acc: pe.matmul(
                        acc[0:65, :], lhsT=V[:, kt, :], rhs=p_[:], start=(kt == 0), stop=(kt == 63)),
                        reads=[p_.name, ("V", kt // 8)], writes=[acc.name])
                    if kt == 63:
                        g_i = gidx[qb]
                        S.op("act", lambda e, acc=acc: e.activation(out=osb[:], in_=acc[0:64, :], func=AF.Copy),
                             reads=[acc.name], writes=["osb"])
                        normalize_gate(osb[0:64, :], acc[64:65, :], gts[g_i][po:po + 64, :], [gts[g_i].name], po, chunk,
                                       qb, ["osb", acc.name])
                st["sstep"] = base + len(steps)
            if fz is not None and fz.get("lato") is not None:
                S.dma("sp", fz["lato"], lat[:], reads=[("lat", i) for i in range(4)] + [("latr", i) for i in range(4)],
                      slot=("lato",))
            S.barrier()
            S.emit()

        with ExitStack() as es2:
            sb2 = lambda n, s, d=BF16: es2.enter_context(nc.sbuf_tensor(n + tag, s, d))
            Kg = [sb2(f"Kg{i}", [128, T + 2 * HALO[2]]) for i in range(2)]
            Qg = [sb2(f"Qg{i}", [128, T]) for i in range(2)]
            Vraw = [sb2(f"Vraw{i}", [128, 32, 128]) for i in range(2)]
            accT = sb2("accT", [65, 2, T], F32)
            gtb = sb2("gtb", [128, T])
            P2p = [sb2(f"P2p{i}", [128, 4, 128]) for i in range(4)]
            Sbanks = [pS[0], pS[1], pS[2], pK[1]]
            st["pk_single"] = True
            if fz is not None:
                candK = [sb2(f"candK{i}", [128, 4, HALO[2]]) for i in range(2)]
                candV = [sb2(f"candV{i}", [64, 4, 16, 128]) for i in range(2)]
            jobs = [(hp, g) for hp in dil_hps for g in dil_groups]

            jinfo = {}

            sel_ops = {}

            def stage(n):
                sel = []
                hp, g = jobs[n]
                d, halo = DILS[g], HALO[g]
                W = T + 2 * halo
                nbr = 16 // d + 1
                njb = 16 // d
                bi = n % 2
                kg, qg_, vr = Kg[bi], Qg[bi], Vraw[bi]
                kkeys, vkeys = [(kg.name, "own")], []
                S.dma("sp", qg_[:], qd[g, hp * 128:(hp + 1) * 128, :], writes=[qg_.name], slot=("qg", bi))
                cols = slice(hp * 128, (hp + 1) * 128)
                if fz is None:
                    S.dma("sp", kg[:, 0:W], kd[g][hp * 128:(hp + 1) * 128, :], writes=[(kg.name, "own")], slot=("kg", bi))
                    vsrc = vd[g].rearrange("(kb n r) c -> n r kb c", n=128, r=d)
                    for r in range(d):
                        vkeys.append((vr.name, "o", r, 0))
                        S.dma("sp", vr[:, r * nbr:(r + 1) * nbr, :], vsrc[:, r, :, cols],
                              writes=[vkeys[-1]], slot=("vr", bi))
                else:
                    pall = fz["pay_all"]
                    pkeys = fz["pay_keys"](g)
                    S.dma("sp", kg[:, halo:halo + T], kd[g][hp * 128:(hp + 1) * 128, :], writes=[(kg.name, "own")],
                          slot=("kg", bi))
                    vown = vd[g].rearrange("(kb two n r) c -> two n r kb c", two=2, n=64, r=d)
                    for r in range(d):
                        vkeys.append((vr.name, "o", r, 0))
                        S.dma("sp", vr[64:128, r * nbr:r * nbr + njb, :], vown[0][:, r, :, cols],
                              writes=[vkeys[-1]], slot=("vr", bi))
                        vkeys.append((vr.name, "o", r, 1))
                        S.dma("sp", vr[0:64, r * nbr + 1:r * nbr + 1 + njb, :], vown[1][:, r, :, cols],
                              writes=[vkeys[-1]], slot=("vr", bi))
                    for side in range(2):
                        piece = 1 - side
                        ck = candK[side]
                        S.dma("sp", ck[:, :, 0:halo], pay_k(pall, 0, g, piece, hp * 128, 128, 0, halo, ncand=4),
                              reads=pkeys, writes=[ck.name], slot=("candK", side))
                        cv = candV[side]
                        cvkeys = [(cv.name, j) for j in range(4)]
                        for j in range(4):
                            S.dma("sp", cv[:, j, 0:d, :], pay_v_cand(pall, j, g, piece, hp, d),
                                  reads=pkeys, writes=[cvkeys[j]], slot=("candV", side))
                        dst0 = 0 if side == 0 else halo + T
                        for c0 in range(0, halo, 512):
                            nn = min(512, halo - c0)
                            kkeys.append((kg.name, "h", side, c0))

                            def selk(side=side, ck=ck, c0=c0, nn=nn, dst0=dst0, kg=kg, key=kkeys[-1]):
                                pk = get_pk()
                                S.op("pe", mm_group([(selK[:, side * 4 + j, :], ck[:, j, c0:c0 + nn]) for j in range(4)],
                                                    pk[:, 0:nn]), reads=["selK", ck.name], writes=[pk.name])
                                S.op("act", lambda e: e.activation(out=kg[:, dst0 + c0:dst0 + c0 + nn], in_=pk[:, 0:nn],
                                                                   func=AF.Copy), reads=[pk.name], writes=[key])
                            sel.append(selk)
                        for r0 in range(0, d, 4):
                            nr = min(4, d - r0)
                            if side == 0:
                                lhs = [selVL[:, j, :] for j in range(4)]
                                prow, kbs, mrows = slice(0, 64), 0, 64
                            else:
                                lhs = [selVR[:, j, :] for j in range(4)]
                                prow, kbs, mrows = slice(64, 128), nbr - 1, 128
                            vkeys.append((vr.name, "h", side, r0))

                            def selv(lhs=lhs, cv=cv, r0=r0, nr=nr, prow=prow, kbs=kbs, mrows=mrows, vr=vr, nbr=nbr,
                                     cvkeys=cvkeys, key=vkeys[-1]):
                                pk = get_pk()
                                S.op("pe", mm_group([(lhs[j], cv[:, j, r0:r0 + nr, :]) for j in range(4)],
                                                    pk[0:mrows, 0:nr * 128]), reads=["selVL", "selVR"] + cvkeys,
                                     writes=[pk.name])
                                S.op("dve", lambda e: e.tensor_copy(
                                    out=vr[prow, r0 * nbr + kbs:(r0 + nr - 1) * nbr + kbs + 1:nbr, :],
                                    in_=pk[prow, 0:nr * 128].rearrange("p (r c) -> p r c", r=nr)),
                                    reads=[pk.name], writes=[key])
                            sel.append(selv)
                jinfo[n] = (kkeys, vkeys)
                sel_ops[n] = sel

            def run_sel(n):
                for f in sel_ops.pop(n, []):
                    f()

            def compute(n, mid=None):
                hp, g = jobs[n]
                d = DILS[g]
                nbr = 16 // d + 1
                njb = 16 // d
                bi = n % 2
                kg, qg_, vr = Kg[bi], Qg[bi], Vraw[bi]
                kkeys, vkeys = jinfo[n]
                first = (g == dil_groups[0])
                pbase = st.get("pbase", 0)
                accv = [accT[:, hl, :].rearrange("p (jb m r) -> p r jb m", r=d, m=128) for hl in range(2)]
                tiles = []
                for hl in range(2):
                    if d == 1:
                        quads = [[(0, 4 * q + j) for j in range(4)] for q in range(4)]
                        dsts = [accv[hl][:, 0, 4 * q:4 * q + 4, :] for q in range(4)]
                    elif d == 4:
                        quads = [[(r, j) for j in range(4)] for r in range(4)]
                        dsts = [accv[hl][:, r, 0:4, :] for r in range(4)]
                    else:
                        quads = [[(4 * q + j, 0) for j in range(4)] for q in range(4)]
                        dsts = [accv[hl][:, 4 * q:4 * q + 4, 0, :] for q in range(4)]
                    for quad, dst in zip(quads, dsts):
                        for j, (r, jb) in enumerate(quad):
                            tiles.append((hl, r, jb, j, dst))

                pairs = [(tiles[2 * k], tiles[2 * k + 1]) for k in range(len(tiles) // 2)]
                LA = 3

                def qkpair(p):
                    bank = Sbanks[(pbase + p) % 4]
                    mms = []
                    for t, (hl, r, jb, j, dst) in enumerate(pairs[p]):
                        po = hl * 64
                        for half in range(2):
                            kb = jb + half
                            k0 = r + d * kb * 128
                            q0 = r + d * jb * 128
                            mms.append((bank[:, (2 * t + half) * 128:(2 * t + half + 1) * 128],
                                        kg[po:po + 64, k0:k0 + d * 127 + 1:d], qg_[po:po + 64, q0:q0 + d * 127 + 1:d]))

                    def fn(pe):
                        ins = None
                        for o_, kap, qap in mms:
                            ins = pe.matmul(o_, lhsT=kap, rhs=qap, start=True, stop=True)
                        return ins
                    S.op("pe", fn, reads=kkeys + [qg_.name], writes=[bank.name])
                for p in range(min(LA, len(pairs))):
                    qkpair(p)
                for p, pr_ in enumerate(pairs):
                    bank, pp = Sbanks[(pbase + p) % 4], P2p[(pbase + p) % 4]
                    S.op("act", lambda e, bank=bank, pp=pp: e.activation(
                        out=pp[:, :, :], in_=bank[:, :].rearrange("p (a m) -> p a m", a=4), func=AF.Exp,
                        scale=DIL_SCALE), reads=[bank.name], writes=[(pp.name, 0), (pp.name, 1)])
                    for t, (hl, r, jb, j, dst) in enumerate(pr_):
                        ci = (1 if jb == 0 else 0) + (2 if jb == njb - 1 else 0)
                        S.op("dve", lambda e, pp=pp, ci=ci, t=t: e.tensor_tensor(
                            out=pp[:, 2 * t:2 * t + 2, :], in0=pp[:, 2 * t:2 * t + 2, :], in1=m01c[:, ci, :, :],
                            op=ALU.mult), reads=[(pp.name, t), "m01c"], writes=[(pp.name, t)])
                    if p + LA < len(pairs):
                        qkpair(p + LA)
                    if mid is not None and p == len(pairs) // 2:
                        mid()
                    hl0, j1 = pr_[0][0], pr_[1][3]
                    pv = pAcc[((2 * p) // 4) % 2]

                    def pvf(pe, pr_=pr_, pp=pp, pv=pv, vr=vr, nbr=nbr):
                        ins = None
                        for t, (hl, r, jb, j, dst) in enumerate(pr_):
                            for half in range(2):
                                pe.matmul(pv[0:64, j * 128:(j + 1) * 128],
                                          lhsT=vr[:, r * nbr + jb + half, hl * 64:(hl + 1) * 64],
                                          rhs=pp[:, 2 * t + half, :], start=(half == 0), stop=(half == 1))
                            for half in range(2):
                                ins = pe.matmul(pv[64:65, j * 128:(j + 1) * 128], lhsT=onesb[:, 0:1],
                                                rhs=pp[:, 2 * t + half, :], start=(half == 0), stop=(half == 1))
                        return ins
                    S.op("pe", pvf, reads=[(pp.name, 0), (pp.name, 1), "onesb"] + vkeys, writes=[pv.name])
                    if j1 == 3:
                        dst = pr_[1][4]
                        src = pv[0:65, :].rearrange("p (a m) -> p a m", a=4)
                        if first:
                            S.op("dve", lambda e, dst=dst, src=src: e.tensor_copy(out=dst, in_=src),
                                 reads=[pv.name], writes=[("accT", hl0)])
                        else:
                            S.op("dve", lambda e, dst=dst, src=src: e.tensor_tensor(out=dst, in0=src, in1=dst,
                                                                                    op=ALU.add),
                                 reads=[pv.name, ("accT", hl0)], writes=[("accT", hl0)])
                st["pbase"] = pbase + len(pairs)

            if jobs:
                stage(0)
                run_sel(0)
            for n, (hp, g) in enumerate(jobs):
                if g == dil_groups[0]:
                    S.dma("sp", gtb[:], gT[512 + hp * 128: 512 + (hp + 1) * 128, :], writes=["gtb"], slot=("gtb",))
                if n + 1 < len(jobs):
                    stage(n + 1)
                compute(n, mid=(lambda n=n: run_sel(n + 1)))
                if g == dil_groups[-1]:
                    for hl in range(2):
                        po = hl * 64
                        for qb in range(4):
                            tok = slice(qb * 512, (qb + 1) * 512)
                            normalize_gate(accT[0:64, hl, tok], accT[64:65, hl, tok], gtb[po:po + 64, tok], ["gtb"], po,
                                           4 + hp, qb, [("accT", hl)])
            S.barrier()
            S.emit()

        with ExitStack() as es3:
            sb3 = lambda n, s, d=BF16: es3.enter_context(nc.sbuf_tensor(n + tag, s, d))
            wof = sb3("wof", [128, 8, 512], F32)
            wob = sb3("wob", [128, 8, D])
            fgb = sb3("fgb", [128, D], F32)
            xin = [sb3(f"xin{i}", [128, D], F32) for i in range(2)]
            xn = [sb3(f"xn{i}", [128, D], F32) for i in range(2)]
            yv = [sb3(f"yv{i}", [128, D], F32) for i in range(2)]
            junk = sb3("junk", [128, D], F32)
            ssq = sb3("ssq", [128, 16], F32)
            rstd = sb3("rstd", [128, 16], F32)
            S.op("pool", lambda e: e.memset(ssq[:], 0.0), writes=["ssq"])
            S.dma("sp", fgb[:], fg.partition_broadcast(128), writes=["fgb"], slot=("fgb",))
            w_out_v = w_out.rearrange("(c p) n -> p c n", p=128)
            if mixo is not None:
                S.dma("pool", mixo.rearrange("(c p) t -> p c t", p=128), mixedT[:], reads=[("mixedT", c) for c in range(8)],
                      slot=("mixo",))
            for nb in range(2):
                S.dma("sp", wof[:], w_out_v[:, :, nb * 512:(nb + 1) * 512], writes=["wof"], slot=("wof",))
                S.op("pool", lambda e, nb=nb: e.tensor_copy(out=wob[:, :, nb * 512:(nb + 1) * 512], in_=wof[:]),
                     reads=["wof"], writes=["wob"])
            for t in range(16):
                xs, xnn, yy = xin[t % 2], xn[t % 2], yv[t % 2]
                S.dma("sp", xs[:], x[t * 128:(t + 1) * 128, :], writes=[xs.name], slot=("xin", t % 2))
                for nb in range(2):
                    pa = pS[(2 * t + nb) % 3]
                    S.op("pe", mm_group([(mixedT[:, c, t * 128:(t + 1) * 128], wob[:, c, nb * 512:(nb + 1) * 512])
                                         for c in range(8)], pa[:, :]),
                         reads=["wob"] + [("mixedT", c) for c in range(8)], writes=[pa.name])
                    S.op("dve", lambda e, pa=pa, nb=nb, xs=xs, xnn=xnn: e.tensor_tensor(
                        out=xnn[:, nb * 512:(nb + 1) * 512], in0=pa[:, :], in1=xs[:, nb * 512:(nb + 1) * 512], op=ALU.add),
                        reads=[pa.name, xs.name], writes=[xnn.name])
                S.dma("pool", xo[t * 128:(t + 1) * 128, :], xnn[:], reads=[xnn.name], slot=("xo", t % 2))
                if not final:
                    continue
                S.op("act", lambda e, xnn=xnn, t=t: e.activation(out=junk[:], in_=xnn[:], func=AF.Square,
                                                                 accum_out=ssq[:, t:t + 1]),
                     reads=[xnn.name, "ssq"], writes=["junk", ("ssq", t)])
                S.op("act", lambda e, t=t: e.activation(out=rstd[:, t:t + 1], in_=ssq[:, t:t + 1], func=AF.Sqrt,
                                                        scale=1.0 / D, bias=epst[:, 0:1]),
                     reads=[("ssq", t), "epst"], writes=[("rstd", t)])
                S.op("dve", lambda e, t=t: e.reciprocal(out=rstd[:, t:t + 1], in_=rstd[:, t:t + 1]),
                     reads=[("rstd", t)], writes=[("rstd", t)])
                S.op("dve", lambda e, xnn=xnn, yy=yy, t=t: e.scalar_tensor_tensor(
                    out=yy[:], in0=xnn[:], scalar=rstd[:, t:t + 1], in1=fgb[:], op0=ALU.mult, op1=ALU.mult),
                    reads=[xnn.name, ("rstd", t), "fgb"], writes=[yy.name])
                S.dma("pool", yo[t * 128:(t + 1) * 128, :], yy[:], reads=[yy.name], slot=("yo", t % 2))
            if last:
                S.wait_all("sp")
            else:
                S.barrier()
            S.emit()


def run_B(in_maps):
    nc = _get("B", build_B)
    res = run_bass_kernel_spmd(nc, in_maps, core_ids=list(range(NCORE)))
    return res.results


def _exchange(resA, x_shards, w_ukv_l, kvg_l, w_out_l, fg):
    in_maps = []
    for b in range(2):
        cores = [4 * b + r for r in range(4)]
        lat_full = np.concatenate([np.asarray(resA[c]["lat"]) for c in cores], axis=1)
        kd_full = np.concatenate([np.asarray(resA[c]["kd"]) for c in cores], axis=2)
        vd_full = np.concatenate([np.asarray(resA[c]["vd"]) for c in cores], axis=1)
        for r in range(4):
            c = cores[r]
            m = {"x": x_shards[c], "latT": lat_full, "qT": np.asarray(resA[c]["qT"]), "gT": np.asarray(resA[c]["gT"]),
                 "qd": np.asarray(resA[c]["qd"]), "w_ukv": w_ukv_l, "kvg": kvg_l, "w_out": w_out_l, "fg": fg}
            for g in range(3):
                hl = HALO[g]
                kp = np.zeros((512, S_FULL + 2 * hl), dtype=kd_full.dtype)
                kp[:, hl:hl + S_FULL] = kd_full[g]
                vp = np.zeros((S_FULL + 2 * hl, 512), dtype=vd_full.dtype)
                vp[hl:hl + S_FULL] = vd_full[g]
                m[f"kd{g}"] = np.ascontiguousarray(kp[:, r * T: r * T + T + 2 * hl])
                m[f"vd{g}"] = np.ascontiguousarray(vp[r * T: r * T + T + 2 * hl])
            m.update(_get(("cB", r), lambda: _consts_B(r)))
            in_maps.append(m)
    return in_maps


def kernel_unfused(x, norm_g, w_in, q_norm_g, kv_norm_g, w_uq, w_ukv, w_out, final_g):
    x = np.asarray(x, dtype=np.float32)
    xs = [np.ascontiguousarray(x[c // 4, (c % 4) * T:(c % 4 + 1) * T]) for c in range(NCORE)]
    fg = np.ascontiguousarray(np.asarray(final_g, np.float32).reshape(1, D))
    ys = None
    for l in range(DEPTH):
        ng8 = np.ascontiguousarray(np.asarray(norm_g[l], np.float32).reshape(8, 128).T)
        qg3 = np.ascontiguousarray(np.asarray(q_norm_g[l], np.float32).reshape(3, 128).T)
        kvg2 = np.ascontiguousarray(np.asarray(kv_norm_g[l], np.float32).reshape(2, 128).T)
        resA = run_A(xs, np.ascontiguousarray(w_in[l], dtype=np.float32), ng8,
                     np.ascontiguousarray(w_uq[l], dtype=np.float32), qg3)
        in_maps = _exchange(resA, xs, np.ascontiguousarray(w_ukv[l], dtype=np.float32), kvg2,
                            np.ascontiguousarray(w_out[l], dtype=np.float32), fg)
        resB = run_B(in_maps)
        xs = [np.asarray(resB[c]["xo"]) for c in range(NCORE)]
        ys = [np.asarray(resB[c]["yo"]) for c in range(NCORE)]
    out = np.zeros((2, S_FULL, D), np.float32)
    for c in range(NCORE):
        out[c // 4, (c % 4) * T:(c % 4 + 1) * T] = ys[c]
    return out


def _consts_F(rank):
    c = dict(_consts_A(rank))
    c.update(_consts_B(rank))
    eye = np.eye(128, dtype=np.float32)
    selK = np.zeros((128, 8, 128), np.float32)
    selVL = np.zeros((64, 4, 64), np.float32)
    selVR = np.zeros((64, 4, 128), np.float32)
    for j in range(4):
        if j == rank - 1:
            selK[:, j, :] = eye
            selVL[:, j, :] = np.eye(64, dtype=np.float32)
        if j == rank + 1:
            selK[:, 4 + j, :] = eye
            selVR[:, j, 64:128] = np.eye(64, dtype=np.float32)
    c["selK"] = selK.astype(NPBF)
    c["selVL"] = selVL.astype(NPBF)
    c["selVR"] = selVR.astype(NPBF)
    return c


def build_fused(depth=DEPTH, debug=False):
    nc = bass.Bass("TRN2", target_bir_lowering=False)
    dt_in = lambda n, s, d=F32: nc.dram_tensor(n, s, d, kind="ExternalInput").ap()
    x = dt_in("x", [T, D])
    w_in = dt_in("w_in", [DEPTH, D, INW])
    ng = dt_in("ng", [DEPTH, 128, 8])
    w_uq = dt_in("w_uq", [DEPTH, 384, 768])
    qg = dt_in("qg", [DEPTH, 128, 3])
    w_ukv = dt_in("w_ukv", [DEPTH, 256, 1024])
    kvg = dt_in("kvg", [DEPTH, 128, 2])
    w_out = dt_in("w_out", [DEPTH, D, D])
    fg = dt_in("fg", [1, D])
    c96 = dt_in("c96", [96, T])
    s96 = dt_in("s96", [96, T])
    c128 = dt_in("c128", [128, T])
    s128 = dt_in("s128", [128, T])
    r96 = dt_in("r96", [96, 96], BF16)
    r128 = dt_in("r128", [128, 128], BF16)
    ident = dt_in("ident", [128, 128], BF16)
    masks = dt_in("masks", [4, 128, 128], BF16)
    e32 = dt_in("e32", [32, 96], BF16)
    selK = dt_in("selK", [128, 8, 128], BF16)
    selVL = dt_in("selVL", [64, 4, 64], BF16)
    selVR = dt_in("selVR", [64, 4, 128], BF16)
    yo = nc.dram_tensor("yo", [T, D], F32, kind="ExternalOutput").ap()
    mixo = nc.dram_tensor("mixo", [1024, T], BF16, kind="ExternalOutput").ap() if debug else None
    lato = nc.dram_tensor("lato", [128, 3, S_FULL], BF16, kind="ExternalOutput").ap() if debug else None
    xb = [nc.dram_tensor(f"xb{i}", [T, D], F32).ap() for i in range(2)]
    lat_loc = [nc.dram_tensor(f"lat_loc{i}", [288, T], BF16).ap() for i in range(2)]
    lat_allA = [nc.dram_tensor(f"lat_allA{i}", [4 * 256, T], BF16).ap() for i in range(2)]
    lat_allB = [nc.dram_tensor(f"lat_allB{i}", [4 * 32, T], BF16).ap() for i in range(2)]
    pay_h = [pay_tensors(nc, f"pay{i}", 1) for i in range(2)]
    pay_all_h = [pay_tensors(nc, f"pay_all{i}", 4) for i in range(2)]
    qT_s = nc.dram_tensor("qT_s", [8, 96, T], BF16).ap()
    gT_s = nc.dram_tensor("gT_s", [1024, T], BF16).ap()
    qd_s = nc.dram_tensor("qd_s", [3, 512, T], BF16).ap()
    kd_s = nc.dram_tensor("kd_s", [3, 512, T], BF16).ap()
    vd_s = nc.dram_tensor("vd_s", [3, T, 512], BF16).ap()
    groups = [[0, 1, 2, 3], [4, 5, 6, 7]]
    with ExitStack() as es:
        S = Sched(nc, es)
        pb = [es.enter_context(nc.psum_tensor(f"pb{i}", [128, 512], F32)) for i in range(7)]
        ptr = es.enter_context(nc.psum_tensor("ptr", [128, 1024], BF16))
        psA = {"pacc0": pb[0], "pacc1": pb[1], "pacc2": pb[2], "prot0": pb[3], "prot1": pb[4], "pssq": pb[5],
               "ptr": ptr}
        for l in range(depth):
            xin = x if l == 0 else xb[(l - 1) % 2]
            b = l % 2
            def ag(src, dst, key, ci, reads=()):
                S.custom("pool", lambda e, src=src, dst=dst: e.collective_compute(
                    "AllGather", ALU.bypass, replica_groups=groups, ins=[src], outs=[dst]), slot=("cc", ci),
                    reads=list(reads), writes=[key])

            def lat_ags(b=b):
                ag(lat_loc[b][0:256, :], lat_allA[b][:, :], "lat_allA", 0, [("lat_loc", "c", tb) for tb in range(4)])
                ag(lat_loc[b][256:288, :], lat_allB[b][:, :], "lat_allB", 1, [("lat_loc", "r", tb) for tb in range(4)])
            emit_A(nc, xin, w_in[l], ng[l], w_uq[l], qg[l], c96, s96, c128, s128, r96, r128, ident,
                   lat_loc[b], qT_s, gT_s, qd_s, [kd_s[g] for g in range(3)], [vd_s[g] for g in range(3)],
                   S=S, psum=psA, pay=pay_h[b], tag=f"_A{l}", last=False, after_rope=lat_ags)

            def halo_ags(b=b):
                for ci, key in enumerate(pay_h[b]):
                    ag(pay_h[b][key].ap()[:, :], pay_all_h[b][key].ap()[:, :], "pay_all_" + "_".join(map(str, key)), 2 + ci)
            pay_keys = lambda g: (["pay_all_2_%d" % k for k in range(4)] if g == 2 else ["pay_all_%d" % g])
            lastl = (l == depth - 1)
            hook = halo_ags
            if not OVERLAP_EXCHANGE:
                halo_ags()
                S.barrier()
                hook = None
            emit_B(nc, xin, None, qT_s, gT_s, qd_s, kd_s, vd_s, w_ukv[l], kvg[l], w_out[l], fg, masks, ident, e32,
                   xb[l % 2], yo, S=S, psum=pb, tag=f"_B{l}", mixo=(mixo if l == depth - 1 else None),
                   fz={"lat_allA": lat_allA[b], "lat_allB": lat_allB[b], "pay_all": pay_all_h[b], "after_prologue": hook, "lato": lato,
                       "pay_keys": pay_keys, "selK": selK, "selVL": selVL, "selVR": selVR},
                   last=lastl, final=lastl)
    return nc


def kernel(x, norm_g, w_in, q_norm_g, kv_norm_g, w_uq, w_ukv, w_out, final_g):
    f32c = lambda a: np.ascontiguousarray(np.asarray(a, dtype=np.float32))
    x = f32c(x)
    shared = {
        "w_in": f32c(w_in), "w_uq": f32c(w_uq), "w_ukv": f32c(w_ukv), "w_out": f32c(w_out),
        "ng": f32c(np.asarray(norm_g, np.float32).reshape(DEPTH, 8, 128).transpose(0, 2, 1)),
        "qg": f32c(np.asarray(q_norm_g, np.float32).reshape(DEPTH, 3, 128).transpose(0, 2, 1)),
        "kvg": f32c(np.asarray(kv_norm_g, np.float32).reshape(DEPTH, 2, 128).transpose(0, 2, 1)),
        "fg": f32c(np.asarray(final_g, np.float32).reshape(1, D)),
    }
    nc = _get("F", build_fused)
    in_maps = []
    for c in range(NCORE):
        m = dict(shared)
        m["x"] = np.ascontiguousarray(x[c // 4, (c % 4) * T:(c % 4 + 1) * T])
        m.update(_get(("cF", c % 4), lambda: _consts_F(c % 4)))
        in_maps.append(m)
    res = run_bass_kernel_spmd(nc, in_maps, core_ids=list(range(NCORE))).results
    out = np.zeros((2, S_FULL, D), np.float32)
    for c in range(NCORE):
        out[c // 4, (c % 4) * T:(c % 4 + 1) * T] = np.asarray(res[c]["yo"])
    return out
```

```python
import numpy as np
import ml_dtypes
from contextlib import ExitStack
import concourse.bass as bass
import concourse.mybir as mybir
from concourse.bass_utils import run_bass_kernel_spmd

F32 = mybir.dt.float32
BF16 = mybir.dt.bfloat16
AF = mybir.ActivationFunctionType
ALU = mybir.AluOpType
NPBF = ml_dtypes.bfloat16

D = 1024
T = 2048
S_FULL = 8192
NCORE = 8
DEPTH = 4
EPS = 1e-6
INW = 6304
DILS = (1, 4, 16)
HALO = tuple(64 * d for d in DILS)
MLA_SCALE = 96 ** -0.5
DIL_SCALE = 64 ** -0.5
MASKNEG = -32768.0
OVERLAP_EXCHANGE = True


class Sched:
    EP = 30000
    ENGS = ("pe", "act", "dve", "pool", "sp")

    def __init__(self, nc, es):
        self.nc, self.es = nc, es
        self.ops = {e: [] for e in self.ENGS}
        self.nops = {e: 0 for e in self.ENGS}
        self.esems = {}
        self.lastw = {}
        self.readers = {}
        self.waited = {e: {} for e in self.ENGS}
        self.dsem = {}
        self.last_tok = {}
        self.pending_barrier = {e: None for e in self.ENGS}
        self.nsem = 0

    def _newsem(self, name):
        self.nsem += 1
        return self.es.enter_context(self.nc.semaphore(name))

    def _esem(self, e, epoch):
        k = (e, epoch)
        if k not in self.esems:
            self.esems[k] = self._newsem(f"e_{e}_{epoch}")
        return self.esems[k]

    def _waits(self, eng, reads, writes):
        toks = []
        for k in reads:
            w = self.lastw.get(k)
            if w is not None:
                toks.append(w)
        for k in writes:
            w = self.lastw.get(k)
            if w is not None:
                toks.append(w)
            toks.extend(self.readers.get(k, {}).values())
        if self.pending_barrier[eng] is not None:
            toks.extend(self.pending_barrier[eng])
            self.pending_barrier[eng] = None
        need = {}
        for (skey, sem, val, src) in toks:
            if src == eng and eng == "pe":
                continue
            if val > self.waited[eng].get(skey, 0):
                self.waited[eng][skey] = val
                need[skey] = (sem, val)
        return list(need.values())

    def _commit(self, tok, reads, writes):
        for k in writes:
            self.lastw[k] = tok
            self.readers[k] = {}
        for k in reads:
            if k in writes:
                continue
            self.readers.setdefault(k, {})[tok[0]] = tok
        self.last_tok[tok[0]] = tok

    def op(self, eng, fn, reads=(), writes=()):
        waits = self._waits(eng, reads, writes)
        idx = self.nops[eng]
        self.nops[eng] += 1
        epoch, val = idx // self.EP, idx % self.EP + 1
        sem = self._esem(eng, epoch)
        tok = ((eng, epoch), sem, val, eng)
        self.ops[eng].append((waits, fn, sem, 1))
        self._commit(tok, reads, writes)

    def dma(self, q, out, in_, reads=(), writes=(), slot=None):
        waits = self._waits(q, reads, writes)
        if slot not in self.dsem:
            self.dsem[slot] = [self._newsem(f"d_{len(self.dsem)}"), 0]
        ent = self.dsem[slot]
        ent[1] += 16
        tok = (("dma", slot), ent[0], ent[1], "dma")
        self.ops[q].append((waits, lambda e, o=out, i=in_: e.dma_start(out=o, in_=i), ent[0], 16))
        self._commit(tok, reads, writes)

    def custom(self, eng, fn, slot, reads=(), writes=()):
        waits = self._waits(eng, reads, writes)
        if slot not in self.dsem:
            self.dsem[slot] = [self._newsem(f"d_{len(self.dsem)}"), 0]
        ent = self.dsem[slot]
        ent[1] += 1
        tok = (("dma", slot), ent[0], ent[1], "dma")
        self.ops[eng].append((waits, fn, ent[0], 1))
        self._commit(tok, reads, writes)

    def barrier(self):
        toks = list(self.last_tok.values())
        for e in self.ENGS:
            self.pending_barrier[e] = list(toks)

    def wait_all(self, eng="sp"):
        self.barrier()
        waits = self._waits(eng, (), ())
        self.ops[eng].append((waits, None, None, 0))

    def emit(self):
        nc = self.nc
        ops = self.ops
        self.ops = {e: [] for e in self.ENGS}

        def replay(name, e):
            for waits, fn, sem, inc in ops[name]:
                for (s, v) in waits:
                    e.wait_ge(s, v)
                if fn is not None:
                    fn(e).then_inc(sem, inc)

        with nc.Block() as block:
            @block.tensor
            def _(e):
                replay("pe", e)

            @block.scalar
            def _(e):
                replay("act", e)

            @block.vector
            def _(e):
                replay("dve", e)

            @block.gpsimd
            def _(e):
                replay("pool", e)

            @block.sync
            def _(e):
                replay("sp", e)


def pay_tensors(nc, name, mult):
    t = {}
    t[(0,)] = nc.dram_tensor(f"{name}_g0", [mult * 4 * HALO[0], 512], BF16)
    t[(1,)] = nc.dram_tensor(f"{name}_g1", [mult * 4 * HALO[1], 512], BF16)
    for k in range(4):
        t[(2, k)] = nc.dram_tensor(f"{name}_g2_{k}", [mult * HALO[2], 512], BF16)
    return t


def pay_piece(tens, g, kind, side):
    k = kind * 2 + side
    if g == 2:
        return tens[(2, k)], 0, HALO[2]
    return tens[(g,)], k * HALO[g], 4 * HALO[g]


def pay_k(tens, j, g, side, f0, nf, t0, nt, ncand=None):
    halo = HALO[g]
    h, roff, rpr = pay_piece(tens, g, 0, side)
    base = (j * rpr + roff) * 512 + f0 * halo + t0
    if ncand is None:
        return bass.AP(h, base, [[halo, nf], [1, nt]])
    return bass.AP(h, base, [[halo, nf], [rpr * 512, ncand], [1, nt]])


def pay_v(tens, j, g, side, r0, nr):
    h, roff, rpr = pay_piece(tens, g, 1, side)
    base = (j * rpr + roff + r0) * 512
    return bass.AP(h, base, [[512, nr], [1, 512]])


def pay_v_cand(tens, j, g, side, hp, d):
    h, roff, rpr = pay_piece(tens, g, 1, side)
    base = (j * rpr + roff) * 512 + hp * 128
    return bass.AP(h, base, [[d * 512, 64], [512, d], [1, 128]])


def mm_group(lhs_rhs, out):
    def fn(pe):
        n = len(lhs_rhs)
        ins = None
        for i, (l, r) in enumerate(lhs_rhs):
            ins = pe.matmul(out, lhsT=l, rhs=r, start=(i == 0), stop=(i == n - 1))
        return ins
    return fn


def _rope_tables(pos, dim):
    inv = (1.0 / (500000.0 ** (np.arange(0, dim, 2, dtype=np.float32) / np.float32(dim)))).astype(np.float32)
    ang = pos.astype(np.float32)[:, None] * inv[None, :]
    return np.cos(ang).astype(np.float32), np.sin(ang).astype(np.float32)


def _consts_A(rank):
    pos = np.arange(rank * T, (rank + 1) * T)
    cm, sm = _rope_tables(pos, 32)
    cd, sd = _rope_tables(pos, 16)
    c96 = np.ones((96, T), np.float32)
    s96 = np.zeros((96, T), np.float32)
    c96[64:80] = cm.T
    c96[80:96] = cm.T
    s96[64:80] = sm.T
    s96[80:96] = sm.T
    c128 = np.ones((128, T), np.float32)
    s128 = np.zeros((128, T), np.float32)
    for b in (0, 64):
        c128[b:b + 8] = cd.T
        c128[b + 8:b + 16] = cd.T
        s128[b:b + 8] = sd.T
        s128[b + 8:b + 16] = sd.T
    r96 = np.zeros((96, 96), np.float32)
    for i in range(16):
        r96[80 + i, 64 + i] = -1.0
        r96[64 + i, 80 + i] = 1.0
    r128 = np.zeros((128, 128), np.float32)
    for b in (0, 64):
        for i in range(8):
            r128[b + 8 + i, b + i] = -1.0
            r128[b + i, b + 8 + i] = 1.0
    return {
        "c96": c96, "s96": s96, "c128": c128, "s128": s128,
        "r96": r96.astype(NPBF), "r128": r128.astype(NPBF),
        "ident": np.eye(128, dtype=np.float32).astype(NPBF),
    }


def _consts_B(rank):
    n = np.arange(128)[:, None]
    m = np.arange(128)[None, :]
    m0 = np.where(m <= n, 0.0, MASKNEG)
    m1 = np.where(m >= n, 0.0, MASKNEG)
    m0f = m0.copy()
    m1l = m1.copy()
    if rank == 0:
        m0f[:64, :] = MASKNEG
    if rank == 3:
        m1l[64:, :] = MASKNEG
    masks = np.stack([m0, m0f, m1, m1l]).astype(np.float32).astype(NPBF)
    e32 = np.zeros((32, 96), np.float32)
    for i in range(32):
        e32[i, 64 + i] = 1.0
    return {"masks": masks, "ident": np.eye(128, dtype=np.float32).astype(NPBF), "e32": e32.astype(NPBF)}


def build_A():
    nc = bass.Bass("TRN2", target_bir_lowering=False)
    dt_in = lambda n, s, d=F32: nc.dram_tensor(n, s, d, kind="ExternalInput").ap()
    dt_out = lambda n, s, d=BF16: nc.dram_tensor(n, s, d, kind="ExternalOutput").ap()
    x = dt_in("x", [T, D])
    w_in = dt_in("w_in", [D, INW])
    ng = dt_in("ng", [128, 8])
    w_uq = dt_in("w_uq", [384, 768])
    qg = dt_in("qg", [128, 3])
    c96d = dt_in("c96", [96, T])
    s96d = dt_in("s96", [96, T])
    c128d = dt_in("c128", [128, T])
    s128d = dt_in("s128", [128, T])
    r96d = dt_in("r96", [96, 96], BF16)
    r128d = dt_in("r128", [128, 128], BF16)
    identd = dt_in("ident", [128, 128], BF16)
    lat_o = dt_out("lat", [288, T])
    qT_o = dt_out("qT", [8, 96, T])
    gT_o = dt_out("gT", [1024, T])
    qd_o = dt_out("qd", [3, 512, T])
    kd_o = dt_out("kd", [3, 512, T])
    vd_o = dt_out("vd", [3, T, 512])
    emit_A(nc, x, w_in, ng, w_uq, qg, c96d, s96d, c128d, s128d, r96d, r128d, identd,
           lat_o, qT_o, gT_o, qd_o, kd_o, vd_o)
    return nc


def emit_A(nc, x, w_in, ng, w_uq, qg, c96d, s96d, c128d, s128d, r96d, r128d, identd,
           lat_o, qT_o, gT_o, qd_o, kd_o, vd_o, S=None, psum=None, pay=None, tag="", last=True, after_rope=None):
    w_in_v = w_in.rearrange("(c p) n -> p c n", p=128)
    w_uq_v = w_uq.rearrange("(c p) n -> p c n", p=128)
    with ExitStack() as es:
        if S is None:
            S = Sched(nc, es)
        sb = lambda n, s, d=BF16: es.enter_context(nc.sbuf_tensor(n + tag, s, d))
        if psum is None:
            ps = lambda n, s, d=F32: es.enter_context(nc.psum_tensor(n, s, d))
        else:
            ps = lambda n, s, d=F32: psum[n]
        hT = sb("hT", [128, 8, T])
        c96 = sb("c96s", [96, T], F32)
        s96 = sb("s96s", [96, T], F32)
        c128 = sb("c128s", [128, T], F32)
        s128 = sb("s128s", [128, T], F32)
        r96 = sb("r96s", [96, 96])
        r128 = sb("r128s", [128, 128])
        ident = sb("idents", [128, 128])
        ones = sb("ones", [128, 128])
        ng8 = sb("ng8", [128, 8], F32)
        qg3 = sb("qg3", [128, 3], F32)
        xin = [sb(f"xin{i}", [128, D], F32) for i in range(2)]
        junk = sb("junk", [128, D], F32)
        hb = [sb(f"hb{i}", [128, D]) for i in range(2)]
        ssq = sb("ssq", [128, 16], F32)
        epst = sb("epst", [128, 1], F32)
        rstd = sb("rstd", [128, 16], F32)
        wf = [sb(f"wf{i}", [128, 8, 512], F32) for i in range(3)]
        wb = [sb(f"wb{i}", [128, 8, 512]) for i in range(2)]
        wkr = sb("wkr", [128, 8, 96])
        wuqf = sb("wuqf", [128, 3, 768], F32)
        wuqb = sb("wuqb", [128, 3, 768])
        cq_b = sb("cq_b", [128, 3, 512])
        sq_b = sb("sq_b", [128, 3, 512])
        cqn = sb("cqn", [128, 3, 512])
        rbc = sb("rbc", [128, 512], F32)
        qh = [sb(f"qh{i}", [96, 512]) for i in range(3)]
        t1 = [sb(f"t1_{i}", [128, 512], F32) for i in range(2)]
        t2 = [sb(f"t2_{i}", [128, 512], F32) for i in range(2)]
        pbf = [sb(f"pbf{i}", [128, 512]) for i in range(3)]
        ob = [sb(f"ob{i}", [128, 512]) for i in range(3)]
        pacc = [ps(f"pacc{i}", [128, 512]) for i in range(3)]
        prot = [ps(f"prot{i}", [128, 512]) for i in range(2)]
        pssq = ps("pssq", [128, 512])
        ptr = ps("ptr", [128, 1024], BF16)

        for i, (dst, src) in enumerate([(c96, c96d), (s96, s96d), (c128, c128d), (s128, s128d),
                                        (r96, r96d), (r128, r128d), (ident, identd), (ng8, ng), (qg3, qg)]):
            S.dma("sp", dst[:], src, writes=[dst.name], slot=("c", i))
        S.op("pool", lambda e: e.memset(ones[:], 1.0), writes=["ones"])
        S.op("pool", lambda e: e.memset(wkr[:], 0.0), writes=[wkr.name])
        S.op("pool", lambda e: e.memset(ssq[:], 0.0), writes=["ssq"])
        S.op("pool", lambda e: e.memset(epst[:], EPS), writes=["epst"])
        S.dma("sp", wuqf[:], w_uq_v, writes=["wuqf"], slot=("wuq",))
        S.op("pool", lambda e: e.tensor_tensor(out=wuqb[:], in0=wuqf[:],
                                               in1=qg3[:, :].unsqueeze(2).broadcast_to([128, 3, 768]), op=ALU.mult),
             reads=["wuqf", qg3.name], writes=["wuqb"])

        for t in range(16):
            xs, hbs = xin[t % 2], hb[t % 2]
            S.dma("sp", xs[:], x[t * 128:(t + 1) * 128, :], writes=[xs.name], slot=("xin", t % 2))
            S.op("act", lambda e, xs=xs, t=t: e.activation(out=junk[:], in_=xs[:], func=AF.Square,
                                                           accum_out=ssq[:, t:t + 1]),
                 reads=[xs.name, "ssq"], writes=["junk", ("ssq", t)])
            S.op("act", lambda e, t=t: e.activation(out=rstd[:, t:t + 1], in_=ssq[:, t:t + 1], func=AF.Sqrt,
                                                    scale=1.0 / D, bias=epst[:, 0:1]),
                 reads=[("ssq", t), "epst"], writes=[("rstd", t)])
            S.op("dve", lambda e, t=t: e.reciprocal(out=rstd[:, t:t + 1], in_=rstd[:, t:t + 1]),
                 reads=[("rstd", t)], writes=[("rstd", t)])
            S.op("dve", lambda e, xs=xs, hbs=hbs, t=t: e.tensor_scalar(out=hbs[:], in0=xs[:], scalar1=rstd[:, t:t + 1],
                                                                       scalar2=None, op0=ALU.mult),
                 reads=[xs.name, ("rstd", t)], writes=[hbs.name])

            def tr(pe, hbs=hbs):
                ins = None
                for c in range(8):
                    ins = pe.transpose(out=ptr[:, c * 128:(c + 1) * 128], in_=hbs[:, c * 128:(c + 1) * 128],
                                       identity=ident[:])
                return ins
            S.op("pe", tr, reads=[hbs.name, ident.name], writes=["ptr"])
            S.op("act", lambda e, t=t: e.activation(out=hT[:, :, t * 128:(t + 1) * 128],
                                                    in_=ptr[:, :].rearrange("p (c n) -> p c n", c=8), func=AF.Copy),
                 reads=["ptr"], writes=[("hT", t // 4)])

        state = {"wg": 0, "pa": 0, "pr": 0, "ob": 0, "pb": 0, "qh": 0}

        def load_dma(k, col0, ncols):
            i = k % 3
            S.dma("sp", wf[i][:, :, 0:ncols], w_in_v[:, :, col0:col0 + ncols], writes=[wf[i].name], slot=("wf", i))

        def load_cast(k, ncols):
            i, j = k % 3, k % 2

            def cast(e):
                ins = None
                for c in range(8):
                    ins = e.activation(out=wb[j][:, c, 0:ncols], in_=wf[i][:, c, 0:ncols], func=AF.Copy,
                                       scale=ng8[:, c:c + 1])
                return ins
            S.op("act", cast, reads=[wf[i].name, ng8.name], writes=[wb[j].name])
            return wb[j]

        def nxt(k, n):
            v = state[k] % n
            state[k] += 1
            return v

        hTk = [("hT", i) for i in range(4)]

        def proj_fm(wt, c0, m, tb, out_ps):
            S.op("pe", mm_group([(wt[:, c, c0:c0 + m], hT[:, c, tb * 512:(tb + 1) * 512]) for c in range(8)],
                                out_ps[0:m, :]),
                 reads=[wt.name, ("hT", tb)], writes=[out_ps.name])

        def rope_store(src_bf, rows, rT, cs, sn, tb, dst_ap, p0=0, wkey=None):
            i = nxt("pr", 2)
            pr = prot[i]
            S.op("pe", mm_group([(rT[0:rows, 0:rows], src_bf[0:rows, :])], pr[0:rows, :]),
                 reads=[src_bf.name, rT.name], writes=[pr.name])
            a, b = t1[i], t2[i]
            tok = slice(tb * 512, (tb + 1) * 512)
            S.op("pool", lambda e: e.tensor_tensor(out=a[p0:rows, :], in0=src_bf[p0:rows, :], in1=cs[p0:rows, tok],
                                                   op=ALU.mult),
                 reads=[src_bf.name, cs.name], writes=[a.name])
            S.op("dve", lambda e: e.tensor_tensor(out=b[p0:rows, :], in0=pr[p0:rows, :], in1=sn[p0:rows, tok],
                                                  op=ALU.mult),
                 reads=[pr.name, sn.name], writes=[b.name])
            S.op("dve", lambda e: e.tensor_tensor(out=src_bf[p0:rows, :], in0=a[p0:rows, :], in1=b[p0:rows, :],
                                                  op=ALU.add),
                 reads=[a.name, b.name], writes=[src_bf.name])
            S.dma("sp", dst_ap, src_bf[dst_rows(p0, rows, dst_ap), :], reads=[src_bf.name],
                  writes=([wkey] if wkey is not None else []), slot=("st", src_bf.name))

        def dst_rows(p0, rows, dst_ap):
            n = dst_ap.shape[0]
            return slice(rows - n, rows)

        def latent_norm(nchunk, width, tb):
            S.op("pe", mm_group([(ones[:, :], sq_b[:, c, :]) for c in range(nchunk)], pssq[:, :]),
                 reads=["sq_b", "ones"], writes=["pssq"])
            S.op("act", lambda e: e.activation(out=rbc[:], in_=pssq[:], func=AF.Sqrt, scale=1.0 / width,
                                               bias=epst[:, 0:1]), reads=["pssq", "epst"], writes=["rbc"])
            S.op("dve", lambda e: e.reciprocal(out=rbc[:], in_=rbc[:]), reads=["rbc"], writes=["rbc"])
            for c in range(nchunk):
                S.op("dve", lambda e, c=c: e.tensor_tensor(out=cqn[:, c, :], in0=cq_b[:, c, :], in1=rbc[:], op=ALU.mult),
                     reads=["cq_b", "rbc"], writes=["cqn"])

        def latent_chunks(wt, nchunk, tb):
            for c3 in range(nchunk):
                pa = pacc[nxt("pa", 3)]
                proj_fm(wt, c3 * 128, 128, tb, pa)
                S.op("act", lambda e, pa=pa, c3=c3: e.activation(out=cq_b[:, c3, :], in_=pa[:], func=AF.Copy),
                     reads=[pa.name], writes=["cq_b"])
                S.op("act", lambda e, pa=pa, c3=c3: e.activation(out=sq_b[:, c3, :], in_=pa[:], func=AF.Square),
                     reads=[pa.name], writes=["sq_b"])

        pend = {"f": None}

        def defer(fn):
            prev = pend["f"]
            pend["f"] = fn
            if prev is not None:
                prev()

        def flush():
            prev = pend["f"]
            pend["f"] = None
            if prev is not None:
                prev()

        def unit_cq(wt):
            for tb in range(4):
                latent_chunks(wt, 3, tb)
                latent_norm(3, 384, tb)
                for h in range(8):
                    pa = pacc[nxt("pa", 3)]
                    q = qh[nxt("qh", 3)]
                    S.op("pe", mm_group([(wuqb[:, c, h * 96:(h + 1) * 96], cqn[:, c, :]) for c in range(3)], pa[0:96, :]),
                         reads=["wuqb", "cqn"], writes=[pa.name])
                    S.op("act", lambda e, pa=pa, q=q: e.activation(out=q[:], in_=pa[0:96, :], func=AF.Copy),
                         reads=[pa.name], writes=[q.name])
                    defer(lambda q=q, tb=tb, h=h: rope_store(q, 96, r96, c96, s96, tb,
                                                             qT_o[h, :, tb * 512:(tb + 1) * 512], p0=64))
                flush()

        def unit_ckv(wt):
            S.op("pool", lambda e: e.tensor_copy(out=wkr[:, :, 64:96], in_=wt[:, :, 256:288]), reads=[wt.name],
                 writes=[wkr.name])
            lat_v = lat_o[0:256, :].rearrange("(c p) t -> p c t", p=128)
            for tb in range(4):
                latent_chunks(wt, 2, tb)
                latent_norm(2, 256, tb)
                S.dma("sp", lat_v[:, :, tb * 512:(tb + 1) * 512], cqn[:, 0:2, :], reads=["cqn"],
                      writes=[("lat_loc", "c", tb)], slot=("st", "cqn"))
                pa = pacc[nxt("pa", 3)]
                q = qh[nxt("qh", 3)]
                proj_fm(wkr, 0, 96, tb, pa)
                S.op("act", lambda e, pa=pa, q=q: e.activation(out=q[:], in_=pa[0:96, :], func=AF.Copy),
                     reads=[pa.name], writes=[q.name])
                defer(lambda q=q, tb=tb: rope_store(q, 96, r96, c96, s96, tb,
                                                    lat_o[256:288, tb * 512:(tb + 1) * 512], p0=64,
                                                    wkey=("lat_loc", "r", tb)))
            flush()

        def gate_blocks(wt, row0):
            for blk in range(4):
                for tb in range(4):
                    pa = pacc[nxt("pa", 3)]
                    o = ob[nxt("ob", 3)]
                    proj_fm(wt, blk * 128, 128, tb, pa)
                    S.op("act", lambda e, pa=pa, o=o: e.activation(out=o[:], in_=pa[:], func=AF.Silu),
                         reads=[pa.name], writes=[o.name])
                    S.dma("sp", gT_o[row0 + blk * 128: row0 + (blk + 1) * 128, tb * 512:(tb + 1) * 512], o[:],
                          reads=[o.name], slot=("st", o.name))

        def rope_post(p, blk, tb, dst, g):
            rope_store(p, 128, r128, c128, s128, tb, dst[blk * 128:(blk + 1) * 128, tb * 512:(tb + 1) * 512])
            if pay is not None and g is not None:
                halo = HALO[g]
                if tb * 512 < halo:
                    n = min(halo - tb * 512, 512)
                    S.dma("sp", pay_k(pay, 0, g, 0, blk * 128, 128, tb * 512, n), p[:, 0:n], reads=[p.name],
                          slot=("st", p.name))
                lo, hi = max(tb * 512, T - halo), tb * 512 + 512
                if lo < hi:
                    S.dma("sp", pay_k(pay, 0, g, 1, blk * 128, 128, lo - (T - halo), hi - lo),
                          p[:, lo - tb * 512:512], reads=[p.name], slot=("st", p.name))

        def rope_blocks(wt, dst, g=None):
            for blk in range(4):
                for tb in range(4):
                    pa = pacc[nxt("pa", 3)]
                    p = pbf[nxt("pb", 3)]
                    proj_fm(wt, blk * 128, 128, tb, pa)
                    S.op("act", lambda e, pa=pa, p=p: e.activation(out=p[:], in_=pa[:], func=AF.Copy),
                         reads=[pa.name], writes=[p.name])
                    defer(lambda p=p, blk=blk, tb=tb: rope_post(p, blk, tb, dst, g))
            flush()

        def v_blocks(wt, dst, g=None):
            for t in range(16):
                pa = pacc[nxt("pa", 3)]
                o = ob[nxt("ob", 3)]
                S.op("pe", mm_group([(hT[:, c, t * 128:(t + 1) * 128], wt[:, c, 0:512]) for c in range(8)], pa[:, :]),
                     reads=[wt.name, ("hT", t // 4)], writes=[pa.name])
                S.op("act", lambda e, pa=pa, o=o: e.activation(out=o[:], in_=pa[:], func=AF.Copy),
                     reads=[pa.name], writes=[o.name])
                S.dma("sp", dst[t * 128:(t + 1) * 128, :], o[:], reads=[o.name], slot=("st", o.name))
                if pay is not None and g is not None:
                    halo = HALO[g]
                    if t * 128 < halo:
                        n = min(halo - t * 128, 128)
                        S.dma("sp", pay_v(pay, 0, g, 0, t * 128, n), o[0:n, :], reads=[o.name], slot=("st", o.name))
                    lo, hi = max(t * 128, T - halo), t * 128 + 128
                    if lo < hi:
                        S.dma("sp", pay_v(pay, 0, g, 1, lo - (T - halo), hi - lo), o[lo - t * 128:128, :],
                              reads=[o.name], slot=("st", o.name))

        units = [(0, 384, unit_cq), (384, 288, unit_ckv)]
        for g in range(3):
            base = 1184 + g * 1536
            units.append((base, 512, lambda wt, g=g: rope_blocks(wt, qd_o[g])))
            units.append((base + 512, 512, lambda wt, g=g: rope_blocks(wt, kd_o[g], g)))
        n_rope_units = len(units)
        units.append((672, 512, lambda wt: gate_blocks(wt, 0)))
        for g in range(3):
            base = 1184 + g * 1536
            units.append((base + 1024, 512, lambda wt, g=g: v_blocks(wt, vd_o[g], g)))
        units.append((5792, 512, lambda wt: gate_blocks(wt, 512)))
        load_dma(0, units[0][0], units[0][1])
        load_dma(1, units[1][0], units[1][1])
        wt = load_cast(0, units[0][1])
        for ui, (c0, ncols, fn) in enumerate(units):
            if ui + 2 < len(units):
                load_dma(ui + 2, units[ui + 2][0], units[ui + 2][1])
            wnext = load_cast(ui + 1, units[ui + 1][1]) if ui + 1 < len(units) else None
            fn(wt)
            wt = wnext
            if ui == n_rope_units - 1 and after_rope is not None:
                after_rope()
        if last:
            S.wait_all("sp")
        else:
            S.barrier()
        S.emit()


_CACHE = {}


def _get(name, builder):
    if name not in _CACHE:
        _CACHE[name] = builder()
    return _CACHE[name]


def run_A(x_shards, w_in_l, ng_l, w_uq_l, qg_l):
    nc = _get("A", build_A)
    in_maps = []
    for c in range(NCORE):
        m = {"x": x_shards[c], "w_in": w_in_l, "ng": ng_l, "w_uq": w_uq_l, "qg": qg_l}
        m.update(_get(("cA", c % 4), lambda: _consts_A(c % 4)))
        in_maps.append(m)
    res = run_bass_kernel_spmd(nc, in_maps, core_ids=list(range(NCORE)))
    return res.results


def build_B(mla_heads=tuple(range(8)), dil_hps=tuple(range(4)), debug=False, dil_groups=(0, 1, 2)):
    nc = bass.Bass("TRN2", target_bir_lowering=False)
    dt_in = lambda n, s, d=F32: nc.dram_tensor(n, s, d, kind="ExternalInput").ap()
    dt_out = lambda n, s, d=F32: nc.dram_tensor(n, s, d, kind="ExternalOutput").ap()
    a = dict(
        x=dt_in("x", [T, D]),
        latT=dt_in("latT", [288, S_FULL], BF16),
        qT=dt_in("qT", [8, 96, T], BF16),
        gT=dt_in("gT", [1024, T], BF16),
        qd=dt_in("qd", [3, 512, T], BF16),
        kd=[dt_in(f"kd{g}", [512, T + 2 * HALO[g]], BF16) for g in range(3)],
        vd=[dt_in(f"vd{g}", [T + 2 * HALO[g], 512], BF16) for g in range(3)],
        w_ukv=dt_in("w_ukv", [256, 1024]),
        kvg=dt_in("kvg", [128, 2]),
        w_out=dt_in("w_out", [D, D]),
        fg=dt_in("fg", [1, D]),
        masks=dt_in("masks", [4, 128, 128], BF16),
        ident=dt_in("ident", [128, 128], BF16),
        e32=dt_in("e32", [32, 96], BF16),
        xo=dt_out("xo", [T, D]),
        yo=dt_out("yo", [T, D]),
    )
    if debug:
        a["mixo"] = nc.dram_tensor("mixo", [1024, T], BF16, kind="ExternalOutput").ap()
    emit_B(nc, mla_heads=mla_heads, dil_hps=dil_hps, dil_groups=dil_groups, **a)
    return nc


def emit_B(nc, x, latT, qT, gT, qd, kd, vd, w_ukv, kvg, w_out, fg, masks, ident, e32, xo, yo,
           mla_heads=tuple(range(8)), dil_hps=tuple(range(4)), mixo=None, dil_groups=(0, 1, 2),
           S=None, psum=None, tag="", fz=None, last=True, final=True):
    with ExitStack() as es:
        if S is None:
            S = Sched(nc, es)
        sb = lambda n, s, d=BF16: es.enter_context(nc.sbuf_tensor(n + tag, s, d))
        if psum is None:
            pb = [es.enter_context(nc.psum_tensor(f"pb{i}", [128, 512], F32)) for i in range(8)]
        else:
            pb = psum
        pS, pAcc, pK = pb[0:3], pb[3:5], pb[5:7]
        if fz is not None:
            selK = sb("selK", [128, 8, 128])
            selVL = sb("selVL", [64, 4, 64])
            selVR = sb("selVR", [64, 4, 128])
            S.dma("sp", selK[:], fz["selK"], writes=["selK"], slot=("c", 4))
            S.dma("sp", selVL[:], fz["selVL"], writes=["selVL"], slot=("c", 5))
            S.dma("sp", selVR[:], fz["selVR"], writes=["selVR"], slot=("c", 6))
        mixedT = sb("mixedT", [128, 8, T])
        identS = sb("identS", [128, 128])
        e32S = sb("e32S", [32, 96])
        maskS = sb("maskS", [128, 4, 128])
        onesf = sb("onesf", [128, 64], F32)
        kvg2 = sb("kvg2", [128, 2], F32)
        epst = sb("epst", [128, 1], F32)
        rd = sb("rd", [128, 512], F32)
        tmp = [sb(f"tmp{i}", [128, 512], F32) for i in range(2)]
        gts = [sb(f"gts{i}", [128, 512]) for i in range(2)]
        S.dma("sp", identS[:], ident, writes=["identS"], slot=("c", 0))
        S.dma("sp", e32S[:], e32, writes=["e32S"], slot=("c", 1))
        S.dma("sp", maskS[:], masks.rearrange("k n m -> n k m"), writes=["maskS"], slot=("c", 2))
        S.dma("sp", kvg2[:], kvg, writes=["kvg2"], slot=("c", 3))
        S.op("pool", lambda e: e.memset(onesf[:], 1.0), writes=["onesf"])
        onesb = sb("onesb", [128, 2])
        S.op("pool", lambda e: e.memset(onesb[:], 1.0), writes=["onesb"])
        S.op("pool", lambda e: e.memset(epst[:], EPS), writes=["epst"])
        m01 = sb("m01", [128, 4, 128])
        m01c = sb("m01c", [128, 4, 2, 128])
        S.op("dve", lambda e: e.tensor_scalar(out=m01[:], in0=maskS[:], scalar1=1.0 / 32768.0, scalar2=1.0,
                                              op0=ALU.mult, op1=ALU.add), reads=["maskS"], writes=["m01"])
        for ci in range(4):
            h0 = 1 if (ci & 1) else 0
            h1 = 3 if (ci & 2) else 2
            S.op("pool", lambda e, ci=ci, h0=h0: e.tensor_copy(out=m01c[:, ci, 0, :], in_=m01[:, h0, :]),
                 reads=["m01"], writes=["m01c"])
            S.op("pool", lambda e, ci=ci, h1=h1: e.tensor_copy(out=m01c[:, ci, 1, :], in_=m01[:, h1, :]),
                 reads=["m01"], writes=["m01c"])
        st = {"tmp": 0, "gts": 0, "acc": 0, "pk": 0}

        def nxt(k, n):
            v = st[k] % n
            st[k] += 1
            return v

        def get_pk():
            return pK[0] if st.get("pk_single") else pK[nxt("pk", 2)]

        def normalize_gate(src_o, src_den, gate_ap, gate_reads, po, chunk, qb, src_reads):
            tok = slice(qb * 512, (qb + 1) * 512)
            pk = get_pk()
            tm = tmp[nxt("tmp", 2)]
            S.op("act", lambda e: e.activation(out=rd[64:65, :], in_=src_den, func=AF.Ln), reads=src_reads, writes=["rd"])
            S.op("act", lambda e: e.activation(out=rd[64:65, :], in_=rd[64:65, :], func=AF.Exp, scale=-1.0),
                 reads=["rd"], writes=["rd"])
            S.op("pe", mm_group([(onesf[64:65, 0:64], rd[64:65, :])], pk[0:64, :]), reads=["rd", "onesf"],
                 writes=[pk.name])
            S.op("dve", lambda e: e.tensor_tensor(out=tm[po:po + 64, :], in0=src_o, in1=pk[0:64, :], op=ALU.mult),
                 reads=src_reads + [pk.name], writes=[tm.name])
            S.op("dve", lambda e: e.tensor_tensor(out=mixedT[po:po + 64, chunk, tok], in0=tm[po:po + 64, :],
                                                  in1=gate_ap, op=ALU.mult),
                 reads=[tm.name] + gate_reads, writes=[("mixedT", chunk)])

        with ExitStack() as es1:
            sb1 = lambda n, s, d=BF16: es1.enter_context(nc.sbuf_tensor(n + tag, s, d))
            lat = sb1("lat", [128, 3, S_FULL])
            KT = sb1("KT", [96, S_FULL])
            V = sb1("V", [128, 64, 65])
            Qh = [sb1(f"Qh{i}", [96, T]) for i in range(2)]
            PT = [sb1(f"PT{i}", [128, 512]) for i in range(3)]
            osb = sb1("osb", [64, 512], F32)
            wukvf = sb1("wukvf", [128, 2, 1024], F32)
            wk96 = sb1("wk96", [128, 2, 8, 96])
            wv = sb1("wv", [128, 2, 8, 64])
            S.dma("sp", wukvf[:], w_ukv.rearrange("(c p) n -> p c n", p=128), writes=["wukvf"], slot=("wukv",))
            S.op("pool", lambda e: e.memset(wk96[:], 0.0), writes=["wk96"])
            S.op("pool", lambda e: e.memset(V[:], 1.0), writes=["V"])
            for c in range(2):
                wsrc = wukvf[:, c, :].rearrange("p (h k) -> p h k", h=8)
                S.op("pool", lambda e, c=c, wsrc=wsrc: e.tensor_tensor(
                    out=wk96[:, c, :, 0:64], in0=wsrc[:, :, 0:64],
                    in1=kvg2[:, c:c + 1].unsqueeze(2).broadcast_to([128, 8, 64]), op=ALU.mult),
                    reads=["wukvf", "kvg2"], writes=["wk96"])
                S.op("pool", lambda e, c=c, wsrc=wsrc: e.tensor_tensor(
                    out=wv[:, c, :, :], in0=wsrc[:, :, 64:128],
                    in1=kvg2[:, c:c + 1].unsqueeze(2).broadcast_to([128, 8, 64]), op=ALU.mult),
                    reads=["wukvf", "kvg2"], writes=["wv"])
            if fz is not None and fz.get("after_prologue") is not None:
                fz["after_prologue"]()
            for i in range(4):
                sl = slice(i * 2048, (i + 1) * 2048)
                if fz is None:
                    lat_v = latT[0:256, :].rearrange("(c p) t -> p c t", p=128)
                    S.dma("sp", lat[:, 0:2, sl], lat_v[:, :, sl], writes=[("lat", i)], slot=("lat", i))
                    S.dma("sp", lat[0:32, 2, sl], latT[256:288, sl], writes=[("latr", i)], slot=("latr", i))
                else:
                    S.dma("sp", lat[:, 0:2, sl],
                          fz["lat_allA"][i * 256:(i + 1) * 256, :].rearrange("(c p) t -> p c t", p=128),
                          reads=["lat_allA"], writes=[("lat", i)], slot=("lat", i))
                    S.dma("sp", lat[0:32, 2, sl], fz["lat_allB"][i * 32:(i + 1) * 32, :], reads=["lat_allB"],
                          writes=[("latr", i)], slot=("latr", i))

            for h in mla_heads:
                po, chunk = (h % 2) * 64, h // 2
                Q = Qh[h % 2]
                S.dma("sp", Q[:], qT[h], writes=[Q.name], slot=("Q", h % 2))
                for tb in range(16):
                    pk = pK[nxt("pk", 2)]
                    sl = slice(tb * 512, (tb + 1) * 512)
                    S.op("pe", mm_group([(wk96[:, 0, h, :], lat[:, 0, sl]), (wk96[:, 1, h, :], lat[:, 1, sl]),
                                         (e32S[0:32, :], lat[0:32, 2, sl])], pk[0:96, :]),
                         reads=["wk96", "e32S", ("lat", tb // 4), ("latr", tb // 4)], writes=[pk.name])
                    eng = "dve" if tb % 2 else "act"
                    if eng == "act":
                        S.op("act", lambda e, pk=pk, sl=sl: e.activation(out=KT[:, sl], in_=pk[0:96, :], func=AF.Copy),
                             reads=[pk.name], writes=[("KT", tb)])
                    else:
                        S.op("dve", lambda e, pk=pk, sl=sl: e.tensor_copy(out=KT[:, sl], in_=pk[0:96, :]),
                             reads=[pk.name], writes=[("KT", tb)])
                for k8 in range(8):
                    pk = pK[nxt("pk", 2)]

                    def vb(pe, pk=pk, k8=k8, h=h):
                        ins = None
                        for j in range(8):
                            kt = k8 * 8 + j
                            for c in range(2):
                                ins = pe.matmul(pk[:, j * 64:(j + 1) * 64], lhsT=lat[:, c, kt * 128:(kt + 1) * 128],
                                                rhs=wv[:, c, h, :], start=(c == 0), stop=(c == 1))
                        return ins
                    S.op("pe", vb, reads=["wv", ("lat", k8 // 2)], writes=[pk.name])
                    S.op("dve", lambda e, pk=pk, k8=k8: e.tensor_copy(
                        out=V[:, k8 * 8:(k8 + 1) * 8, 0:64], in_=pk[:, :].rearrange("p (j d) -> p j d", j=8)),
                        reads=[pk.name], writes=[("V", k8)])
                steps = [(qb, kt) for qb in range(4) for kt in range(64)]
                accs, gidx = {}, {}
                for qb in range(4):
                    accs[qb] = pAcc[nxt("acc", 2)]
                base = st.get("sstep", 0)

                def qk(n):
                    qb, kt = steps[n]
                    s_ = pS[(base + n) % 3]
                    S.op("pe", mm_group([(KT[:, kt * 128:(kt + 1) * 128], Q[:, qb * 512:(qb + 1) * 512])], s_[:, :]),
                         reads=[("KT", kt // 4), Q.name], writes=[s_.name])
                qk(0)
                qk(1)
                for n, (qb, kt) in enumerate(steps):
                    s_, p_ = pS[(base + n) % 3], PT[(base + n) % 3]
                    acc = accs[qb]
                    if kt == 0:
                        g_i = nxt("gts", 2)
                        gidx[qb] = g_i
                        S.dma("sp", gts[g_i][po:po + 64, :], gT[h * 64:(h + 1) * 64, qb * 512:(qb + 1) * 512],
                              writes=[gts[g_i].name], slot=("gts", g_i))
                    S.op("act", lambda e, s_=s_, p_=p_: e.activation(out=p_[:], in_=s_[:], func=AF.Exp,
                                                                     scale=MLA_SCALE),
                         reads=[s_.name], writes=[p_.name])
                    if n + 2 < len(steps):
                        qk(n + 2)
                    S.op("pe", lambda pe, kt=kt, p_=p_, acc=acc: pe.matmul(
                        acc[0:65, :], lhsT=V[:, kt, :], rhs=p_[:], start=(kt == 0), stop=(kt == 63)),
                        reads=[p_.name, ("V", kt // 8)], writes=[acc.name])
                    if kt == 63:
                        g_i = gidx[qb]
                        S.op("act", lambda e, acc=acc: e.activation(out=osb[:], in_=acc[0:64, :], func=AF.Copy),
                             reads=[acc.name], writes=["osb"])
                        normalize_gate(osb[0:64, :], acc[64:65, :], gts[g_i][po:po + 64, :], [gts[g_i].name], po, chunk,
                                       qb, ["osb", acc.name])
                st["sstep"] = base + len(steps)
            if fz is not None and fz.get("lato") is not None:
                S.dma("sp", fz["lato"], lat[:], reads=[("lat", i) for i in range(4)] + [("latr", i) for i in range(4)],
                      slot=("lato",))
            S.barrier()
            S.emit()

        with ExitStack() as es2:
            sb2 = lambda n, s, d=BF16: es2.enter_context(nc.sbuf_tensor(n + tag, s, d))
            Kg = [sb2(f"Kg{i}", [128, T + 2 * HALO[2]]) for i in range(2)]
            Qg = [sb2(f"Qg{i}", [128, T]) for i in range(2)]
            Vraw = [sb2(f"Vraw{i}", [128, 32, 128]) for i in range(2)]
            accT = sb2("accT", [65, 2, T], F32)
            gtb = sb2("gtb", [128, T])
            P2p = [sb2(f"P2p{i}", [128, 4, 128]) for i in range(4)]
            Sbanks = [pS[0], pS[1], pS[2], pK[1]]
            st["pk_single"] = True
            if fz is not None:
                candK = [sb2(f"candK{i}", [128, 4, HALO[2]]) for i in range(2)]
                candV = [sb2(f"candV{i}", [64, 4, 16, 128]) for i in range(2)]
            jobs = [(hp, g) for hp in dil_hps for g in dil_groups]

            jinfo = {}

            sel_ops = {}

            def stage(n):
                sel = []
                hp, g = jobs[n]
                d, halo = DILS[g], HALO[g]
                W = T + 2 * halo
                nbr = 16 // d + 1
                njb = 16 // d
                bi = n % 2
                kg, qg_, vr = Kg[bi], Qg[bi], Vraw[bi]
                kkeys, vkeys = [(kg.name, "own")], []
                S.dma("sp", qg_[:], qd[g, hp * 128:(hp + 1) * 128, :], writes=[qg_.name], slot=("qg", bi))
                cols = slice(hp * 128, (hp + 1) * 128)
                if fz is None:
                    S.dma("sp", kg[:, 0:W], kd[g][hp * 128:(hp + 1) * 128, :], writes=[(kg.name, "own")], slot=("kg", bi))
                    vsrc = vd[g].rearrange("(kb n r) c -> n r kb c", n=128, r=d)
                    for r in range(d):
                        vkeys.append((vr.name, "o", r, 0))
                        S.dma("sp", vr[:, r * nbr:(r + 1) * nbr, :], vsrc[:, r, :, cols],
                              writes=[vkeys[-1]], slot=("vr", bi))
                else:
                    pall = fz["pay_all"]
                    pkeys = fz["pay_keys"](g)
                    S.dma("sp", kg[:, halo:halo + T], kd[g][hp * 128:(hp + 1) * 128, :], writes=[(kg.name, "own")],
                          slot=("kg", bi))
                    vown = vd[g].rearrange("(kb two n r) c -> two n r kb c", two=2, n=64, r=d)
                    for r in range(d):
                        vkeys.append((vr.name, "o", r, 0))
                        S.dma("sp", vr[64:128, r * nbr:r * nbr + njb, :], vown[0][:, r, :, cols],
                              writes=[vkeys[-1]], slot=("vr", bi))
                        vkeys.append((vr.name, "o", r, 1))
                        S.dma("sp", vr[0:64, r * nbr + 1:r * nbr + 1 + njb, :], vown[1][:, r, :, cols],
                              writes=[vkeys[-1]], slot=("vr", bi))
                    for side in range(2):
                        piece = 1 - side
                        ck = candK[side]
                        S.dma("sp", ck[:, :, 0:halo], pay_k(pall, 0, g, piece, hp * 128, 128, 0, halo, ncand=4),
                              reads=pkeys, writes=[ck.name], slot=("candK", side))
                        cv = candV[side]
                        cvkeys = [(cv.name, j) for j in range(4)]
                        for j in range(4):
                            S.dma("sp", cv[:, j, 0:d, :], pay_v_cand(pall, j, g, piece, hp, d),
                                  reads=pkeys, writes=[cvkeys[j]], slot=("candV", side))
                        dst0 = 0 if side == 0 else halo + T
                        for c0 in range(0, halo, 512):
                            nn = min(512, halo - c0)
                            kkeys.append((kg.name, "h", side, c0))

                            def selk(side=side, ck=ck, c0=c0, nn=nn, dst0=dst0, kg=kg, key=kkeys[-1]):
                                pk = get_pk()
                                S.op("pe", mm_group([(selK[:, side * 4 + j, :], ck[:, j, c0:c0 + nn]) for j in range(4)],
                                                    pk[:, 0:nn]), reads=["selK", ck.name], writes=[pk.name])
                                S.op("act", lambda e: e.activation(out=kg[:, dst0 + c0:dst0 + c0 + nn], in_=pk[:, 0:nn],
                                                                   func=AF.Copy), reads=[pk.name], writes=[key])
                            sel.append(selk)
                        for r0 in range(0, d, 4):
                            nr = min(4, d - r0)
                            if side == 0:
                                lhs = [selVL[:, j, :] for j in range(4)]
                                prow, kbs, mrows = slice(0, 64), 0, 64
                            else:
                                lhs = [selVR[:, j, :] for j in range(4)]
                                prow, kbs, mrows = slice(64, 128), nbr - 1, 128
                            vkeys.append((vr.name, "h", side, r0))

                            def selv(lhs=lhs, cv=cv, r0=r0, nr=nr, prow=prow, kbs=kbs, mrows=mrows, vr=vr, nbr=nbr,
                                     cvkeys=cvkeys, key=vkeys[-1]):
                                pk = get_pk()
                                S.op("pe", mm_group([(lhs[j], cv[:, j, r0:r0 + nr, :]) for j in range(4)],
                                                    pk[0:mrows, 0:nr * 128]), reads=["selVL", "selVR"] + cvkeys,
                                     writes=[pk.name])
                                S.op("dve", lambda e: e.tensor_copy(
                                    out=vr[prow, r0 * nbr + kbs:(r0 + nr - 1) * nbr + kbs + 1:nbr, :],
                                    in_=pk[prow, 0:nr * 128].rearrange("p (r c) -> p r c", r=nr)),
                                    reads=[pk.name], writes=[key])
                            sel.append(selv)
                jinfo[n] = (kkeys, vkeys)
                sel_ops[n] = sel

            def run_sel(n):
                for f in sel_ops.pop(n, []):
                    f()

            def compute(n, mid=None):
                hp, g = jobs[n]
                d = DILS[g]
                nbr = 16 // d + 1
                njb = 16 // d
                bi = n % 2
                kg, qg_, vr = Kg[bi], Qg[bi], Vraw[bi]
                kkeys, vkeys = jinfo[n]
                first = (g == dil_groups[0])
                pbase = st.get("pbase", 0)
                accv = [accT[:, hl, :].rearrange("p (jb m r) -> p r jb m", r=d, m=128) for hl in range(2)]
                tiles = []
                for hl in range(2):
                    if d == 1:
                        quads = [[(0, 4 * q + j) for j in range(4)] for q in range(4)]
                        dsts = [accv[hl][:, 0, 4 * q:4 * q + 4, :] for q in range(4)]
                    elif d == 4:
                        quads = [[(r, j) for j in range(4)] for r in range(4)]
                        dsts = [accv[hl][:, r, 0:4, :] for r in range(4)]
                    else:
                        quads = [[(4 * q + j, 0) for j in range(4)] for q in range(4)]
                        dsts = [accv[hl][:, 4 * q:4 * q + 4, 0, :] for q in range(4)]
                    for quad, dst in zip(quads, dsts):
                        for j, (r, jb) in enumerate(quad):
                            tiles.append((hl, r, jb, j, dst))

                pairs = [(tiles[2 * k], tiles[2 * k + 1]) for k in range(len(tiles) // 2)]
                LA = 3

                def qkpair(p):
                    bank = Sbanks[(pbase + p) % 4]
                    mms = []
                    for t, (hl, r, jb, j, dst) in enumerate(pairs[p]):
                        po = hl * 64
                        for half in range(2):
                            kb = jb + half
                            k0 = r + d * kb * 128
                            q0 = r + d * jb * 128
                            mms.append((bank[:, (2 * t + half) * 128:(2 * t + half + 1) * 128],
                                        kg[po:po + 64, k0:k0 + d * 127 + 1:d], qg_[po:po + 64, q0:q0 + d * 127 + 1:d]))

                    def fn(pe):
                        ins = None
                        for o_, kap, qap in mms:
                            ins = pe.matmul(o_, lhsT=kap, rhs=qap, start=True, stop=True)
                        return ins
                    S.op("pe", fn, reads=kkeys + [qg_.name], writes=[bank.name])
                for p in range(min(LA, len(pairs))):
                    qkpair(p)
                for p, pr_ in enumerate(pairs):
                    bank, pp = Sbanks[(pbase + p) % 4], P2p[(pbase + p) % 4]
                    S.op("act", lambda e, bank=bank, pp=pp: e.activation(
                        out=pp[:, :, :], in_=bank[:, :].rearrange("p (a m) -> p a m", a=4), func=AF.Exp,
                        scale=DIL_SCALE), reads=[bank.name], writes=[(pp.name, 0), (pp.name, 1)])
                    for t, (hl, r, jb, j, dst) in enumerate(pr_):
                        ci = (1 if jb == 0 else 0) + (2 if jb == njb - 1 else 0)
                        S.op("dve", lambda e, pp=pp, ci=ci, t=t: e.tensor_tensor(
                            out=pp[:, 2 * t:2 * t + 2, :], in0=pp[:, 2 * t:2 * t + 2, :], in1=m01c[:, ci, :, :],
                            op=ALU.mult), reads=[(pp.name, t), "m01c"], writes=[(pp.name, t)])
                    if p + LA < len(pairs):
                        qkpair(p + LA)
                    if mid is not None and p == len(pairs) // 2:
                        mid()
                    hl0, j1 = pr_[0][0], pr_[1][3]
                    pv = pAcc[((2 * p) // 4) % 2]

                    def pvf(pe, pr_=pr_, pp=pp, pv=pv, vr=vr, nbr=nbr):
                        ins = None
                        for t, (hl, r, jb, j, dst) in enumerate(pr_):
                            for half in range(2):
                                pe.matmul(pv[0:64, j * 128:(j + 1) * 128],
                                          lhsT=vr[:, r * nbr + jb + half, hl * 64:(hl + 1) * 64],
                                          rhs=pp[:, 2 * t + half, :], start=(half == 0), stop=(half == 1))
                            for half in range(2):
                                ins = pe.matmul(pv[64:65, j * 128:(j + 1) * 128], lhsT=onesb[:, 0:1],
                                                rhs=pp[:, 2 * t + half, :], start=(half == 0), stop=(half == 1))
                        return ins
                    S.op("pe", pvf, reads=[(pp.name, 0), (pp.name, 1), "onesb"] + vkeys, writes=[pv.name])
                    if j1 == 3:
                        dst = pr_[1][4]
                        src = pv[0:65, :].rearrange("p (a m) -> p a m", a=4)
                        if first:
                            S.op("dve", lambda e, dst=dst, src=src: e.tensor_copy(out=dst, in_=src),
                                 reads=[pv.name], writes=[("accT", hl0)])
                        else:
                            S.op("dve", lambda e, dst=dst, src=src: e.tensor_tensor(out=dst, in0=src, in1=dst,
                                                                                    op=ALU.add),
                                 reads=[pv.name, ("accT", hl0)], writes=[("accT", hl0)])
                st["pbase"] = pbase + len(pairs)

            if jobs:
                stage(0)
                run_sel(0)
            for n, (hp, g) in enumerate(jobs):
                if g == dil_groups[0]:
                    S.dma("sp", gtb[:], gT[512 + hp * 128: 512 + (hp + 1) * 128, :], writes=["gtb"], slot=("gtb",))
                if n + 1 < len(jobs):
                    stage(n + 1)
                compute(n, mid=(lambda n=n: run_sel(n + 1)))
                if g == dil_groups[-1]:
                    for hl in range(2):
                        po = hl * 64
                        for qb in range(4):
                            tok = slice(qb * 512, (qb + 1) * 512)
                            normalize_gate(accT[0:64, hl, tok], accT[64:65, hl, tok], gtb[po:po + 64, tok], ["gtb"], po,
                                           4 + hp, qb, [("accT", hl)])
            S.barrier()
            S.emit()

        with ExitStack() as es3:
            sb3 = lambda n, s, d=BF16: es3.enter_context(nc.sbuf_tensor(n + tag, s, d))
            wof = sb3("wof", [128, 8, 512], F32)
            wob = sb3("wob", [128, 8, D])
            fgb = sb3("fgb", [128, D], F32)
            xin = [sb3(f"xin{i}", [128, D], F32) for i in range(2)]
            xn = [sb3(f"xn{i}", [128, D], F32) for i in range(2)]
            yv = [sb3(f"yv{i}", [128, D], F32) for i in range(2)]
            junk = sb3("junk", [128, D], F32)
            ssq = sb3("ssq", [128, 16], F32)
            rstd = sb3("rstd", [128, 16], F32)
            S.op("pool", lambda e: e.memset(ssq[:], 0.0), writes=["ssq"])
            S.dma("sp", fgb[:], fg.partition_broadcast(128), writes=["fgb"], slot=("fgb",))
            w_out_v = w_out.rearrange("(c p) n -> p c n", p=128)
            if mixo is not None:
                S.dma("pool", mixo.rearrange("(c p) t -> p c t", p=128), mixedT[:], reads=[("mixedT", c) for c in range(8)],
                      slot=("mixo",))
            for nb in range(2):
                S.dma("sp", wof[:], w_out_v[:, :, nb * 512:(nb + 1) * 512], writes=["wof"], slot=("wof",))
                S.op("pool", lambda e, nb=nb: e.tensor_copy(out=wob[:, :, nb * 512:(nb + 1) * 512], in_=wof[:]),
                     reads=["wof"], writes=["wob"])
            for t in range(16):
                xs, xnn, yy = xin[t % 2], xn[t % 2], yv[t % 2]
                S.dma("sp", xs[:], x[t * 128:(t + 1) * 128, :], writes=[xs.name], slot=("xin", t % 2))
                for nb in range(2):
                    pa = pS[(2 * t + nb) % 3]
                    S.op("pe", mm_group([(mixedT[:, c, t * 128:(t + 1) * 128], wob[:, c, nb * 512:(nb + 1) * 512])
                                         for c in range(8)], pa[:, :]),
                         reads=["wob"] + [("mixedT", c) for c in range(8)], writes=[pa.name])
                    S.op("dve", lambda e, pa=pa, nb=nb, xs=xs, xnn=xnn: e.tensor_tensor(
                        out=xnn[:, nb * 512:(nb + 1) * 512], in0=pa[:, :], in1=xs[:, nb * 512:(nb + 1) * 512], op=ALU.add),
                        reads=[pa.name, xs.name], writes=[xnn.name])
                S.dma("pool", xo[t * 128:(t + 1) * 128, :], xnn[:], reads=[xnn.name], slot=("xo", t % 2))
                if not final:
                    continue
                S.op("act", lambda e, xnn=xnn, t=t: e.activation(out=junk[:], in_=xnn[:], func=AF.Square,
                                                                 accum_out=ssq[:, t:t + 1]),
                     reads=[xnn.name, "ssq"], writes=["junk", ("ssq", t)])
                S.op("act", lambda e, t=t: e.activation(out=rstd[:, t:t + 1], in_=ssq[:, t:t + 1], func=AF.Sqrt,
                                                        scale=1.0 / D, bias=epst[:, 0:1]),
                     reads=[("ssq", t), "epst"], writes=[("rstd", t)])
                S.op("dve", lambda e, t=t: e.reciprocal(out=rstd[:, t:t + 1], in_=rstd[:, t:t + 1]),
                     reads=[("rstd", t)], writes=[("rstd", t)])
                S.op("dve", lambda e, xnn=xnn, yy=yy, t=t: e.scalar_tensor_tensor(
                    out=yy[:], in0=xnn[:], scalar=rstd[:, t:t + 1], in1=fgb[:], op0=ALU.mult, op1=ALU.mult),
                    reads=[xnn.name, ("rstd", t), "fgb"], writes=[yy.name])
                S.dma("pool", yo[t * 128:(t + 1) * 128, :], yy[:], reads=[yy.name], slot=("yo", t % 2))
            if last:
                S.wait_all("sp")
            else:
                S.barrier()
            S.emit()


def run_B(in_maps):
    nc = _get("B", build_B)
    res = run_bass_kernel_spmd(nc, in_maps, core_ids=list(range(NCORE)))
    return res.results


def _exchange(resA, x_shards, w_ukv_l, kvg_l, w_out_l, fg):
    in_maps = []
    for b in range(2):
        cores = [4 * b + r for r in range(4)]
        lat_full = np.concatenate([np.asarray(resA[c]["lat"]) for c in cores], axis=1)
        kd_full = np.concatenate([np.asarray(resA[c]["kd"]) for c in cores], axis=2)
        vd_full = np.concatenate([np.asarray(resA[c]["vd"]) for c in cores], axis=1)
        for r in range(4):
            c = cores[r]
            m = {"x": x_shards[c], "latT": lat_full, "qT": np.asarray(resA[c]["qT"]), "gT": np.asarray(resA[c]["gT"]),
                 "qd": np.asarray(resA[c]["qd"]), "w_ukv": w_ukv_l, "kvg": kvg_l, "w_out": w_out_l, "fg": fg}
            for g in range(3):
                hl = HALO[g]
                kp = np.zeros((512, S_FULL + 2 * hl), dtype=kd_full.dtype)
                kp[:, hl:hl + S_FULL] = kd_full[g]
                vp = np.zeros((S_FULL + 2 * hl, 512), dtype=vd_full.dtype)
                vp[hl:hl + S_FULL] = vd_full[g]
                m[f"kd{g}"] = np.ascontiguousarray(kp[:, r * T: r * T + T + 2 * hl])
                m[f"vd{g}"] = np.ascontiguousarray(vp[r * T: r * T + T + 2 * hl])
            m.update(_get(("cB", r), lambda: _consts_B(r)))
            in_maps.append(m)
    return in_maps


def kernel_unfused(x, norm_g, w_in, q_norm_g, kv_norm_g, w_uq, w_ukv, w_out, final_g):
    x = np.asarray(x, dtype=np.float32)
    xs = [np.ascontiguousarray(x[c // 4, (c % 4) * T:(c % 4 + 1) * T]) for c in range(NCORE)]
    fg = np.ascontiguousarray(np.asarray(final_g, np.float32).reshape(1, D))
    ys = None
    for l in range(DEPTH):
        ng8 = np.ascontiguousarray(np.asarray(norm_g[l], np.float32).reshape(8, 128).T)
        qg3 = np.ascontiguousarray(np.asarray(q_norm_g[l], np.float32).reshape(3, 128).T)
        kvg2 = np.ascontiguousarray(np.asarray(kv_norm_g[l], np.float32).reshape(2, 128).T)
        resA = run_A(xs, np.ascontiguousarray(w_in[l], dtype=np.float32), ng8,
                     np.ascontiguousarray(w_uq[l], dtype=np.float32), qg3)
        in_maps = _exchange(resA, xs, np.ascontiguousarray(w_ukv[l], dtype=np.float32), kvg2,
                            np.ascontiguousarray(w_out[l], dtype=np.float32), fg)
        resB = run_B(in_maps)
        xs = [np.asarray(resB[c]["xo"]) for c in range(NCORE)]
        ys = [np.asarray(resB[c]["yo"]) for c in range(NCORE)]
    out = np.zeros((2, S_FULL, D), np.float32)
    for c in range(NCORE):
        out[c // 4, (c % 4) * T:(c % 4 + 1) * T] = ys[c]
    return out


def _consts_F(rank):
    c = dict(_consts_A(rank))
    c.update(_consts_B(rank))
    eye = np.eye(128, dtype=np.float32)
    selK = np.zeros((128, 8, 128), np.float32)
    selVL = np.zeros((64, 4, 64), np.float32)
    selVR = np.zeros((64, 4, 128), np.float32)
    for j in range(4):
        if j == rank - 1:
            selK[:, j, :] = eye
            selVL[:, j, :] = np.eye(64, dtype=np.float32)
        if j == rank + 1:
            selK[:, 4 + j, :] = eye
            selVR[:, j, 64:128] = np.eye(64, dtype=np.float32)
    c["selK"] = selK.astype(NPBF)
    c["selVL"] = selVL.astype(NPBF)
    c["selVR"] = selVR.astype(NPBF)
    return c


def build_fused(depth=DEPTH, debug=False):
    nc = bass.Bass("TRN2", target_bir_lowering=False)
    dt_in = lambda n, s, d=F32: nc.dram_tensor(n, s, d, kind="ExternalInput").ap()
    x = dt_in("x", [T, D])
    w_in = dt_in("w_in", [DEPTH, D, INW])
    ng = dt_in("ng", [DEPTH, 128, 8])
    w_uq = dt_in("w_uq", [DEPTH, 384, 768])
    qg = dt_in("qg", [DEPTH, 128, 3])
    w_ukv = dt_in("w_ukv", [DEPTH, 256, 1024])
    kvg = dt_in("kvg", [DEPTH, 128, 2])
    w_out = dt_in("w_out", [DEPTH, D, D])
    fg = dt_in("fg", [1, D])
    c96 = dt_in("c96", [96, T])
    s96 = dt_in("s96", [96, T])
    c128 = dt_in("c128", [128, T])
    s128 = dt_in("s128", [128, T])
    r96 = dt_in("r96", [96, 96], BF16)
    r128 = dt_in("r128", [128, 128], BF16)
    ident = dt_in("ident", [128, 128], BF16)
    masks = dt_in("masks", [4, 128, 128], BF16)
    e32 = dt_in("e32", [32, 96], BF16)
    selK = dt_in("selK", [128, 8, 128], BF16)
    selVL = dt_in("selVL", [64, 4, 64], BF16)
    selVR = dt_in("selVR", [64, 4, 128], BF16)
    yo = nc.dram_tensor("yo", [T, D], F32, kind="ExternalOutput").ap()
    mixo = nc.dram_tensor("mixo", [1024, T], BF16, kind="ExternalOutput").ap() if debug else None
    lato = nc.dram_tensor("lato", [128, 3, S_FULL], BF16, kind="ExternalOutput").ap() if debug else None
    xb = [nc.dram_tensor(f"xb{i}", [T, D], F32).ap() for i in range(2)]
    lat_loc = [nc.dram_tensor(f"lat_loc{i}", [288, T], BF16).ap() for i in range(2)]
    lat_allA = [nc.dram_tensor(f"lat_allA{i}", [4 * 256, T], BF16).ap() for i in range(2)]
    lat_allB = [nc.dram_tensor(f"lat_allB{i}", [4 * 32, T], BF16).ap() for i in range(2)]
    pay_h = [pay_tensors(nc, f"pay{i}", 1) for i in range(2)]
    pay_all_h = [pay_tensors(nc, f"pay_all{i}", 4) for i in range(2)]
    qT_s = nc.dram_tensor("qT_s", [8, 96, T], BF16).ap()
    gT_s = nc.dram_tensor("gT_s", [1024, T], BF16).ap()
    qd_s = nc.dram_tensor("qd_s", [3, 512, T], BF16).ap()
    kd_s = nc.dram_tensor("kd_s", [3, 512, T], BF16).ap()
    vd_s = nc.dram_tensor("vd_s", [3, T, 512], BF16).ap()
    groups = [[0, 1, 2, 3], [4, 5, 6, 7]]
    with ExitStack() as es:
        S = Sched(nc, es)
        pb = [es.enter_context(nc.psum_tensor(f"pb{i}", [128, 512], F32)) for i in range(7)]
        ptr = es.enter_context(nc.psum_tensor("ptr", [128, 1024], BF16))
        psA = {"pacc0": pb[0], "pacc1": pb[1], "pacc2": pb[2], "prot0": pb[3], "prot1": pb[4], "pssq": pb[5],
               "ptr": ptr}
        for l in range(depth):
            xin = x if l == 0 else xb[(l - 1) % 2]
            b = l % 2
            def ag(src, dst, key, ci, reads=()):
                S.custom("pool", lambda e, src=src, dst=dst: e.collective_compute(
                    "AllGather", ALU.bypass, replica_groups=groups, ins=[src], outs=[dst]), slot=("cc", ci),
                    reads=list(reads), writes=[key])

            def lat_ags(b=b):
                ag(lat_loc[b][0:256, :], lat_allA[b][:, :], "lat_allA", 0, [("lat_loc", "c", tb) for tb in range(4)])
                ag(lat_loc[b][256:288, :], lat_allB[b][:, :], "lat_allB", 1, [("lat_loc", "r", tb) for tb in range(4)])
            emit_A(nc, xin, w_in[l], ng[l], w_uq[l], qg[l], c96, s96, c128, s128, r96, r128, ident,
                   lat_loc[b], qT_s, gT_s, qd_s, [kd_s[g] for g in range(3)], [vd_s[g] for g in range(3)],
                   S=S, psum=psA, pay=pay_h[b], tag=f"_A{l}", last=False, after_rope=lat_ags)

            def halo_ags(b=b):
                for ci, key in enumerate(pay_h[b]):
                    ag(pay_h[b][key].ap()[:, :], pay_all_h[b][key].ap()[:, :], "pay_all_" + "_".join(map(str, key)), 2 + ci)
            pay_keys = lambda g: (["pay_all_2_%d" % k for k in range(4)] if g == 2 else ["pay_all_%d" % g])
            lastl = (l == depth - 1)
            hook = halo_ags
            if not OVERLAP_EXCHANGE:
                halo_ags()
                S.barrier()
                hook = None
            emit_B(nc, xin, None, qT_s, gT_s, qd_s, kd_s, vd_s, w_ukv[l], kvg[l], w_out[l], fg, masks, ident, e32,
                   xb[l % 2], yo, S=S, psum=pb, tag=f"_B{l}", mixo=(mixo if l == depth - 1 else None),
                   fz={"lat_allA": lat_allA[b], "lat_allB": lat_allB[b], "pay_all": pay_all_h[b], "after_prologue": hook, "lato": lato,
                       "pay_keys": pay_keys, "selK": selK, "selVL": selVL, "selVR": selVR},
                   last=lastl, final=lastl)
    return nc


def kernel(x, norm_g, w_in, q_norm_g, kv_norm_g, w_uq, w_ukv, w_out, final_g):
    f32c = lambda a: np.ascontiguousarray(np.asarray(a, dtype=np.float32))
    x = f32c(x)
    shared = {
        "w_in": f32c(w_in), "w_uq": f32c(w_uq), "w_ukv": f32c(w_ukv), "w_out": f32c(w_out),
        "ng": f32c(np.asarray(norm_g, np.float32).reshape(DEPTH, 8, 128).transpose(0, 2, 1)),
        "qg": f32c(np.asarray(q_norm_g, np.float32).reshape(DEPTH, 3, 128).transpose(0, 2, 1)),
        "kvg": f32c(np.asarray(kv_norm_g, np.float32).reshape(DEPTH, 2, 128).transpose(0, 2, 1)),
        "fg": f32c(np.asarray(final_g, np.float32).reshape(1, D)),
    }
    nc = _get("F", build_fused)
    in_maps = []
    for c in range(NCORE):
        m = dict(shared)
        m["x"] = np.ascontiguousarray(x[c // 4, (c % 4) * T:(c % 4 + 1) * T])
        m.update(_get(("cF", c % 4), lambda: _consts_F(c % 4)))
        in_maps.append(m)
    res = run_bass_kernel_spmd(nc, in_maps, core_ids=list(range(NCORE))).results
    out = np.zeros((2, S_FULL, D), np.float32)
    for c in range(NCORE):
        out[c // 4, (c % 4) * T:(c % 4 + 1) * T] = np.asarray(res[c]["yo"])
    return out
```

```python
import numpy as np
import ml_dtypes
from contextlib import ExitStack
import concourse.bass as bass
import concourse.mybir as mybir
from concourse.bass_utils import run_bass_kernel_spmd

F32 = mybir.dt.float32
BF16 = mybir.dt.bfloat16
AF = mybir.ActivationFunctionType
ALU = mybir.AluOpType
NPBF = ml_dtypes.bfloat16

D = 1024
T = 2048
S_FULL = 8192
NCORE = 8
DEPTH = 4
EPS = 1e-6
INW = 6304
DILS = (1, 4, 16)
HALO = tuple(64 * d for d in DILS)
MLA_SCALE = 96 ** -0.5
DIL_SCALE = 64 ** -0.5
MASKNEG = -32768.0
OVERLAP_EXCHANGE = True


class Sched:
    EP = 30000
    ENGS = ("pe", "act", "dve", "pool", "sp")

    def __init__(self, nc, es):
        self.nc, self.es = nc, es
        self.ops = {e: [] for e in self.ENGS}
        self.nops = {e: 0 for e in self.ENGS}
        self.esems = {}
        self.lastw = {}
        self.readers = {}
        self.waited = {e: {} for e in self.ENGS}
        self.dsem = {}
        self.last_tok = {}
        self.pending_barrier = {e: None for e in self.ENGS}
        self.nsem = 0

    def _newsem(self, name):
        self.nsem += 1
        return self.es.enter_context(self.nc.semaphore(name))

    def _esem(self, e, epoch):
        k = (e, epoch)
        if k not in self.esems:
            self.esems[k] = self._newsem(f"e_{e}_{epoch}")
        return self.esems[k]

    def _waits(self, eng, reads, writes):
        toks = []
        for k in reads:
            w = self.lastw.get(k)
            if w is not None:
                toks.append(w)
        for k in writes:
            w = self.lastw.get(k)
            if w is not None:
                toks.append(w)
            toks.extend(self.readers.get(k, {}).values())
        if self.pending_barrier[eng] is not None:
            toks.extend(self.pending_barrier[eng])
            self.pending_barrier[eng] = None
        need = {}
        for (skey, sem, val, src) in toks:
            if src == eng and eng == "pe":
                continue
            if val > self.waited[eng].get(skey, 0):
                self.waited[eng][skey] = val
                need[skey] = (sem, val)
        return list(need.values())

    def _commit(self, tok, reads, writes):
        for k in writes:
            self.lastw[k] = tok
            self.readers[k] = {}
        for k in reads:
            if k in writes:
                continue
            self.readers.setdefault(k, {})[tok[0]] = tok
        self.last_tok[tok[0]] = tok

    def op(self, eng, fn, reads=(), writes=()):
        waits = self._waits(eng, reads, writes)
        idx = self.nops[eng]
        self.nops[eng] += 1
        epoch, val = idx // self.EP, idx % self.EP + 1
        sem = self._esem(eng, epoch)
        tok = ((eng, epoch), sem, val, eng)
        self.ops[eng].append((waits, fn, sem, 1))
        self._commit(tok, reads, writes)

    def dma(self, q, out, in_, reads=(), writes=(), slot=None):
        waits = self._waits(q, reads, writes)
        if slot not in self.dsem:
            self.dsem[slot] = [self._newsem(f"d_{len(self.dsem)}"), 0]
        ent = self.dsem[slot]
        ent[1] += 16
        tok = (("dma", slot), ent[0], ent[1], "dma")
        self.ops[q].append((waits, lambda e, o=out, i=in_: e.dma_start(out=o, in_=i), ent[0], 16))
        self._commit(tok, reads, writes)

    def custom(self, eng, fn, slot, reads=(), writes=()):
        waits = self._waits(eng, reads, writes)
        if slot not in self.dsem:
            self.dsem[slot] = [self._newsem(f"d_{len(self.dsem)}"), 0]
        ent = self.dsem[slot]
        ent[1] += 1
        tok = (("dma", slot), ent[0], ent[1], "dma")
        self.ops[eng].append((waits, fn, ent[0], 1))
        self._commit(tok, reads, writes)

    def barrier(self):
        toks = list(self.last_tok.values())
        for e in self.ENGS:
            self.pending_barrier[e] = list(toks)

    def wait_all(self, eng="sp"):
        self.barrier()
        waits = self._waits(eng, (), ())
        self.ops[eng].append((waits, None, None, 0))

    def emit(self):
        nc = self.nc
        ops = self.ops
        self.ops = {e: [] for e in self.ENGS}

        def replay(name, e):
            for waits, fn, sem, inc in ops[name]:
                for (s, v) in waits:
                    e.wait_ge(s, v)
                if fn is not None:
                    fn(e).then_inc(sem, inc)

        with nc.Block() as block:
            @block.tensor
            def _(e):
                replay("pe", e)

            @block.scalar
            def _(e):
                replay("act", e)

            @block.vector
            def _(e):
                replay("dve", e)

            @block.gpsimd
            def _(e):
                replay("pool", e)

            @block.sync
            def _(e):
                replay("sp", e)


def pay_tensors(nc, name, mult):
    t = {}
    t[(0,)] = nc.dram_tensor(f"{name}_g0", [mult * 4 * HALO[0], 512], BF16)
    t[(1,)] = nc.dram_tensor(f"{name}_g1", [mult * 4 * HALO[1], 512], BF16)
    for k in range(4):
        t[(2, k)] = nc.dram_tensor(f"{name}_g2_{k}", [mult * HALO[2], 512], BF16)
    return t


def pay_piece(tens, g, kind, side):
    k = kind * 2 + side
    if g == 2:
        return tens[(2, k)], 0, HALO[2]
    return tens[(g,)], k * HALO[g], 4 * HALO[g]


def pay_k(tens, j, g, side, f0, nf, t0, nt, ncand=None):
    halo = HALO[g]
    h, roff, rpr = pay_piece(tens, g, 0, side)
    base = (j * rpr + roff) * 512 + f0 * halo + t0
    if ncand is None:
        return bass.AP(h, base, [[halo, nf], [1, nt]])
    return bass.AP(h, base, [[halo, nf], [rpr * 512, ncand], [1, nt]])


def pay_v(tens, j, g, side, r0, nr):
    h, roff, rpr = pay_piece(tens, g, 1, side)
    base = (j * rpr + roff + r0) * 512
    return bass.AP(h, base, [[512, nr], [1, 512]])


def pay_v_cand(tens, j, g, side, hp, d):
    h, roff, rpr = pay_piece(tens, g, 1, side)
    base = (j * rpr + roff) * 512 + hp * 128
    return bass.AP(h, base, [[d * 512, 64], [512, d], [1, 128]])


def mm_group(lhs_rhs, out):
    def fn(pe):
        n = len(lhs_rhs)
        ins = None
        for i, (l, r) in enumerate(lhs_rhs):
            ins = pe.matmul(out, lhsT=l, rhs=r, start=(i == 0), stop=(i == n - 1))
        return ins
    return fn


def _rope_tables(pos, dim):
    inv = (1.0 / (500000.0 ** (np.arange(0, dim, 2, dtype=np.float32) / np.float32(dim)))).astype(np.float32)
    ang = pos.astype(np.float32)[:, None] * inv[None, :]
    return np.cos(ang).astype(np.float32), np.sin(ang).astype(np.float32)


def _consts_A(rank):
    pos = np.arange(rank * T, (rank + 1) * T)
    cm, sm = _rope_tables(pos, 32)
    cd, sd = _rope_tables(pos, 16)
    c96 = np.ones((96, T), np.float32)
    s96 = np.zeros((96, T), np.float32)
    c96[64:80] = cm.T
    c96[80:96] = cm.T
    s96[64:80] = sm.T
    s96[80:96] = sm.T
    c128 = np.ones((128, T), np.float32)
    s128 = np.zeros((128, T), np.float32)
    for b in (0, 64):
        c128[b:b + 8] = cd.T
        c128[b + 8:b + 16] = cd.T
        s128[b:b + 8] = sd.T
        s128[b + 8:b + 16] = sd.T
    r96 = np.zeros((96, 96), np.float32)
    for i in range(16):
        r96[80 + i, 64 + i] = -1.0
        r96[64 + i, 80 + i] = 1.0
    r128 = np.zeros((128, 128), np.float32)
    for b in (0, 64):
        for i in range(8):
            r128[b + 8 + i, b + i] = -1.0
            r128[b + i, b + 8 + i] = 1.0
    return {
        "c96": c96, "s96": s96, "c128": c128, "s128": s128,
        "r96": r96.astype(NPBF), "r128": r128.astype(NPBF),
        "ident": np.eye(128, dtype=np.float32).astype(NPBF),
    }


def _consts_B(rank):
    n = np.arange(128)[:, None]
    m = np.arange(128)[None, :]
    m0 = np.where(m <= n, 0.0, MASKNEG)
    m1 = np.where(m >= n, 0.0, MASKNEG)
    m0f = m0.copy()
    m1l = m1.copy()
    if rank == 0:
        m0f[:64, :] = MASKNEG
    if rank == 3:
        m1l[64:, :] = MASKNEG
    masks = np.stack([m0, m0f, m1, m1l]).astype(np.float32).astype(NPBF)
    e32 = np.zeros((32, 96), np.float32)
    for i in range(32):
        e32[i, 64 + i] = 1.0
    return {"masks": masks, "ident": np.eye(128, dtype=np.float32).astype(NPBF), "e32": e32.astype(NPBF)}


def build_A():
    nc = bass.Bass("TRN2", target_bir_lowering=False)
    dt_in = lambda n, s, d=F32: nc.dram_tensor(n, s, d, kind="ExternalInput").ap()
    dt_out = lambda n, s, d=BF16: nc.dram_tensor(n, s, d, kind="ExternalOutput").ap()
    x = dt_in("x", [T, D])
    w_in = dt_in("w_in", [D, INW])
    ng = dt_in("ng", [128, 8])
    w_uq = dt_in("w_uq", [384, 768])
    qg = dt_in("qg", [128, 3])
    c96d = dt_in("c96", [96, T])
    s96d = dt_in("s96", [96, T])
    c128d = dt_in("c128", [128, T])
    s128d = dt_in("s128", [128, T])
    r96d = dt_in("r96", [96, 96], BF16)
    r128d = dt_in("r128", [128, 128], BF16)
    identd = dt_in("ident", [128, 128], BF16)
    lat_o = dt_out("lat", [288, T])
    qT_o = dt_out("qT", [8, 96, T])
    gT_o = dt_out("gT", [1024, T])
    qd_o = dt_out("qd", [3, 512, T])
    kd_o = dt_out("kd", [3, 512, T])
    vd_o = dt_out("vd", [3, T, 512])
    emit_A(nc, x, w_in, ng, w_uq, qg, c96d, s96d, c128d, s128d, r96d, r128d, identd,
           lat_o, qT_o, gT_o, qd_o, kd_o, vd_o)
    return nc


def emit_A(nc, x, w_in, ng, w_uq, qg, c96d, s96d, c128d, s128d, r96d, r128d, identd,
           lat_o, qT_o, gT_o, qd_o, kd_o, vd_o, S=None, psum=None, pay=None, tag="", last=True, after_rope=None):
    w_in_v = w_in.rearrange("(c p) n -> p c n", p=128)
    w_uq_v = w_uq.rearrange("(c p) n -> p c n", p=128)
    with ExitStack() as es:
        if S is None:
            S = Sched(nc, es)
        sb = lambda n, s, d=BF16: es.enter_context(nc.sbuf_tensor(n + tag, s, d))
        if psum is None:
            ps = lambda n, s, d=F32: es.enter_context(nc.psum_tensor(n, s, d))
        else:
            ps = lambda n, s, d=F32: psum[n]
        hT = sb("hT", [128, 8, T])
        c96 = sb("c96s", [96, T], F32)
        s96 = sb("s96s", [96, T], F32)
        c128 = sb("c128s", [128, T], F32)
        s128 = sb("s128s", [128, T], F32)
        r96 = sb("r96s", [96, 96])
        r128 = sb("r128s", [128, 128])
        ident = sb("idents", [128, 128])
        ones = sb("ones", [128, 128])
        ng8 = sb("ng8", [128, 8], F32)
        qg3 = sb("qg3", [128, 3], F32)
        xin = [sb(f"xin{i}", [128, D], F32) for i in range(2)]
        junk = sb("junk", [128, D], F32)
        hb = [sb(f"hb{i}", [128, D]) for i in range(2)]
        ssq = sb("ssq", [128, 16], F32)
        epst = sb("epst", [128, 1], F32)
        rstd = sb("rstd", [128, 16], F32)
        wf = [sb(f"wf{i}", [128, 8, 512], F32) for i in range(3)]
        wb = [sb(f"wb{i}", [128, 8, 512]) for i in range(2)]
        wkr = sb("wkr", [128, 8, 96])
        wuqf = sb("wuqf", [128, 3, 768], F32)
        wuqb = sb("wuqb", [128, 3, 768])
        cq_b = sb("cq_b", [128, 3, 512])
        sq_b = sb("sq_b", [128, 3, 512])
        cqn = sb("cqn", [128, 3, 512])
        rbc = sb("rbc", [128, 512], F32)
        qh = [sb(f"qh{i}", [96, 512]) for i in range(3)]
        t1 = [sb(f"t1_{i}", [128, 512], F32) for i in range(2)]
        t2 = [sb(f"t2_{i}", [128, 512], F32) for i in range(2)]
        pbf = [sb(f"pbf{i}", [128, 512]) for i in range(3)]
        ob = [sb(f"ob{i}", [128, 512]) for i in range(3)]
        pacc = [ps(f"pacc{i}", [128, 512]) for i in range(3)]
        prot = [ps(f"prot{i}", [128, 512]) for i in range(2)]
        pssq = ps("pssq", [128, 512])
        ptr = ps("ptr", [128, 1024], BF16)

        for i, (dst, src) in enumerate([(c96, c96d), (s96, s96d), (c128, c128d), (s128, s128d),
                                        (r96, r96d), (r128, r128d), (ident, identd), (ng8, ng), (qg3, qg)]):
            S.dma("sp", dst[:], src, writes=[dst.name], slot=("c", i))
        S.op("pool", lambda e: e.memset(ones[:], 1.0), writes=["ones"])
        S.op("pool", lambda e: e.memset(wkr[:], 0.0), writes=[wkr.name])
        S.op("pool", lambda e: e.memset(ssq[:], 0.0), writes=["ssq"])
        S.op("pool", lambda e: e.memset(epst[:], EPS), writes=["epst"])
        S.dma("sp", wuqf[:], w_uq_v, writes=["wuqf"], slot=("wuq",))
        S.op("pool", lambda e: e.tensor_tensor(out=wuqb[:], in0=wuqf[:],
                                               in1=qg3[:, :].unsqueeze(2).broadcast_to([128, 3, 768]), op=ALU.mult),
             reads=["wuqf", qg3.name], writes=["wuqb"])

        def stats(t):
            xs, hbs = xin[t % 2], hb[t % 2]
            S.dma("sp", xs[:], x[t * 128:(t + 1) * 128, :], writes=[xs.name], slot=("xin", t % 2))
            S.op("act", lambda e, xs=xs, t=t: e.activation(out=junk[:], in_=xs[:], func=AF.Square,
                                                           accum_out=ssq[:, t:t + 1]),
                 reads=[xs.name, "ssq"], writes=["junk", ("ssq", t)])
            S.op("act", lambda e, t=t: e.activation(out=rstd[:, t:t + 1], in_=ssq[:, t:t + 1], func=AF.Sqrt,
                                                    scale=1.0 / D, bias=epst[:, 0:1]),
                 reads=[("ssq", t), "epst"], writes=[("rstd", t)])
            S.op("dve", lambda e, t=t: e.reciprocal(out=rstd[:, t:t + 1], in_=rstd[:, t:t + 1]),
                 reads=[("rstd", t)], writes=[("rstd", t)])
            S.op("dve", lambda e, xs=xs, hbs=hbs, t=t: e.tensor_scalar(out=hbs[:], in0=xs[:], scalar1=rstd[:, t:t + 1],
                                                                       scalar2=None, op0=ALU.mult),
                 reads=[xs.name, ("rstd", t)], writes=[hbs.name])

        def transp(t):
            hbs = hb[t % 2]

            def tr(pe, hbs=hbs):
                ins = None
                for c in range(8):
                    ins = pe.transpose(out=ptr[:, c * 128:(c + 1) * 128], in_=hbs[:, c * 128:(c + 1) * 128],
                                       identity=ident[:])
                return ins
            S.op("pe", tr, reads=[hbs.name, ident.name], writes=["ptr"])
            S.op("act", lambda e, t=t: e.activation(out=hT[:, :, t * 128:(t + 1) * 128],
                                                    in_=ptr[:, :].rearrange("p (c n) -> p c n", c=8), func=AF.Copy),
                 reads=["ptr"], writes=[("hT", t // 4)])

        stats(0)
        for t in range(16):
            if t + 1 < 16:
                stats(t + 1)
            transp(t)

        state = {"wg": 0, "pa": 0, "pr": 0, "ob": 0, "pb": 0, "qh": 0}

        def load_dma(k, col0, ncols):
            i = k % 3
            S.dma("sp", wf[i][:, :, 0:ncols], w_in_v[:, :, col0:col0 + ncols], writes=[wf[i].name], slot=("wf", i))

        def load_cast(k, ncols):
            i, j = k % 3, k % 2

            def cast(e):
                ins = None
                for c in range(8):
                    ins = e.activation(out=wb[j][:, c, 0:ncols], in_=wf[i][:, c, 0:ncols], func=AF.Copy,
                                       scale=ng8[:, c:c + 1])
                return ins
            S.op("act", cast, reads=[wf[i].name, ng8.name], writes=[wb[j].name])
            return wb[j]

        def nxt(k, n):
            v = state[k] % n
            state[k] += 1
            return v

        hTk = [("hT", i) for i in range(4)]

        def proj_fm(wt, c0, m, tb, out_ps):
            S.op("pe", mm_group([(wt[:, c, c0:c0 + m], hT[:, c, tb * 512:(tb + 1) * 512]) for c in range(8)],
                                out_ps[0:m, :]),
                 reads=[wt.name, ("hT", tb)], writes=[out_ps.name])

        def rope_store(src_bf, rows, rT, cs, sn, tb, dst_ap, p0=0, wkey=None):
            i = nxt("pr", 2)
            pr = prot[i]
            S.op("pe", mm_group([(rT[0:rows, 0:rows], src_bf[0:rows, :])], pr[0:rows, :]),
                 reads=[src_bf.name, rT.name], writes=[pr.name])
            a, b = t1[i], t2[i]
            tok = slice(tb * 512, (tb + 1) * 512)
            S.op("pool", lambda e: e.tensor_tensor(out=a[p0:rows, :], in0=src_bf[p0:rows, :], in1=cs[p0:rows, tok],
                                                   op=ALU.mult),
                 reads=[src_bf.name, cs.name], writes=[a.name])
            S.op("dve", lambda e: e.tensor_tensor(out=b[p0:rows, :], in0=pr[p0:rows, :], in1=sn[p0:rows, tok],
                                                  op=ALU.mult),
                 reads=[pr.name, sn.name], writes=[b.name])
            S.op("dve", lambda e: e.tensor_tensor(out=src_bf[p0:rows, :], in0=a[p0:rows, :], in1=b[p0:rows, :],
                                                  op=ALU.add),
                 reads=[a.name, b.name], writes=[src_bf.name])
            S.dma("sp", dst_ap, src_bf[dst_rows(p0, rows, dst_ap), :], reads=[src_bf.name],
                  writes=([wkey] if wkey is not None else []), slot=("st", src_bf.name))

        def dst_rows(p0, rows, dst_ap):
            n = dst_ap.shape[0]
            return slice(rows - n, rows)

        def latent_norm(nchunk, width, tb):
            S.op("pe", mm_group([(ones[:, :], sq_b[:, c, :]) for c in range(nchunk)], pssq[:, :]),
                 reads=["sq_b", "ones"], writes=["pssq"])
            S.op("act", lambda e: e.activation(out=rbc[:], in_=pssq[:], func=AF.Sqrt, scale=1.0 / width,
                                               bias=epst[:, 0:1]), reads=["pssq", "epst"], writes=["rbc"])
            S.op("dve", lambda e: e.reciprocal(out=rbc[:], in_=rbc[:]), reads=["rbc"], writes=["rbc"])
            for c in range(nchunk):
                S.op("dve", lambda e, c=c: e.tensor_tensor(out=cqn[:, c, :], in0=cq_b[:, c, :], in1=rbc[:], op=ALU.mult),
                     reads=["cq_b", "rbc"], writes=["cqn"])

        def latent_chunks(wt, nchunk, tb):
            for c3 in range(nchunk):
                pa = pacc[nxt("pa", 3)]
                proj_fm(wt, c3 * 128, 128, tb, pa)
                S.op("act", lambda e, pa=pa, c3=c3: e.activation(out=cq_b[:, c3, :], in_=pa[:], func=AF.Copy),
                     reads=[pa.name], writes=["cq_b"])
                S.op("act", lambda e, pa=pa, c3=c3: e.activation(out=sq_b[:, c3, :], in_=pa[:], func=AF.Square),
                     reads=[pa.name], writes=["sq_b"])

        pend = {"f": None}

        def defer(fn):
            prev = pend["f"]
            pend["f"] = fn
            if prev is not None:
                prev()

        def flush():
            prev = pend["f"]
            pend["f"] = None
            if prev is not None:
                prev()

        def unit_cq(wt):
            for tb in range(4):
                latent_chunks(wt, 3, tb)
                latent_norm(3, 384, tb)
                for h in range(8):
                    pa = pacc[nxt("pa", 3)]
                    q = qh[nxt("qh", 3)]
                    S.op("pe", mm_group([(wuqb[:, c, h * 96:(h + 1) * 96], cqn[:, c, :]) for c in range(3)], pa[0:96, :]),
                         reads=["wuqb", "cqn"], writes=[pa.name])
                    S.op("act", lambda e, pa=pa, q=q: e.activation(out=q[:], in_=pa[0:96, :], func=AF.Copy),
                         reads=[pa.name], writes=[q.name])
                    defer(lambda q=q, tb=tb, h=h: rope_store(q, 96, r96, c96, s96, tb,
                                                             qT_o[h, :, tb * 512:(tb + 1) * 512], p0=64))
                flush()

        def unit_ckv(wt):
            S.op("pool", lambda e: e.tensor_copy(out=wkr[:, :, 64:96], in_=wt[:, :, 256:288]), reads=[wt.name],
                 writes=[wkr.name])
            lat_v = lat_o[0:256, :].rearrange("(c p) t -> p c t", p=128)
            for tb in range(4):
                latent_chunks(wt, 2, tb)
                latent_norm(2, 256, tb)
                S.dma("sp", lat_v[:, :, tb * 512:(tb + 1) * 512], cqn[:, 0:2, :], reads=["cqn"],
                      writes=[("lat_loc", "c", tb)], slot=("st", "cqn"))
                pa = pacc[nxt("pa", 3)]
                q = qh[nxt("qh", 3)]
                proj_fm(wkr, 0, 96, tb, pa)
                S.op("act", lambda e, pa=pa, q=q: e.activation(out=q[:], in_=pa[0:96, :], func=AF.Copy),
                     reads=[pa.name], writes=[q.name])
                defer(lambda q=q, tb=tb: rope_store(q, 96, r96, c96, s96, tb,
                                                    lat_o[256:288, tb * 512:(tb + 1) * 512], p0=64,
                                                    wkey=("lat_loc", "r", tb)))
            flush()

        def gate_blocks(wt, row0):
            for blk in range(4):
                for tb in range(4):
                    pa = pacc[nxt("pa", 3)]
                    o = ob[nxt("ob", 3)]
                    proj_fm(wt, blk * 128, 128, tb, pa)
                    S.op("act", lambda e, pa=pa, o=o: e.activation(out=o[:], in_=pa[:], func=AF.Silu),
                         reads=[pa.name], writes=[o.name])
                    S.dma("sp", gT_o[row0 + blk * 128: row0 + (blk + 1) * 128, tb * 512:(tb + 1) * 512], o[:],
                          reads=[o.name], slot=("st", o.name))

        def rope_post(p, blk, tb, dst, g):
            rope_store(p, 128, r128, c128, s128, tb, dst[blk * 128:(blk + 1) * 128, tb * 512:(tb + 1) * 512])
            if pay is not None and g is not None:
                halo = HALO[g]
                if tb * 512 < halo:
                    n = min(halo - tb * 512, 512)
                    S.dma("sp", pay_k(pay, 0, g, 0, blk * 128, 128, tb * 512, n), p[:, 0:n], reads=[p.name],
                          slot=("st", p.name))
                lo, hi = max(tb * 512, T - halo), tb * 512 + 512
                if lo < hi:
                    S.dma("sp", pay_k(pay, 0, g, 1, blk * 128, 128, lo - (T - halo), hi - lo),
                          p[:, lo - tb * 512:512], reads=[p.name], slot=("st", p.name))

        def rope_blocks(wt, dst, g=None):
            for blk in range(4):
                for tb in range(4):
                    pa = pacc[nxt("pa", 3)]
                    p = pbf[nxt("pb", 3)]
                    proj_fm(wt, blk * 128, 128, tb, pa)
                    S.op("act", lambda e, pa=pa, p=p: e.activation(out=p[:], in_=pa[:], func=AF.Copy),
                         reads=[pa.name], writes=[p.name])
                    defer(lambda p=p, blk=blk, tb=tb: rope_post(p, blk, tb, dst, g))
            flush()

        def v_blocks(wt, dst, g=None):
            for t in range(16):
                pa = pacc[nxt("pa", 3)]
                o = ob[nxt("ob", 3)]
                S.op("pe", mm_group([(hT[:, c, t * 128:(t + 1) * 128], wt[:, c, 0:512]) for c in range(8)], pa[:, :]),
                     reads=[wt.name, ("hT", t // 4)], writes=[pa.name])
                S.op("act", lambda e, pa=pa, o=o: e.activation(out=o[:], in_=pa[:], func=AF.Copy),
                     reads=[pa.name], writes=[o.name])
                S.dma("sp", dst[t * 128:(t + 1) * 128, :], o[:], reads=[o.name], slot=("st", o.name))
                if pay is not None and g is not None:
                    halo = HALO[g]
                    if t * 128 < halo:
                        n = min(halo - t * 128, 128)
                        S.dma("sp", pay_v(pay, 0, g, 0, t * 128, n), o[0:n, :], reads=[o.name], slot=("st", o.name))
                    lo, hi = max(t * 128, T - halo), t * 128 + 128
                    if lo < hi:
                        S.dma("sp", pay_v(pay, 0, g, 1, lo - (T - halo), hi - lo), o[lo - t * 128:128, :],
                              reads=[o.name], slot=("st", o.name))

        units = [(0, 384, unit_cq), (384, 288, unit_ckv)]
        for g in range(3):
            base = 1184 + g * 1536
            units.append((base, 512, lambda wt, g=g: rope_blocks(wt, qd_o[g])))
            units.append((base + 512, 512, lambda wt, g=g: rope_blocks(wt, kd_o[g], g)))
        n_rope_units = len(units)
        units.append((672, 512, lambda wt: gate_blocks(wt, 0)))
        for g in range(3):
            base = 1184 + g * 1536
            units.append((base + 1024, 512, lambda wt, g=g: v_blocks(wt, vd_o[g], g)))
        units.append((5792, 512, lambda wt: gate_blocks(wt, 512)))
        load_dma(0, units[0][0], units[0][1])
        load_dma(1, units[1][0], units[1][1])
        wt = load_cast(0, units[0][1])
        for ui, (c0, ncols, fn) in enumerate(units):
            if ui + 2 < len(units):
                load_dma(ui + 2, units[ui + 2][0], units[ui + 2][1])
            wnext = load_cast(ui + 1, units[ui + 1][1]) if ui + 1 < len(units) else None
            fn(wt)
            wt = wnext
            if ui == n_rope_units - 1 and after_rope is not None:
                after_rope()
        if last:
            S.wait_all("sp")
        else:
            S.barrier()
        S.emit()


_CACHE = {}


def _get(name, builder):
    if name not in _CACHE:
        _CACHE[name] = builder()
    return _CACHE[name]


def run_A(x_shards, w_in_l, ng_l, w_uq_l, qg_l):
    nc = _get("A", build_A)
    in_maps = []
    for c in range(NCORE):
        m = {"x": x_shards[c], "w_in": w_in_l, "ng": ng_l, "w_uq": w_uq_l, "qg": qg_l}
        m.update(_get(("cA", c % 4), lambda: _consts_A(c % 4)))
        in_maps.append(m)
    res = run_bass_kernel_spmd(nc, in_maps, core_ids=list(range(NCORE)))
    return res.results


def build_B(mla_heads=tuple(range(8)), dil_hps=tuple(range(4)), debug=False, dil_groups=(0, 1, 2)):
    nc = bass.Bass("TRN2", target_bir_lowering=False)
    dt_in = lambda n, s, d=F32: nc.dram_tensor(n, s, d, kind="ExternalInput").ap()
    dt_out = lambda n, s, d=F32: nc.dram_tensor(n, s, d, kind="ExternalOutput").ap()
    a = dict(
        x=dt_in("x", [T, D]),
        latT=dt_in("latT", [288, S_FULL], BF16),
        qT=dt_in("qT", [8, 96, T], BF16),
        gT=dt_in("gT", [1024, T], BF16),
        qd=dt_in("qd", [3, 512, T], BF16),
        kd=[dt_in(f"kd{g}", [512, T + 2 * HALO[g]], BF16) for g in range(3)],
        vd=[dt_in(f"vd{g}", [T + 2 * HALO[g], 512], BF16) for g in range(3)],
        w_ukv=dt_in("w_ukv", [256, 1024]),
        kvg=dt_in("kvg", [128, 2]),
        w_out=dt_in("w_out", [D, D]),
        fg=dt_in("fg", [1, D]),
        masks=dt_in("masks", [4, 128, 128], BF16),
        ident=dt_in("ident", [128, 128], BF16),
        e32=dt_in("e32", [32, 96], BF16),
        xo=dt_out("xo", [T, D]),
        yo=dt_out("yo", [T, D]),
    )
    if debug:
        a["mixo"] = nc.dram_tensor("mixo", [1024, T], BF16, kind="ExternalOutput").ap()
    emit_B(nc, mla_heads=mla_heads, dil_hps=dil_hps, dil_groups=dil_groups, **a)
    return nc


def emit_B(nc, x, latT, qT, gT, qd, kd, vd, w_ukv, kvg, w_out, fg, masks, ident, e32, xo, yo,
           mla_heads=tuple(range(8)), dil_hps=tuple(range(4)), mixo=None, dil_groups=(0, 1, 2),
           S=None, psum=None, tag="", fz=None, last=True, final=True):
    with ExitStack() as es:
        if S is None:
            S = Sched(nc, es)
        sb = lambda n, s, d=BF16: es.enter_context(nc.sbuf_tensor(n + tag, s, d))
        if psum is None:
            pb = [es.enter_context(nc.psum_tensor(f"pb{i}", [128, 512], F32)) for i in range(8)]
        else:
            pb = psum
        pS, pAcc, pK = pb[0:3], pb[3:5], pb[5:7]
        if fz is not None:
            selK = sb("selK", [128, 8, 128])
            selVL = sb("selVL", [64, 4, 64])
            selVR = sb("selVR", [64, 4, 128])
            S.dma("sp", selK[:], fz["selK"], writes=["selK"], slot=("c", 4))
            S.dma("sp", selVL[:], fz["selVL"], writes=["selVL"], slot=("c", 5))
            S.dma("sp", selVR[:], fz["selVR"], writes=["selVR"], slot=("c", 6))
        mixedT = sb("mixedT", [128, 8, T])
        identS = sb("identS", [128, 128])
        e32S = sb("e32S", [32, 96])
        maskS = sb("maskS", [128, 4, 128])
        onesf = sb("onesf", [128, 64], F32)
        kvg2 = sb("kvg2", [128, 2], F32)
        epst = sb("epst", [128, 1], F32)
        rd = sb("rd", [128, 512], F32)
        tmp = [sb(f"tmp{i}", [128, 512], F32) for i in range(2)]
        gts = [sb(f"gts{i}", [128, 512]) for i in range(2)]
        S.dma("sp", identS[:], ident, writes=["identS"], slot=("c", 0))
        S.dma("sp", e32S[:], e32, writes=["e32S"], slot=("c", 1))
        S.dma("sp", maskS[:], masks.rearrange("k n m -> n k m"), writes=["maskS"], slot=("c", 2))
        S.dma("sp", kvg2[:], kvg, writes=["kvg2"], slot=("c", 3))
        S.op("pool", lambda e: e.memset(onesf[:], 1.0), writes=["onesf"])
        onesb = sb("onesb", [128, 2])
        S.op("pool", lambda e: e.memset(onesb[:], 1.0), writes=["onesb"])
        S.op("pool", lambda e: e.memset(epst[:], EPS), writes=["epst"])
        m01 = sb("m01", [128, 4, 128])
        m01c = sb("m01c", [128, 4, 2, 128])
        S.op("dve", lambda e: e.tensor_scalar(out=m01[:], in0=maskS[:], scalar1=1.0 / 32768.0, scalar2=1.0,
                                              op0=ALU.mult, op1=ALU.add), reads=["maskS"], writes=["m01"])
        for ci in range(4):
            h0 = 1 if (ci & 1) else 0
            h1 = 3 if (ci & 2) else 2
            S.op("pool", lambda e, ci=ci, h0=h0: e.tensor_copy(out=m01c[:, ci, 0, :], in_=m01[:, h0, :]),
                 reads=["m01"], writes=["m01c"])
            S.op("pool", lambda e, ci=ci, h1=h1: e.tensor_copy(out=m01c[:, ci, 1, :], in_=m01[:, h1, :]),
                 reads=["m01"], writes=["m01c"])
        st = {"tmp": 0, "gts": 0, "acc": 0, "pk": 0}

        def nxt(k, n):
            v = st[k] % n
            st[k] += 1
            return v

        def get_pk():
            return pK[0] if st.get("pk_single") else pK[nxt("pk", 2)]

        def normalize_gate(src_o, src_den, gate_ap, gate_reads, po, chunk, qb, src_reads):
            tok = slice(qb * 512, (qb + 1) * 512)
            pk = get_pk()
            tm = tmp[nxt("tmp", 2)]
            S.op("act", lambda e: e.activation(out=rd[64:65, :], in_=src_den, func=AF.Ln), reads=src_reads, writes=["rd"])
            S.op("act", lambda e: e.activation(out=rd[64:65, :], in_=rd[64:65, :], func=AF.Exp, scale=-1.0),
                 reads=["rd"], writes=["rd"])
            S.op("pe", mm_group([(onesf[64:65, 0:64], rd[64:65, :])], pk[0:64, :]), reads=["rd", "onesf"],
                 writes=[pk.name])
            S.op("dve", lambda e: e.tensor_tensor(out=tm[po:po + 64, :], in0=src_o, in1=pk[0:64, :], op=ALU.mult),
                 reads=src_reads + [pk.name], writes=[tm.name])
            S.op("dve", lambda e: e.tensor_tensor(out=mixedT[po:po + 64, chunk, tok], in0=tm[po:po + 64, :],
                                                  in1=gate_ap, op=ALU.mult),
                 reads=[tm.name] + gate_reads, writes=[("mixedT", chunk)])

        with ExitStack() as es1:
            sb1 = lambda n, s, d=BF16: es1.enter_context(nc.sbuf_tensor(n + tag, s, d))
            lat = sb1("lat", [128, 3, S_FULL])
            KT = sb1("KT", [96, S_FULL])
            V = sb1("V", [128, 64, 65])
            Qh = [sb1(f"Qh{i}", [96, T]) for i in range(2)]
            PT = [sb1(f"PT{i}", [128, 512]) for i in range(3)]
            osb = sb1("osb", [64, 512], F32)
            wukvf = sb1("wukvf", [128, 2, 1024], F32)
            wk96 = sb1("wk96", [128, 2, 8, 96])
            wv = sb1("wv", [128, 2, 8, 64])
            S.dma("sp", wukvf[:], w_ukv.rearrange("(c p) n -> p c n", p=128), writes=["wukvf"], slot=("wukv",))
            S.op("pool", lambda e: e.memset(wk96[:], 0.0), writes=["wk96"])
            S.op("pool", lambda e: e.memset(V[:], 1.0), writes=["V"])
            for c in range(2):
                wsrc = wukvf[:, c, :].rearrange("p (h k) -> p h k", h=8)
                S.op("pool", lambda e, c=c, wsrc=wsrc: e.tensor_tensor(
                    out=wk96[:, c, :, 0:64], in0=wsrc[:, :, 0:64],
                    in1=kvg2[:, c:c + 1].unsqueeze(2).broadcast_to([128, 8, 64]), op=ALU.mult),
                    reads=["wukvf", "kvg2"], writes=["wk96"])
                S.op("pool", lambda e, c=c, wsrc=wsrc: e.tensor_tensor(
                    out=wv[:, c, :, :], in0=wsrc[:, :, 64:128],
                    in1=kvg2[:, c:c + 1].unsqueeze(2).broadcast_to([128, 8, 64]), op=ALU.mult),
                    reads=["wukvf", "kvg2"], writes=["wv"])
            if fz is not None and fz.get("after_prologue") is not None:
                fz["after_prologue"]()
            for i in range(4):
                sl = slice(i * 2048, (i + 1) * 2048)
                if fz is None:
                    lat_v = latT[0:256, :].rearrange("(c p) t -> p c t", p=128)
                    S.dma("sp", lat[:, 0:2, sl], lat_v[:, :, sl], writes=[("lat", i)], slot=("lat", i))
                    S.dma("sp", lat[0:32, 2, sl], latT[256:288, sl], writes=[("latr", i)], slot=("latr", i))
                else:
                    S.dma("sp", lat[:, 0:2, sl],
                          fz["lat_allA"][i * 256:(i + 1) * 256, :].rearrange("(c p) t -> p c t", p=128),
                          reads=["lat_allA"], writes=[("lat", i)], slot=("lat", i))
                    S.dma("sp", lat[0:32, 2, sl], fz["lat_allB"][i * 32:(i + 1) * 32, :], reads=["lat_allB"],
                          writes=[("latr", i)], slot=("latr", i))

            for h in mla_heads:
                po, chunk = (h % 2) * 64, h // 2
                Q = Qh[h % 2]
                S.dma("sp", Q[:], qT[h], writes=[Q.name], slot=("Q", h % 2))
                for tb in range(16):
                    pk = pK[nxt("pk", 2)]
                    sl = slice(tb * 512, (tb + 1) * 512)
                    S.op("pe", mm_group([(wk96[:, 0, h, :], lat[:, 0, sl]), (wk96[:, 1, h, :], lat[:, 1, sl]),
                                         (e32S[0:32, :], lat[0:32, 2, sl])], pk[0:96, :]),
                         reads=["wk96", "e32S", ("lat", tb // 4), ("latr", tb // 4)], writes=[pk.name])
                    eng = "dve" if tb % 2 else "act"
                    if eng == "act":
                        S.op("act", lambda e, pk=pk, sl=sl: e.activation(out=KT[:, sl], in_=pk[0:96, :], func=AF.Copy),
                             reads=[pk.name], writes=[("KT", tb)])
                    else:
                        S.op("dve", lambda e, pk=pk, sl=sl: e.tensor_copy(out=KT[:, sl], in_=pk[0:96, :]),
                             reads=[pk.name], writes=[("KT", tb)])
                for k8 in range(8):
                    pk = pK[nxt("pk", 2)]

                    def vb(pe, pk=pk, k8=k8, h=h):
                        ins = None
                        for j in range(8):
                            kt = k8 * 8 + j
                            for c in range(2):
                                ins = pe.matmul(pk[:, j * 64:(j + 1) * 64], lhsT=lat[:, c, kt * 128:(kt + 1) * 128],
                                                rhs=wv[:, c, h, :], start=(c == 0), stop=(c == 1))
                        return ins
                    S.op("pe", vb, reads=["wv", ("lat", k8 // 2)], writes=[pk.name])
                    S.op("dve", lambda e, pk=pk, k8=k8: e.tensor_copy(
                        out=V[:, k8 * 8:(k8 + 1) * 8, 0:64], in_=pk[:, :].rearrange("p (j d) -> p j d", j=8)),
                        reads=[pk.name], writes=[("V", k8)])
                steps = [(qb, kt) for qb in range(4) for kt in range(64)]
                accs, gidx = {}, {}
                for qb in range(4):
                    accs[qb] = pAcc[nxt("acc", 2)]
                base = st.get("sstep", 0)

                def qk(n):
                    qb, kt = steps[n]
                    s_ = pS[(base + n) % 3]
                    S.op("pe", mm_group([(KT[:, kt * 128:(kt + 1) * 128], Q[:, qb * 512:(qb + 1) * 512])], s_[:, :]),
                         reads=[("KT", kt // 4), Q.name], writes=[s_.name])
                qk(0)
                qk(1)
                for n, (qb, kt) in enumerate(steps):
                    s_, p_ = pS[(base + n) % 3], PT[(base + n) % 3]
                    acc = accs[qb]
                    if kt == 0:
                        g_i = nxt("gts", 2)
                        gidx[qb] = g_i
                        S.dma("sp", gts[g_i][po:po + 64, :], gT[h * 64:(h + 1) * 64, qb * 512:(qb + 1) * 512],
                              writes=[gts[g_i].name], slot=("gts", g_i))
                    S.op("act", lambda e, s_=s_, p_=p_: e.activation(out=p_[:], in_=s_[:], func=AF.Exp,
                                                                     scale=MLA_SCALE),
                         reads=[s_.name], writes=[p_.name])
                    if n + 2 < len(steps):
                        qk(n + 2)
                    S.op("pe", lambda pe, kt=kt, p_=p_, acc=acc: pe.matmul(
                        acc[0:65, :], lhsT=V[:, kt, :], rhs=p_[:], start=(kt == 0), stop=(kt == 63)),
                        reads=[p_.name, ("V", kt // 8)], writes=[acc.name])
                    if kt == 63:
                        g_i = gidx[qb]
                        S.op("act", lambda e, acc=acc: e.activation(out=osb[:], in_=acc[0:64, :], func=AF.Copy),
                             reads=[acc.name], writes=["osb"])
                        normalize_gate(osb[0:64, :], acc[64:65, :], gts[g_i][po:po + 64, :], [gts[g_i].name], po, chunk,
                                       qb, ["osb", acc.name])
                st["sstep"] = base + len(steps)
            if fz is not None and fz.get("lato") is not None:
                S.dma("sp", fz["lato"], lat[:], reads=[("lat", i) for i in range(4)] + [("latr", i) for i in range(4)],
                      slot=("lato",))
            S.barrier()
            S.emit()

        with ExitStack() as es2:
            sb2 = lambda n, s, d=BF16: es2.enter_context(nc.sbuf_tensor(n + tag, s, d))
            Kg = [sb2(f"Kg{i}", [128, T + 2 * HALO[2]]) for i in range(2)]
            Qg = [sb2(f"Qg{i}", [128, T]) for i in range(2)]
            Vraw = [sb2(f"Vraw{i}", [128, 32, 128]) for i in range(2)]
            accT = sb2("accT", [65, 2, T], F32)
            gtb = sb2("gtb", [128, T])
            P2p = [sb2(f"P2p{i}", [128, 4, 128]) for i in range(4)]
            Sbanks = [pS[0], pS[1], pS[2], pK[1]]
            st["pk_single"] = True
            if fz is not None:
                candK = [sb2(f"candK{i}", [128, 4, HALO[2]]) for i in range(2)]
                candV = [sb2(f"candV{i}", [64, 4, 16, 128]) for i in range(2)]
            jobs = [(hp, g) for hp in dil_hps for g in dil_groups]

            jinfo = {}

            sel_ops = {}

            def stage(n):
                sel = []
                hp, g = jobs[n]
                d, halo = DILS[g], HALO[g]
                W = T + 2 * halo
                nbr = 16 // d + 1
                njb = 16 // d
                bi = n % 2
                kg, qg_, vr = Kg[bi], Qg[bi], Vraw[bi]
                kkeys, vkeys = [(kg.name, "own")], []
                S.dma("sp", qg_[:], qd[g, hp * 128:(hp + 1) * 128, :], writes=[qg_.name], slot=("qg", bi))
                cols = slice(hp * 128, (hp + 1) * 128)
                if fz is None:
                    S.dma("sp", kg[:, 0:W], kd[g][hp * 128:(hp + 1) * 128, :], writes=[(kg.name, "own")], slot=("kg", bi))
                    vsrc = vd[g].rearrange("(kb n r) c -> n r kb c", n=128, r=d)
                    for r in range(d):
                        vkeys.append((vr.name, "o", r, 0))
                        S.dma("sp", vr[:, r * nbr:(r + 1) * nbr, :], vsrc[:, r, :, cols],
                              writes=[vkeys[-1]], slot=("vr", bi))
                else:
                    pall = fz["pay_all"]
                    pkeys = fz["pay_keys"](g)
                    S.dma("sp", kg[:, halo:halo + T], kd[g][hp * 128:(hp + 1) * 128, :], writes=[(kg.name, "own")],
                          slot=("kg", bi))
                    vown = vd[g].rearrange("(kb two n r) c -> two n r kb c", two=2, n=64, r=d)
                    for r in range(d):
                        vkeys.append((vr.name, "o", r, 0))
                        S.dma("sp", vr[64:128, r * nbr:r * nbr + njb, :], vown[0][:, r, :, cols],
                              writes=[vkeys[-1]], slot=("vr", bi))
                        vkeys.append((vr.name, "o", r, 1))
                        S.dma("sp", vr[0:64, r * nbr + 1:r * nbr + 1 + njb, :], vown[1][:, r, :, cols],
                              writes=[vkeys[-1]], slot=("vr", bi))
                    for side in range(2):
                        piece = 1 - side
                        ck = candK[side]
                        S.dma("sp", ck[:, :, 0:halo], pay_k(pall, 0, g, piece, hp * 128, 128, 0, halo, ncand=4),
                              reads=pkeys, writes=[ck.name], slot=("candK", side))
                        cv = candV[side]
                        cvkeys = [(cv.name, j) for j in range(4)]
                        for j in range(4):
                            S.dma("sp", cv[:, j, 0:d, :], pay_v_cand(pall, j, g, piece, hp, d),
                                  reads=pkeys, writes=[cvkeys[j]], slot=("candV", side))
                        dst0 = 0 if side == 0 else halo + T
                        for c0 in range(0, halo, 512):
                            nn = min(512, halo - c0)
                            kkeys.append((kg.name, "h", side, c0))

                            def selk(side=side, ck=ck, c0=c0, nn=nn, dst0=dst0, kg=kg, key=kkeys[-1]):
                                pk = get_pk()
                                S.op("pe", mm_group([(selK[:, side * 4 + j, :], ck[:, j, c0:c0 + nn]) for j in range(4)],
                                                    pk[:, 0:nn]), reads=["selK", ck.name], writes=[pk.name])
                                S.op("act", lambda e: e.activation(out=kg[:, dst0 + c0:dst0 + c0 + nn], in_=pk[:, 0:nn],
                                                                   func=AF.Copy), reads=[pk.name], writes=[key])
                            sel.append(selk)
                        for r0 in range(0, d, 4):
                            nr = min(4, d - r0)
                            if side == 0:
                                lhs = [selVL[:, j, :] for j in range(4)]
                                prow, kbs, mrows = slice(0, 64), 0, 64
                            else:
                                lhs = [selVR[:, j, :] for j in range(4)]
                                prow, kbs, mrows = slice(64, 128), nbr - 1, 128
                            vkeys.append((vr.name, "h", side, r0))

                            def selv(lhs=lhs, cv=cv, r0=r0, nr=nr, prow=prow, kbs=kbs, mrows=mrows, vr=vr, nbr=nbr,
                                     cvkeys=cvkeys, key=vkeys[-1]):
                                pk = get_pk()
                                S.op("pe", mm_group([(lhs[j], cv[:, j, r0:r0 + nr, :]) for j in range(4)],
                                                    pk[0:mrows, 0:nr * 128]), reads=["selVL", "selVR"] + cvkeys,
                                     writes=[pk.name])
                                S.op("dve", lambda e: e.tensor_copy(
                                    out=vr[prow, r0 * nbr + kbs:(r0 + nr - 1) * nbr + kbs + 1:nbr, :],
                                    in_=pk[prow, 0:nr * 128].rearrange("p (r c) -> p r c", r=nr)),
                                    reads=[pk.name], writes=[key])
                            sel.append(selv)
                jinfo[n] = (kkeys, vkeys)
                sel_ops[n] = sel

            def run_sel(n):
                for f in sel_ops.pop(n, []):
                    f()

            def compute(n, mid=None):
                hp, g = jobs[n]
                d = DILS[g]
                nbr = 16 // d + 1
                njb = 16 // d
                bi = n % 2
                kg, qg_, vr = Kg[bi], Qg[bi], Vraw[bi]
                kkeys, vkeys = jinfo[n]
                first = (g == dil_groups[0])
                pbase = st.get("pbase", 0)
                accv = [accT[:, hl, :].rearrange("p (jb m r) -> p r jb m", r=d, m=128) for hl in range(2)]
                tiles = []
                for hl in range(2):
                    if d == 1:
                        quads = [[(0, 4 * q + j) for j in range(4)] for q in range(4)]
                        dsts = [accv[hl][:, 0, 4 * q:4 * q + 4, :] for q in range(4)]
                    elif d == 4:
                        quads = [[(r, j) for j in range(4)] for r in range(4)]
                        dsts = [accv[hl][:, r, 0:4, :] for r in range(4)]
                    else:
                        quads = [[(4 * q + j, 0) for j in range(4)] for q in range(4)]
                        dsts = [accv[hl][:, 4 * q:4 * q + 4, 0, :] for q in range(4)]
                    for quad, dst in zip(quads, dsts):
                        for j, (r, jb) in enumerate(quad):
                            tiles.append((hl, r, jb, j, dst))

                pairs = [(tiles[2 * k], tiles[2 * k + 1]) for k in range(len(tiles) // 2)]
                LA = 3

                def qkpair(p):
                    bank = Sbanks[(pbase + p) % 4]
                    mms = []
                    for t, (hl, r, jb, j, dst) in enumerate(pairs[p]):
                        po = hl * 64
                        for half in range(2):
                            kb = jb + half
                            k0 = r + d * kb * 128
                            q0 = r + d * jb * 128
                            mms.append((bank[:, (2 * t + half) * 128:(2 * t + half + 1) * 128],
                                        kg[po:po + 64, k0:k0 + d * 127 + 1:d], qg_[po:po + 64, q0:q0 + d * 127 + 1:d]))

                    def fn(pe):
                        ins = None
                        for o_, kap, qap in mms:
                            ins = pe.matmul(o_, lhsT=kap, rhs=qap, start=True, stop=True)
                        return ins
                    S.op("pe", fn, reads=kkeys + [qg_.name], writes=[bank.name])
                for p in range(min(LA, len(pairs))):
                    qkpair(p)
                for p, pr_ in enumerate(pairs):
                    bank, pp = Sbanks[(pbase + p) % 4], P2p[(pbase + p) % 4]
                    S.op("act", lambda e, bank=bank, pp=pp: e.activation(
                        out=pp[:, :, :], in_=bank[:, :].rearrange("p (a m) -> p a m", a=4), func=AF.Exp,
                        scale=DIL_SCALE), reads=[bank.name], writes=[(pp.name, 0), (pp.name, 1)])
                    for t, (hl, r, jb, j, dst) in enumerate(pr_):
                        ci = (1 if jb == 0 else 0) + (2 if jb == njb - 1 else 0)
                        S.op("dve", lambda e, pp=pp, ci=ci, t=t: e.tensor_tensor(
                            out=pp[:, 2 * t:2 * t + 2, :], in0=pp[:, 2 * t:2 * t + 2, :], in1=m01c[:, ci, :, :],
                            op=ALU.mult), reads=[(pp.name, t), "m01c"], writes=[(pp.name, t)])
                    if p + LA < len(pairs):
                        qkpair(p + LA)
                    if mid is not None and p == len(pairs) // 2:
                        mid()
                    hl0, j1 = pr_[0][0], pr_[1][3]
                    pv = pAcc[((2 * p) // 4) % 2]

                    def pvf(pe, pr_=pr_, pp=pp, pv=pv, vr=vr, nbr=nbr):
                        ins = None
                        for t, (hl, r, jb, j, dst) in enumerate(pr_):
                            for half in range(2):
                                pe.matmul(pv[0:64, j * 128:(j + 1) * 128],
                                          lhsT=vr[:, r * nbr + jb + half, hl * 64:(hl + 1) * 64],
                                          rhs=pp[:, 2 * t + half, :], start=(half == 0), stop=(half == 1))
                            for half in range(2):
                                ins = pe.matmul(pv[64:65, j * 128:(j + 1) * 128], lhsT=onesb[:, 0:1],
                                                rhs=pp[:, 2 * t + half, :], start=(half == 0), stop=(half == 1))
                        return ins
                    S.op("pe", pvf, reads=[(pp.name, 0), (pp.name, 1), "onesb"] + vkeys, writes=[pv.name])
                    if j1 == 3:
                        dst = pr_[1][4]
                        src = pv[0:65, :].rearrange("p (a m) -> p a m", a=4)
                        if first:
                            S.op("dve", lambda e, dst=dst, src=src: e.tensor_copy(out=dst, in_=src),
                                 reads=[pv.name], writes=[("accT", hl0)])
                        else:
                            S.op("dve", lambda e, dst=dst, src=src: e.tensor_tensor(out=dst, in0=src, in1=dst,
                                                                                    op=ALU.add),
                                 reads=[pv.name, ("accT", hl0)], writes=[("accT", hl0)])
                st["pbase"] = pbase + len(pairs)

            if jobs:
                stage(0)
                run_sel(0)
            for n, (hp, g) in enumerate(jobs):
                if g == dil_groups[0]:
                    S.dma("sp", gtb[:], gT[512 + hp * 128: 512 + (hp + 1) * 128, :], writes=["gtb"], slot=("gtb",))
                if n + 1 < len(jobs):
                    stage(n + 1)
                compute(n, mid=(lambda n=n: run_sel(n + 1)))
                if g == dil_groups[-1]:
                    for hl in range(2):
                        po = hl * 64
                        for qb in range(4):
                            tok = slice(qb * 512, (qb + 1) * 512)
                            normalize_gate(accT[0:64, hl, tok], accT[64:65, hl, tok], gtb[po:po + 64, tok], ["gtb"], po,
                                           4 + hp, qb, [("accT", hl)])
            S.barrier()
            S.emit()

        with ExitStack() as es3:
            sb3 = lambda n, s, d=BF16: es3.enter_context(nc.sbuf_tensor(n + tag, s, d))
            wof = sb3("wof", [128, 8, 512], F32)
            wob = sb3("wob", [128, 8, D])
            fgb = sb3("fgb", [128, D], F32)
            xin = [sb3(f"xin{i}", [128, D], F32) for i in range(2)]
            xn = [sb3(f"xn{i}", [128, D], F32) for i in range(2)]
            yv = [sb3(f"yv{i}", [128, D], F32) for i in range(2)]
            junk = sb3("junk", [128, D], F32)
            ssq = sb3("ssq", [128, 16], F32)
            rstd = sb3("rstd", [128, 16], F32)
            S.op("pool", lambda e: e.memset(ssq[:], 0.0), writes=["ssq"])
            S.dma("sp", fgb[:], fg.partition_broadcast(128), writes=["fgb"], slot=("fgb",))
            w_out_v = w_out.rearrange("(c p) n -> p c n", p=128)
            if mixo is not None:
                S.dma("pool", mixo.rearrange("(c p) t -> p c t", p=128), mixedT[:], reads=[("mixedT", c) for c in range(8)],
                      slot=("mixo",))
            for nb in range(2):
                S.dma("sp", wof[:], w_out_v[:, :, nb * 512:(nb + 1) * 512], writes=["wof"], slot=("wof",))
                S.op("pool", lambda e, nb=nb: e.tensor_copy(out=wob[:, :, nb * 512:(nb + 1) * 512], in_=wof[:]),
                     reads=["wof"], writes=["wob"])
            for t in range(16):
                xs, xnn, yy = xin[t % 2], xn[t % 2], yv[t % 2]
                S.dma("sp", xs[:], x[t * 128:(t + 1) * 128, :], writes=[xs.name], slot=("xin", t % 2))
                for nb in range(2):
                    pa = pS[(2 * t + nb) % 3]
                    S.op("pe", mm_group([(mixedT[:, c, t * 128:(t + 1) * 128], wob[:, c, nb * 512:(nb + 1) * 512])
                                         for c in range(8)], pa[:, :]),
                         reads=["wob"] + [("mixedT", c) for c in range(8)], writes=[pa.name])
                    S.op("dve", lambda e, pa=pa, nb=nb, xs=xs, xnn=xnn: e.tensor_tensor(
                        out=xnn[:, nb * 512:(nb + 1) * 512], in0=pa[:, :], in1=xs[:, nb * 512:(nb + 1) * 512], op=ALU.add),
                        reads=[pa.name, xs.name], writes=[xnn.name])
                S.dma("pool", xo[t * 128:(t + 1) * 128, :], xnn[:], reads=[xnn.name], slot=("xo", t % 2))
                if not final:
                    continue
                S.op("act", lambda e, xnn=xnn, t=t: e.activation(out=junk[:], in_=xnn[:], func=AF.Square,
                                                                 accum_out=ssq[:, t:t + 1]),
                     reads=[xnn.name, "ssq"], writes=["junk", ("ssq", t)])
                S.op("act", lambda e, t=t: e.activation(out=rstd[:, t:t + 1], in_=ssq[:, t:t + 1], func=AF.Sqrt,
                                                        scale=1.0 / D, bias=epst[:, 0:1]),
                     reads=[("ssq", t), "epst"], writes=[("rstd", t)])
                S.op("dve", lambda e, t=t: e.reciprocal(out=rstd[:, t:t + 1], in_=rstd[:, t:t + 1]),
                     reads=[("rstd", t)], writes=[("rstd", t)])
                S.op("dve", lambda e, xnn=xnn, yy=yy, t=t: e.scalar_tensor_tensor(
                    out=yy[:], in0=xnn[:], scalar=rstd[:, t:t + 1], in1=fgb[:], op0=ALU.mult, op1=ALU.mult),
                    reads=[xnn.name, ("rstd", t), "fgb"], writes=[yy.name])
                S.dma("pool", yo[t * 128:(t + 1) * 128, :], yy[:], reads=[yy.name], slot=("yo", t % 2))
            if last:
                S.wait_all("sp")
            else:
                S.barrier()
            S.emit()


def run_B(in_maps):
    nc = _get("B", build_B)
    res = run_bass_kernel_spmd(nc, in_maps, core_ids=list(range(NCORE)))
    return res.results


def _exchange(resA, x_shards, w_ukv_l, kvg_l, w_out_l, fg):
    in_maps = []
    for b in range(2):
        cores = [4 * b + r for r in range(4)]
        lat_full = np.concatenate([np.asarray(resA[c]["lat"]) for c in cores], axis=1)
        kd_full = np.concatenate([np.asarray(resA[c]["kd"]) for c in cores], axis=2)
        vd_full = np.concatenate([np.asarray(resA[c]["vd"]) for c in cores], axis=1)
        for r in range(4):
            c = cores[r]
            m = {"x": x_shards[c], "latT": lat_full, "qT": np.asarray(resA[c]["qT"]), "gT": np.asarray(resA[c]["gT"]),
                 "qd": np.asarray(resA[c]["qd"]), "w_ukv": w_ukv_l, "kvg": kvg_l, "w_out": w_out_l, "fg": fg}
            for g in range(3):
                hl = HALO[g]
                kp = np.zeros((512, S_FULL + 2 * hl), dtype=kd_full.dtype)
                kp[:, hl:hl + S_FULL] = kd_full[g]
                vp = np.zeros((S_FULL + 2 * hl, 512), dtype=vd_full.dtype)
                vp[hl:hl + S_FULL] = vd_full[g]
                m[f"kd{g}"] = np.ascontiguousarray(kp[:, r * T: r * T + T + 2 * hl])
                m[f"vd{g}"] = np.ascontiguousarray(vp[r * T: r * T + T + 2 * hl])
            m.update(_get(("cB", r), lambda: _consts_B(r)))
            in_maps.append(m)
    return in_maps


def kernel_unfused(x, norm_g, w_in, q_norm_g, kv_norm_g, w_uq, w_ukv, w_out, final_g):
    x = np.asarray(x, dtype=np.float32)
    xs = [np.ascontiguousarray(x[c // 4, (c % 4) * T:(c % 4 + 1) * T]) for c in range(NCORE)]
    fg = np.ascontiguousarray(np.asarray(final_g, np.float32).reshape(1, D))
    ys = None
    for l in range(DEPTH):
        ng8 = np.ascontiguousarray(np.asarray(norm_g[l], np.float32).reshape(8, 128).T)
        qg3 = np.ascontiguousarray(np.asarray(q_norm_g[l], np.float32).reshape(3, 128).T)
        kvg2 = np.ascontiguousarray(np.asarray(kv_norm_g[l], np.float32).reshape(2, 128).T)
        resA = run_A(xs, np.ascontiguousarray(w_in[l], dtype=np.float32), ng8,
                     np.ascontiguousarray(w_uq[l], dtype=np.float32), qg3)
        in_maps = _exchange(resA, xs, np.ascontiguousarray(w_ukv[l], dtype=np.float32), kvg2,
                            np.ascontiguousarray(w_out[l], dtype=np.float32), fg)
        resB = run_B(in_maps)
        xs = [np.asarray(resB[c]["xo"]) for c in range(NCORE)]
        ys = [np.asarray(resB[c]["yo"]) for c in range(NCORE)]
    out = np.zeros((2, S_FULL, D), np.float32)
    for c in range(NCORE):
        out[c // 4, (c % 4) * T:(c % 4 + 1) * T] = ys[c]
    return out


def _consts_F(rank):
    c = dict(_consts_A(rank))
    c.update(_consts_B(rank))
    eye = np.eye(128, dtype=np.float32)
    selK = np.zeros((128, 8, 128), np.float32)
    selVL = np.zeros((64, 4, 64), np.float32)
    selVR = np.zeros((64, 4, 128), np.float32)
    for j in range(4):
        if j == rank - 1:
            selK[:, j, :] = eye
            selVL[:, j, :] = np.eye(64, dtype=np.float32)
        if j == rank + 1:
            selK[:, 4 + j, :] = eye
            selVR[:, j, 64:128] = np.eye(64, dtype=np.float32)
    c["selK"] = selK.astype(NPBF)
    c["selVL"] = selVL.astype(NPBF)
    c["selVR"] = selVR.astype(NPBF)
    return c


def build_fused(depth=DEPTH, debug=False):
    nc = bass.Bass("TRN2", target_bir_lowering=False)
    dt_in = lambda n, s, d=F32: nc.dram_tensor(n, s, d, kind="ExternalInput").ap()
    x = dt_in("x", [T, D])
    w_in = dt_in("w_in", [DEPTH, D, INW])
    ng = dt_in("ng", [DEPTH, 128, 8])
    w_uq = dt_in("w_uq", [DEPTH, 384, 768])
    qg = dt_in("qg", [DEPTH, 128, 3])
    w_ukv = dt_in("w_ukv", [DEPTH, 256, 1024])
    kvg = dt_in("kvg", [DEPTH, 128, 2])
    w_out = dt_in("w_out", [DEPTH, D, D])
    fg = dt_in("fg", [1, D])
    c96 = dt_in("c96", [96, T])
    s96 = dt_in("s96", [96, T])
    c128 = dt_in("c128", [128, T])
    s128 = dt_in("s128", [128, T])
    r96 = dt_in("r96", [96, 96], BF16)
    r128 = dt_in("r128", [128, 128], BF16)
    ident = dt_in("ident", [128, 128], BF16)
    masks = dt_in("masks", [4, 128, 128], BF16)
    e32 = dt_in("e32", [32, 96], BF16)
    selK = dt_in("selK", [128, 8, 128], BF16)
    selVL = dt_in("selVL", [64, 4, 64], BF16)
    selVR = dt_in("selVR", [64, 4, 128], BF16)
    yo = nc.dram_tensor("yo", [T, D], F32, kind="ExternalOutput").ap()
    mixo = nc.dram_tensor("mixo", [1024, T], BF16, kind="ExternalOutput").ap() if debug else None
    lato = nc.dram_tensor("lato", [128, 3, S_FULL], BF16, kind="ExternalOutput").ap() if debug else None
    xb = [nc.dram_tensor(f"xb{i}", [T, D], F32).ap() for i in range(2)]
    lat_loc = [nc.dram_tensor(f"lat_loc{i}", [288, T], BF16).ap() for i in range(2)]
    lat_allA = [nc.dram_tensor(f"lat_allA{i}", [4 * 256, T], BF16).ap() for i in range(2)]
    lat_allB = [nc.dram_tensor(f"lat_allB{i}", [4 * 32, T], BF16).ap() for i in range(2)]
    pay_h = [pay_tensors(nc, f"pay{i}", 1) for i in range(2)]
    pay_all_h = [pay_tensors(nc, f"pay_all{i}", 4) for i in range(2)]
    qT_s = nc.dram_tensor("qT_s", [8, 96, T], BF16).ap()
    gT_s = nc.dram_tensor("gT_s", [1024, T], BF16).ap()
    qd_s = nc.dram_tensor("qd_s", [3, 512, T], BF16).ap()
    kd_s = nc.dram_tensor("kd_s", [3, 512, T], BF16).ap()
    vd_s = nc.dram_tensor("vd_s", [3, T, 512], BF16).ap()
    groups = [[0, 1, 2, 3], [4, 5, 6, 7]]
    with ExitStack() as es:
        S = Sched(nc, es)
        pb = [es.enter_context(nc.psum_tensor(f"pb{i}", [128, 512], F32)) for i in range(7)]
        ptr = es.enter_context(nc.psum_tensor("ptr", [128, 1024], BF16))
        psA = {"pacc0": pb[0], "pacc1": pb[1], "pacc2": pb[2], "prot0": pb[3], "prot1": pb[4], "pssq": pb[5],
               "ptr": ptr}
        for l in range(depth):
            xin = x if l == 0 else xb[(l - 1) % 2]
            b = l % 2
            def ag(src, dst, key, ci, reads=()):
                S.custom("pool", lambda e, src=src, dst=dst: e.collective_compute(
                    "AllGather", ALU.bypass, replica_groups=groups, ins=[src], outs=[dst]), slot=("cc", ci),
                    reads=list(reads), writes=[key])

            def lat_ags(b=b):
                ag(lat_loc[b][0:256, :], lat_allA[b][:, :], "lat_allA", 0, [("lat_loc", "c", tb) for tb in range(4)])
                ag(lat_loc[b][256:288, :], lat_allB[b][:, :], "lat_allB", 1, [("lat_loc", "r", tb) for tb in range(4)])
            emit_A(nc, xin, w_in[l], ng[l], w_uq[l], qg[l], c96, s96, c128, s128, r96, r128, ident,
                   lat_loc[b], qT_s, gT_s, qd_s, [kd_s[g] for g in range(3)], [vd_s[g] for g in range(3)],
                   S=S, psum=psA, pay=pay_h[b], tag=f"_A{l}", last=False, after_rope=lat_ags)

            def halo_ags(b=b):
                for ci, key in enumerate(pay_h[b]):
                    ag(pay_h[b][key].ap()[:, :], pay_all_h[b][key].ap()[:, :], "pay_all_" + "_".join(map(str, key)), 2 + ci)
            pay_keys = lambda g: (["pay_all_2_%d" % k for k in range(4)] if g == 2 else ["pay_all_%d" % g])
            lastl = (l == depth - 1)
            hook = halo_ags
            if not OVERLAP_EXCHANGE:
                halo_ags()
                S.barrier()
                hook = None
            emit_B(nc, xin, None, qT_s, gT_s, qd_s, kd_s, vd_s, w_ukv[l], kvg[l], w_out[l], fg, masks, ident, e32,
                   xb[l % 2], yo, S=S, psum=pb, tag=f"_B{l}", mixo=(mixo if l == depth - 1 else None),
                   fz={"lat_allA": lat_allA[b], "lat_allB": lat_allB[b], "pay_all": pay_all_h[b], "after_prologue": hook, "lato": lato,
                       "pay_keys": pay_keys, "selK": selK, "selVL": selVL, "selVR": selVR},
                   last=lastl, final=lastl)
    return nc


def kernel(x, norm_g, w_in, q_norm_g, kv_norm_g, w_uq, w_ukv, w_out, final_g):
    f32c = lambda a: np.ascontiguousarray(np.asarray(a, dtype=np.float32))
    x = f32c(x)
    shared = {
        "w_in": f32c(w_in), "w_uq": f32c(w_uq), "w_ukv": f32c(w_ukv), "w_out": f32c(w_out),
        "ng": f32c(np.asarray(norm_g, np.float32).reshape(DEPTH, 8, 128).transpose(0, 2, 1)),
        "qg": f32c(np.asarray(q_norm_g, np.float32).reshape(DEPTH, 3, 128).transpose(0, 2, 1)),
        "kvg": f32c(np.asarray(kv_norm_g, np.float32).reshape(DEPTH, 2, 128).transpose(0, 2, 1)),
        "fg": f32c(np.asarray(final_g, np.float32).reshape(1, D)),
    }
    nc = _get("F", build_fused)
    in_maps = []
    for c in range(NCORE):
        m = dict(shared)
        m["x"] = np.ascontiguousarray(x[c // 4, (c % 4) * T:(c % 4 + 1) * T])
        m.update(_get(("cF", c % 4), lambda: _consts_F(c % 4)))
        in_maps.append(m)
    res = run_bass_kernel_spmd(nc, in_maps, core_ids=list(range(NCORE))).results
    out = np.zeros((2, S_FULL, D), np.float32)
    for c in range(NCORE):
        out[c // 4, (c % 4) * T:(c % 4 + 1) * T] = np.asarray(res[c]["yo"])
    return out
```
